# Optimizing a Trainium2 kernel written in Bass

```python
import jax, jax.numpy as jnp
from jax import lax
import numpy as np

D_MODEL = 1024
BATCH = 4
SEQ = 4096
DEPTH = 1
DEC_BATCH = 16
DEC_SEQ = 64
PAST_LEN = 1024

CHUNK = 64
HG_HEADS = 8
HG_DK = 128
HG_DV = D_MODEL // HG_HEADS
HG_F = HG_HEADS * HG_DK
HG_V = HG_HEADS * HG_DV
CONV_DIM = 1024
CONV_W = 3
D_FF = ((8 * D_MODEL // 3 + 255) // 256) * 256
PLE_DIM = 256
EPS = 1e-6
IN_SPLITS = (HG_F, 2 * HG_F, 2 * HG_F + HG_V, 2 * HG_F + 2 * HG_V,
             2 * HG_F + 2 * HG_V + CONV_DIM, 2 * HG_F + 2 * HG_V + 2 * CONV_DIM,
             2 * HG_F + 2 * HG_V + 3 * CONV_DIM, 2 * HG_F + 2 * HG_V + 3 * CONV_DIM + D_MODEL)
IN_COLS = 2 * HG_F + 2 * HG_V + 3 * CONV_DIM + 2 * D_MODEL

kernel_name = 'hgrn2_shortconv_gated_stream_step'


def rmsnorm(x, g):
    xf = x.astype(jnp.float32)
    y = xf * lax.rsqrt(jnp.mean(xf * xf, axis=-1, keepdims=True) + EPS)
    return (y * g.astype(jnp.float32)).astype(x.dtype)


def hgrn_block(S, q, k, v, logf):
    c = q.shape[2]
    bcum = jnp.cumsum(logf, axis=2)
    causal = jnp.tril(jnp.ones((c, c), dtype=bool))[None, None, :, :, None]
    diff = bcum[:, :, :, None, :] - bcum[:, :, None, :, :]
    decay = jnp.exp(jnp.where(causal, diff, -jnp.inf))
    scores = jnp.einsum('bhtk,bhtsk,bhsk->bhts', q, decay, k)
    o = (jnp.einsum('bhts,bhsv->bhtv', scores, v)
         + jnp.einsum('bhtk,bhkv->bhtv', q * jnp.exp(bcum), S))
    blast = bcum[:, :, -1:, :]
    S_new = (jnp.exp(blast[:, :, 0, :])[..., None] * S
             + jnp.einsum('bhsk,bhsv->bhkv', k * jnp.exp(blast - bcum), v))
    return S_new, o


def hgrn2(q_raw, f_raw, i_raw, g_raw, lb, S0, g_norm):
    b, t, _ = q_raw.shape
    c = min(CHUNK, t)
    n = t // c

    def blocks(a, d):
        return a.astype(jnp.float32).reshape(b, n, c, HG_HEADS, d).transpose(1, 0, 3, 2, 4)

    fz = f_raw.astype(jnp.float32)
    logf = jnp.log(lb + (1.0 - lb) * jax.nn.sigmoid(fz))
    k = (1.0 - lb) * jax.nn.sigmoid(-fz)
    q = jax.nn.silu(q_raw.astype(jnp.float32)) * HG_DK ** -0.5
    xs = (blocks(q, HG_DK), blocks(k, HG_DK), blocks(i_raw, HG_DV), blocks(logf, HG_DK))

    def step(S, blk):
        return hgrn_block(S, *blk)

    S_fin, o = lax.scan(step, S0.astype(jnp.float32), xs)
    o = o.transpose(1, 0, 3, 2, 4).reshape(b, t, HG_HEADS, HG_DV)
    o = rmsnorm(o, g_norm).reshape(b, t, HG_V)
    return (o * jax.nn.silu(g_raw.astype(jnp.float32))).astype(q_raw.dtype), S_fin


def short_conv(u, buf, w):
    t = u.shape[1]
    full = jnp.concatenate([buf.astype(u.dtype), u], axis=1)
    y = w[0] * full[:, 0:t]
    for j in range(1, CONV_W):
        y = y + w[j] * full[:, j:j + t]
    return y, full[:, t:]


def layer(x, p, S0, buf, lb, norm_mix, w_in, conv_w, hg_norm, w_branch_a, w_branch_b,
          w_out, norm_ffn, w_gate_up, w_down, norm_ple, w_ple, w_ple_gate):
    n = rmsnorm(x, norm_mix)
    z = n @ w_in
    q_raw, f_raw, i_raw, g_raw, b_g, c_g, h_c, za, zb = jnp.split(z, IN_SPLITS, axis=-1)
    o_a, S_new = hgrn2(q_raw, f_raw, i_raw, g_raw, lb, S0, hg_norm)
    conv_out, buf_new = short_conv(c_g * h_c, buf, conv_w)
    o_b = b_g * conv_out
    mix = jax.nn.sigmoid(za) * (o_a @ w_branch_a) + jax.nn.sigmoid(zb) * (o_b @ w_branch_b)
    x = x + mix @ w_out
    gate, up = jnp.split(rmsnorm(x, norm_ffn) @ w_gate_up, 2, axis=-1)
    x = x + (jax.nn.silu(gate) * up) @ w_down
    x = x + jax.nn.sigmoid(rmsnorm(x, norm_ple) @ w_ple_gate) * (p @ w_ple)
    return x, S_new, buf_new


def setup_inputs(seed: int = 0) -> dict:
    key = jax.random.key(seed)
    ks = jax.random.split(key, 21)

    def nrm(k, shape, scale):
        return jax.random.normal(k, shape, jnp.float32) * scale

    return {
        'x_prompt': nrm(ks[0], (BATCH, SEQ, D_MODEL), 1.0),
        'x_sample': nrm(ks[1], (DEC_BATCH, DEC_SEQ, D_MODEL), 1.0),
        'p_prompt': nrm(ks[2], (DEPTH, BATCH, SEQ, PLE_DIM), 1.0),
        'p_sample': nrm(ks[3], (DEPTH, DEC_BATCH, DEC_SEQ, PLE_DIM), 1.0),
        'state_hgrn': nrm(ks[4], (DEPTH, DEC_BATCH, HG_HEADS, HG_DK, HG_DV), 0.5),
        'state_conv': nrm(ks[5], (DEPTH, DEC_BATCH, CONV_W - 1, CONV_DIM), 1.0),
        'lower_bounds': nrm(ks[6], (DEPTH + 1, HG_F), 1.0),
        'norm_mix': 1.0 + nrm(ks[7], (DEPTH, D_MODEL), 0.01),
        'w_in': nrm(ks[8], (DEPTH, D_MODEL, IN_COLS), D_MODEL ** -0.5),
        'conv_w': nrm(ks[9], (DEPTH, CONV_W, CONV_DIM), CONV_W ** -0.5),
        'hg_norm': 1.0 + nrm(ks[10], (DEPTH, HG_DV), 0.01),
        'w_branch_a': nrm(ks[11], (DEPTH, HG_V, D_MODEL), HG_V ** -0.5),
        'w_branch_b': nrm(ks[12], (DEPTH, CONV_DIM, D_MODEL), CONV_DIM ** -0.5),
        'w_out': nrm(ks[13], (DEPTH, D_MODEL, D_MODEL), D_MODEL ** -0.5),
        'norm_ffn': 1.0 + nrm(ks[14], (DEPTH, D_MODEL), 0.01),
        'w_gate_up': nrm(ks[15], (DEPTH, D_MODEL, 2 * D_FF), D_MODEL ** -0.5),
        'w_down': nrm(ks[16], (DEPTH, D_FF, D_MODEL), D_FF ** -0.5),
        'norm_ple': 1.0 + nrm(ks[17], (DEPTH, D_MODEL), 0.01),
        'w_ple': nrm(ks[18], (DEPTH, PLE_DIM, D_MODEL), PLE_DIM ** -0.5),
        'w_ple_gate': nrm(ks[19], (DEPTH, D_MODEL, D_MODEL), D_MODEL ** -0.5),
        'norm_final': 1.0 + nrm(ks[20], (D_MODEL,), 0.01),
    }


def reference(x_prompt, x_sample, p_prompt, p_sample, state_hgrn, state_conv, lower_bounds,
              norm_mix, w_in, conv_w, hg_norm, w_branch_a, w_branch_b, w_out, norm_ffn,
              w_gate_up, w_down, norm_ple, w_ple, w_ple_gate, norm_final):
    lb_all = jnp.cumsum(jax.nn.softmax(lower_bounds.astype(jnp.float32), axis=0), axis=0)
    b = x_prompt.shape[0]
    hp, hs = x_prompt, x_sample
    hgrn_p, conv_p, hgrn_s, conv_s = [], [], [], []
    for l in range(DEPTH):
        w = (lb_all[l], norm_mix[l], w_in[l], conv_w[l], hg_norm[l], w_branch_a[l], w_branch_b[l],
             w_out[l], norm_ffn[l], w_gate_up[l], w_down[l], norm_ple[l], w_ple[l], w_ple_gate[l])
        S0 = jnp.zeros((b, HG_HEADS, HG_DK, HG_DV), jnp.float32)
        buf0 = jnp.zeros((b, CONV_W - 1, CONV_DIM), x_prompt.dtype)
        hp, Sp, cp = layer(hp, p_prompt[l], S0, buf0, *w)
        hs, Ss, cs = layer(hs, p_sample[l], state_hgrn[l], state_conv[l], *w)
        hgrn_p.append(Sp.astype(state_hgrn.dtype))
        conv_p.append(cp.astype(state_conv.dtype))
        hgrn_s.append(Ss.astype(state_hgrn.dtype))
        conv_s.append(cs.astype(state_conv.dtype))
    y_prompt = rmsnorm(hp, norm_final)
    y_sample = rmsnorm(hs, norm_final)
    return (y_prompt, y_sample, jnp.stack(hgrn_p), jnp.stack(conv_p), jnp.stack(hgrn_s), jnp.stack(conv_s))
```

```python
import numpy as np
import concourse.bass as bass
import concourse.mybir as mybir
from concourse.bass_utils import run_bass_kernel_spmd

F32 = mybir.dt.float32
F32R = mybir.dt.float32r
AF = mybir.ActivationFunctionType
ALU = mybir.AluOpType

D = 1024
NH = 8
DFF = 2816
PLE = 256
EPS = 1e-6
NCORES = 8
TPRE = 2048
TMAIN = 2176
MAIN_PASSES = [list(range(0, 6)), list(range(6, 12)), list(range(12, 17))]
PRE_PASSES = [list(range(0, 6)), list(range(6, 12)), list(range(12, 16))]
TMAX = 768
SLOTW = 3072
NSLOT = 3
FFG = [(0, 8), (8, 8), (16, 6)]
REC_W = 4
LIST_SCHED = True
PRIO_BLEVEL = False


SPLIT_KEYS = {"slA0", "slA1", "slC0", "slC1", "slD", "slE", "slF", "QH0", "QH1", "KT", "EB0", "EB1"}


class Op:
    __slots__ = ("eng", "fn", "reads", "writes", "dma", "deps", "signal", "sigval", "idx", "cost", "odeps")

    def __init__(self, eng, fn, reads, writes, dma, cost=0.3):
        self.eng, self.fn, self.reads, self.writes, self.dma = eng, fn, tuple(reads), tuple(writes), dma
        self.cost = cost
        self.odeps = ()
        self.deps = ()
        self.signal = False
        self.sigval = 0


class Sched:
    ENGS = ("pe", "act", "dve", "pool", "sp")

    def __init__(self):
        self.ops = []

    def op(self, eng, fn, reads=(), writes=(), dma=None, cost=0.3):
        def _exp(keys):
            out = []
            for k in keys:
                if k in SPLIT_KEYS:
                    out += [k + "#0", k + "#1"]
                else:
                    out.append(k)
            return out
        reads, writes = _exp(reads), _exp(writes)
        writes = list(writes) + [k for k in reads if k.startswith("pb") and k not in writes]
        o = Op(eng, fn, reads, writes, dma, cost)
        o.idx = len(self.ops)
        self.ops.append(o)
        return o

    def resolve(self):
        last_w = {}
        readers = {}
        for o in self.ops:
            deps = {}
            rset = set(o.reads)
            for k in o.reads:
                d = last_w.get(k)
                if d is not None:
                    deps[d.idx] = True
            for k in o.writes:
                d = last_w.get(k)
                if d is not None:
                    deps.setdefault(d.idx, False)
                for r in readers.get(k, ()):
                    deps.setdefault(r.idx, False)
            need = []
            o.odeps = [self.ops[di] for di in deps if di != o.idx]
            for di, raw in deps.items():
                d = self.ops[di]
                if d is o:
                    continue
                if d.dma is None and d.eng == o.eng:
                    if o.eng == "pe":
                        continue
                need.append(d)
                d.signal = True
            o.deps = need
            for k in o.reads:
                readers.setdefault(k, []).append(o)
            for k in o.writes:
                last_w[k] = o
                readers[k] = []
        self.schedule()
        cnt = {}
        for o in self.issue_order:
            if o.signal or o.dma is not None:
                key = ("dma", o.dma) if o.dma is not None else ("eng", o.eng)
                step = 16 if o.dma is not None else 1
                cnt[key] = cnt.get(key, 0) + step
                o.sigval = cnt[key]
        self.final = dict(cnt)
        return cnt

    def schedule(self):
        import heapq
        ops = self.ops
        if not LIST_SCHED:
            self.issue_order = list(ops)
            self.order = {e: [o for o in ops if o.eng == e] for e in self.ENGS}
            return
        nleft = [len(o.odeps) for o in ops]
        users = [[] for _ in ops]
        for o in ops:
            for d in o.odeps:
                users[d.idx].append(o)
        finish = [0.0] * len(ops)
        ready_t = [0.0] * len(ops)
        blevel = [0.0] * len(ops)
        for o in reversed(ops):
            m = 0.0
            for u in users[o.idx]:
                if blevel[u.idx] > m:
                    m = blevel[u.idx]
            blevel[o.idx] = o.cost + m
        pend = {e: [] for e in self.ENGS}
        avail = {e: [] for e in self.ENGS}
        free = {e: 0.0 for e in self.ENGS}
        for o in ops:
            if nleft[o.idx] == 0:
                heapq.heappush(pend[o.eng], (0.0, o.idx))
        order = {e: [] for e in self.ENGS}
        issue = []
        HOP = 0.8
        done = 0
        while done < len(ops):
            best = None
            for e in self.ENGS:
                while pend[e] and pend[e][0][0] <= free[e]:
                    pi_ = heapq.heappop(pend[e])[1]
                    heapq.heappush(avail[e], ((-blevel[pi_], pi_) if PRIO_BLEVEL else (pi_, pi_)))
                if avail[e]:
                    cand = (free[e], avail[e][0][1], e, True)
                elif pend[e]:
                    cand = (pend[e][0][0], pend[e][0][1], e, False)
                else:
                    continue
                if best is None or cand[:2] < best[:2]:
                    best = cand
            start, idx, e, from_avail = best
            if from_avail:
                heapq.heappop(avail[e])
            else:
                heapq.heappop(pend[e])
            o = ops[idx]
            issue_cost = 0.07 if o.dma is not None else o.cost
            free[e] = start + issue_cost
            finish[idx] = start + o.cost
            order[e].append(o)
            issue.append(o)
            done += 1
            for u in users[idx]:
                lat = 0.0 if (u.eng == e and o.dma is None) else HOP
                ready_t[u.idx] = max(ready_t[u.idx], finish[idx] + lat)
                nleft[u.idx] -= 1
                if nleft[u.idx] == 0:
                    heapq.heappush(pend[u.eng], (ready_t[u.idx], u.idx))
        self.order = order
        self.issue_order = issue

    def emit(self, nc, block, sems):
        engmap = {"pe": "tensor", "act": "scalar", "dve": "vector", "pool": "gpsimd", "sp": "sync"}
        sched = self

        def run(engname, e):
            waited = {}
            for o in sched.order[engname]:
                want = {}
                for d in o.deps:
                    key = ("dma", d.dma) if d.dma is not None else ("eng", d.eng)
                    if d.sigval > want.get(key, 0):
                        want[key] = d.sigval
                for key, val in want.items():
                    if val > waited.get(key, 0):
                        e.wait_ge(sems[key], val)
                        waited[key] = val
                if o.fn is None:
                    continue
                ins = o.fn(e)
                if o.dma is not None:
                    ins.then_inc(sems[("dma", o.dma)], 16)
                elif o.signal:
                    ins.then_inc(sems[("eng", o.eng)], 1)
            if engname == "sp":
                for key, val in sched.final.items():
                    if key[0] == "dma" and val > waited.get(key, 0):
                        e.wait_ge(sems[key], val)

        for engname in self.ENGS:
            getattr(block, engmap[engname])(lambda e, _n=engname: run(_n, e))


def build_program(pre_passes=None, main_passes=None, samp_tile=16):
    pre_passes = PRE_PASSES if pre_passes is None else pre_passes
    main_passes = MAIN_PASSES if main_passes is None else main_passes
    nc = bass.Bass("TRN2", target_bir_lowering=False)
    nc.dge_precook = False
    S = Sched()

    def din(name, shape):
        return nc.dram_tensor(name, list(shape), F32, kind="ExternalInput").ap()

    def dout(name, shape):
        return nc.dram_tensor(name, list(shape), F32, kind="ExternalOutput").ap()

    xpre = din("xpre", [128, 8, TPRE])
    xmain = din("xmain", [128, 8, TMAIN])
    pmain = din("pmain", [128, 2, TMAIN])
    s0 = din("s0", [3, NH, 128, 128])
    cbuf = din("cbuf", [2, 2, D])
    prm = din("prm", [128, 96])
    cst = din("cst", [128, 4, 128])
    wv = din("wv", [3, 128, 8, 384])
    wh = din("wh", [8, 128, 8, 384])
    wza = din("wza", [4, 128, 8, 256])
    wa = din("wa", [4, 128, 8, 256])
    wzb = din("wzb", [4, 128, 8, 256])
    wb = din("wb", [4, 128, 8, 256])
    wo = din("wo", [4, 128, 8, 256])
    wpg = din("wpg", [4, 128, 8, 256])
    wc = din("wc", [8, 128, 8, 384])
    wgu = din("wgu", [22, 128, 8, 256])
    wd = din("wd", [12, 128, 8, 256])
    wpl = din("wpl", [4, 128, 2, 256])
    yout = dout("y", [128, 8, TMAIN])
    sfin = dout("sfin", [3, NH, 128, 128])
    cfin = dout("cfin", [3, 2, D])

    import contextlib
    es = contextlib.ExitStack()

    def sb(name, shape):
        return es.enter_context(nc.sbuf_tensor(name, list(shape), F32))

    with es:
        XT = sb("XT", [128, 8, TMAX])
        NT = sb("NT", [128, 8, TMAX])
        B1 = sb("B1", [128, 8, TMAX])
        B2 = sb("B2", [128, 8 * TMAX])
        SLW = TMAX + 8
        slabs = {n: sb("sl" + n, [128, SLW]) for n in ("A0", "A1", "BS", "C0", "C1", "D", "E", "F")}
        QH = [sb("QH%d" % i, [128, TMAX]) for i in range(2)]
        KT = sb("KT", [128, TMAX])
        SQ = [sb("SQ%d" % i, [128, TMAX]) for i in range(2)]
        KHZ = [sb("KHZ%d" % i, [128, 6, 128]) for i in range(2)]
        SCM = sb("SCM", [128, 6, 128])
        RSTD = slabs["A0"]
        RM = sb("RM", [128, TMAX])
        WS = [sb("WS%d" % i, [128, SLOTW]) for i in range(NSLOT)]
        XIN = [sb("XIN0", [6, D])]
        SM = sb("SM", [128, NH, 128])
        SS = [sb("SS%d" % i, [128, NH, 128]) for i in range(2)]
        ST = [sb("ST%d" % i, [128, 128]) for i in range(4)]
        SHD = [sb("SHD%d" % i, [128, 128]) for i in range(4)]
        CST = sb("CST", [128, 2, 128])
        CSTB = sb("CSTB", [128, 2, 128])
        PRM = sb("PRM", [128, 96])
        DRV = sb("DRV", [128, 64])
        EB = sb("EB", [128, 2, 16])
        UH = sb("UH", [128, 8, 2])
        UES = sb("UES", [128, 2, 66])
        YS = sb("YS", [128, 2, 64])
        CB = sb("CB", [128, 8, 2, 2])
        CBO = sb("CBO", [128, 8, 3, 2])
        COUT = XIN[0][0:6, :]
        CTOK = COUT
        SMALL = sb("SMALL", [128, 32])
        FEN = sb("FEN", [128, 4])
        ps = es.enter_context(nc.psum_tensor("ps", [128, 8, 512], F32))

        ident = CST[:, 0, :]
        mask = CST[:, 1, :]
        onesd = CSTB[:, 0, :].bitcast(F32R)
        onesv = CSTB[:, 1, :].bitcast(F32R)
        P_L0, P_L1, P_GMIX, P_GFFN, P_GPLE, P_HGN, P_CW, P_GFIN = 0, 8, 16, 24, 32, 40, 41, 72
        LB, OML, NOML, HO, NHO, F0 = 0, 8, 16, 32, 40, 48
        epsc = SMALL[:, 0:1]

        dma_ctr = [0]

        def fsz(ap):
            n = 1
            for d in ap.shape[1:]:
                n *= d
            return n

        def dma(out, in_, reads, writes, key):
            S.op("sp", lambda e, o=out, i=in_: e.dma_start(out=o, in_=i), reads=reads, writes=writes, dma=key,
                 cost=2.0 + out.shape[0] * fsz(out) * 4 / 250e3)

        def ecost(eng, n):
            return {"act": 0.22 + n / 1200.0, "dve": 0.06 + n / 960.0, "pool": 0.1 + n / 420.0}[eng]

        def act(out, in_, func, reads, writes, bias=None, scale=None):
            kw = {}
            if bias is not None:
                kw["bias"] = bias
            if scale is not None:
                kw["scale"] = scale
            S.op("act", lambda e, o=out, i=in_, f=func, k=kw: e.activation(out=o, in_=i, func=f, **k),
                 reads=reads, writes=writes, cost=ecost("act", fsz(out)))

        def tt(eng, out, in0, in1, op, reads, writes):
            S.op(eng, lambda e, o=out, a=in0, b=in1, p=op: e.tensor_tensor(out=o, in0=a, in1=b, op=p),
                 reads=reads, writes=writes, cost=ecost(eng, fsz(out)))

        def ts(eng, out, in0, s1, s2, op0, op1, reads, writes):
            if op1 is None:
                S.op(eng, lambda e, o=out, a=in0, x=s1, p0=op0: e.tensor_scalar(out=o, in0=a, scalar1=x, scalar2=None, op0=p0),
                     reads=reads, writes=writes, cost=ecost(eng, fsz(out)))
            else:
                S.op(eng, lambda e, o=out, a=in0, x=s1, y=s2, p0=op0, p1=op1:
                     e.tensor_scalar(out=o, in0=a, scalar1=x, scalar2=y, op0=p0, op1=p1), reads=reads, writes=writes,
                     cost=ecost(eng, fsz(out)))

        def stt(out, in0, scalar, in1, op0, op1, reads, writes):
            S.op("dve", lambda e, o=out, a=in0, s=scalar, b=in1, p0=op0, p1=op1:
                 e.scalar_tensor_tensor(out=o, in0=a, scalar=s, in1=b, op0=p0, op1=p1), reads=reads, writes=writes,
                 cost=ecost("dve", fsz(out)))

        def cp(eng, out, in_, reads, writes):
            if eng == "act":
                act(out, in_, AF.Copy, reads, writes)
            else:
                S.op(eng, lambda e, o=out, i=in_: e.tensor_copy(out=o, in_=i), reads=reads, writes=writes, cost=ecost(eng, fsz(out)))

        def mm(out, lhsT, rhs, start, stop, reads, writes):
            nmov = fsz(rhs)
            S.op("pe", lambda e, o=out, l=lhsT, r=rhs, a=start, b=stop: e.matmul(o, l, r, start=a, stop=b),
                 reads=reads, writes=writes, cost=(nmov if nmov >= 256 else 4 * nmov) / 1900.0 + 0.02)

        def tr(out, in_, idn, reads, writes):
            S.op("pe", lambda e, o=out, i=in_, d=idn: e.transpose(o, i, d), reads=reads, writes=writes, cost=0.28)

        def fence(reads, writes):
            S.op("pool", lambda e: e.memset(FEN[:, 0:1], 0.0), reads=reads, writes=list(writes) + ["FEN"])

        def R(ap):
            return ap.bitcast(F32R)

        slot_ctr = [0]

        def load_unit(dram_ap, kb, w):
            i = slot_ctr[0] % NSLOT
            slot_ctr[0] += 1
            view = WS[i][:, 0:kb * w].rearrange("p (k w) -> p k w", k=kb)
            dma(R(view), R(dram_ap), reads=[], writes=["WS%d" % i], key="WS%d" % i)
            return view, "WS%d" % i

        accsel = [0]

        def next_acc():
            i = accsel[0] % 2
            accsel[0] += 1
            return i

        smsel = [0]

        def next_sm():
            n = smsel[0]
            smsel[0] += 1
            return 4 + n % 2, (n // 2) % 4

        sm4sel = [0]

        def next_sm4():
            n = sm4sel[0]
            sm4sel[0] += 1
            return (4, 5, 6, 7)[n % 4], (n // 4) % 4

        sm6sel = [0]

        def next_sm6():
            n = sm6sel[0]
            sm6sel[0] += 1
            return (4, 5, 0, 1, 2, 3)[n % 6], 0

        def smkey(b, q):
            return "pb%d" % b

        def acckeys(i):
            return ["pb%d" % (2 * i), "pb%d" % (2 * i + 1)]

        OBK = ["pb6", "pb7"]
        SMALLK = [smkey(4, q) for q in range(4)] + [smkey(5, q) for q in range(4)]

        dma(CST[:, 0:2, :], cst[:, 0:2, :], [], ["CSTp"], "c0")
        dma(R(CSTB[:, :, :]), R(cst[:, 2:4, :]), [], ["CST"], "c0b")
        dma(PRM[:, :], prm, [], ["PRM"], "c1")
        S.op("pool", lambda e: e.memset(SMALL[:, 1:2], 1.0), reads=[], writes=["SMALLa"])
        S.op("pool", lambda e: e.memset(SMALL[:, 0:1], EPS), reads=["SMALLa"], writes=["SMALL"])
        for i in range(2):
            ts("dve", R(KHZ[i][:, :, :]), CST[:, 0:1, :].to_broadcast([128, 6, 128]), 0.0, None, ALU.mult, None,
               ["CSTp"], ["KHZ%d.%d" % (i, t) for t in range(6)])
        S.op("pool", lambda e: e.memset(RM[:, :], 1.0), reads=[], writes=["RMa"])
        S.op("pool", lambda e: e.memset(RM[:, 0::64], 0.0), reads=["RMa"], writes=["RM"])
        tt("dve", DRV[:, 24:32], PRM[:, P_L0:P_L0 + 8], PRM[:, P_L1:P_L1 + 8], ALU.subtract, ["PRM"], ["DRVt"])
        act(DRV[:, LB:LB + 8], DRV[:, 24:32], AF.Sigmoid, ["DRVt"], ["DRVlb"])
        ts("dve", DRV[:, OML:OML + 8], DRV[:, LB:LB + 8], -1.0, 1.0, ALU.mult, ALU.add, ["DRVlb"], ["DRVo"])
        ts("dve", DRV[:, NOML:NOML + 8], DRV[:, LB:LB + 8], 1.0, -1.0, ALU.mult, ALU.add, ["DRVo", "DRVlb"], ["DRVn"])
        ts("dve", DRV[:, HO:HO + 8], DRV[:, OML:OML + 8], 0.5, None, ALU.mult, None, ["DRVn", "DRVo"], ["DRVh"])
        ts("dve", DRV[:, NHO:NHO + 8], DRV[:, OML:OML + 8], -0.5, None, ALU.mult, None, ["DRVh"], ["DRVnh"])
        tt("dve", DRV[:, F0:F0 + 8], DRV[:, LB:LB + 8], DRV[:, HO:HO + 8], ALU.add, ["DRVnh", "DRVh", "DRVlb"], ["DRV"])
        dma(SM[:, :, :], s0[0].rearrange("h k v -> k h v"), [], ["SM%d" % h for h in range(NH)], "c3")
        for k in range(2):
            dma(SS[k][:, :, :], s0[1 + k].rearrange("h k v -> k h v"), [], ["SS%d.%d" % (k, h) for h in range(NH)], "c4%d" % k)
        dma(CTOK[0:4, :], cbuf.rearrange("s r d -> (s r) d"), [], ["XIN0"], "c5")
        for blk in range(8):
            b, q = next_sm()
            tr(ps[:, b, q * 128:q * 128 + 4], CTOK[0:4, blk * 128:(blk + 1) * 128], ident[0:4, 0:4],
               ["XIN0", "CSTp"], [smkey(b, q)])
            cp("dve", CB[:, blk, :, :].rearrange("p s r -> p (s r)"), ps[:, b, q * 128:q * 128 + 4], [smkey(b, q)], ["CB"])

        def slab2(ap_slab, T):
            return ap_slab[:, 0:T].rearrange("p (b t) -> p b t", b=2)

        def accview(i, TB):
            return ps[:, 2 * i:2 * i + 2, 0:TB]

        def load_x_and_transpose(src, rows, nt):
            t0, T = rows[0], nt * 128
            for blk in range(8):
                dma(XT[:, blk, 0:T], src[:, blk, t0:t0 + T], [], ["XT%d" % blk], "xl%d" % blk)

        def norm_stats_blk(blk, T):
            TB = T // 2
            i = blk % 2
            act(R(SQ[i][:, 0:T]), XT[:, blk, 0:T], AF.Square, ["XT%d" % blk], ["SQ%d" % i])
            for tb in range(2):
                mm(ps[:, 6 + tb, 0:TB], onesd, R(SQ[i][:, tb * TB:(tb + 1) * TB]), blk == 0, blk == 7,
                   ["SQ%d" % i, "CST"], [OBK[tb]])

        def norm_finish(T, gcol):
            TB = T // 2
            act(slab2(RSTD, T), ps[:, 6:8, 0:TB], AF.Ln, OBK + ["SMALL"], ["slA0"], bias=epsc)
            act(RSTD[:, 0:T], RSTD[:, 0:T], AF.Exp, ["slA0"], ["slA0"], scale=-0.5)
            for blk in range(8):
                stt(R(NT[:, blk, 0:T]), XT[:, blk, 0:T], PRM[:, gcol + blk:gcol + blk + 1], RSTD[:, 0:T],
                    ALU.mult, ALU.mult, ["XT%d" % blk, "slA0", "PRM"], ["NT%d" % blk])

        def norm_to_NT(T, gcol):
            for blk in range(8):
                norm_stats_blk(blk, T)
            norm_finish(T, gcol)

        NTK = ["NT%d" % b for b in range(8)]

        def kloop(accv_i, TB, lhs_fn, rhs_fn, nk, reads):
            fams = ("NT", "B1.", "B2.")
            for kb in range(nk):
                rk = [k for k in reads if not k.startswith(fams)]
                for fam in fams:
                    if any(k.startswith(fam) for k in reads):
                        rk.append("%s%d" % (fam, kb))
                for tb in range(2):
                    mm(ps[:, 2 * accv_i + tb, 0:TB], lhs_fn(kb), rhs_fn(kb, tb), kb == 0, kb == nk - 1,
                       rk, ["pb%d" % (2 * accv_i + tb)])

        def v_phase(T, nt, VT):
            widths = [384, 384, 256]
            c0 = 0
            n = 0
            for cu in range(3):
                w = widths[cu]
                view, wk = load_unit(wv[cu, :, :, 0:w], 8, w)
                for ti in range(nt):
                    bank = n % 4
                    n += 1
                    for blk in range(8):
                        mm(ps[:, bank, 0:w], R(NT[:, blk, ti * 128:(ti + 1) * 128]), R(view[:, blk, :]), blk == 0, blk == 7,
                           [wk, "NT%d" % blk], ["pb%d" % bank])
                    cp("act" if n % 2 == 0 else "dve", R(VT[:, ti, c0:c0 + w]), ps[:, bank, 0:w], ["pb%d" % bank], ["VT%d" % ti])
                    yield
                c0 += w

        def seq(*gens):
            for g in gens:
                yield from g

        def head_proj(h, T, state_only):
            TB = T // 2
            par = h % 2
            A, C = slabs["A%d" % par], slabs["C%d" % par]
            Ak, Ck = "slA%d" % par, "slC%d" % par
            view, wk = load_unit(wh[h, :, :, 0:256], 8, 256)
            cols = [("q", 0), ("f", 128)]
            for name, c0 in cols:
                i = next_acc()
                kloop(i, TB, lambda kb, c0=c0: R(view[:, kb, c0:c0 + 128]),
                      lambda kb, tb: R(NT[:, kb, tb * TB:(tb + 1) * TB]), 8, [wk] + NTK)
                dst, dk, fn = {"q": (A, Ak, AF.Silu), "f": (C, Ck, AF.Tanh)}[name]
                act(slab2(dst, T), accview(i, TB), fn, acckeys(i), [dk], scale=(0.5 if name == "f" else None))
                yield

        def head_chain(h, T, hf):
            par = h % 2
            TB = T // 2
            t0, t1 = hf * TB, (hf + 1) * TB
            sfx = "#%d" % hf
            A, C = slabs["A%d" % par][:, t0:t1], slabs["C%d" % par][:, t0:t1]
            Ak, Ck = "slA%d" % par + sfx, "slC%d" % par + sfx
            Dd, E, F = slabs["D"][:, t0:t1], slabs["E"][:, t0:t1], slabs["F"][:, t0:t1]
            Dk, Ek, Fk = "slD" + sfx, "slE" + sfx, "slF" + sfx
            QHd, QHk = QH[par][:, t0:t1], "QH%d" % par + sfx
            KTd, KTk = KT[:, t0:t1], "KT" + sfx
            nchh = TB // 64
            EBd, EBk = EB[:, par, hf * nchh:(hf + 1) * nchh], "EB%d" % par + sfx
            hoc, nhoc, f0c = DRV[:, HO + h:HO + h + 1], DRV[:, NHO + h:NHO + h + 1], DRV[:, F0 + h:F0 + h + 1]
            ts("dve", Dd, C, hoc, f0c, ALU.mult, ALU.add, [Ck, "DRV"], [Dk])
            act(E, C, AF.Identity, [Ck, "DRV"], [Ek], bias=hoc, scale=nhoc)
            yield
            act(Dd, Dd, AF.Ln, [Dk], [Dk])
            yield
            S.op("dve", lambda e: e.tensor_tensor_scan(out=F, data0=RM[:, t0:t1], data1=Dd,
                                                       initial=0.0, op0=ALU.mult, op1=ALU.add),
                 reads=["RM", Dk], writes=[Fk], cost=0.06 + 2 * TB / 960.0)
            yield
            F3 = F.rearrange("p (c t) -> p c t", t=64)
            act(C, F, AF.Exp, [Fk], [Ck])
            yield
            cp("dve", EBd, C[:, 63:TB:64], [Ck], [EBk])
            act(Dd, F, AF.Exp, [Fk], [Dk], scale=-1.0)
            yield
            stt(R(QHd), A, float(128 ** -0.5), C, ALU.mult, ALU.mult, [Ak, Ck], [QHk])
            yield
            tt("dve", R(KTd), E, Dd, ALU.mult, [Ek, Dk], [KTk])
            yield
            A3 = A.rearrange("p (c t) -> p c t", t=64)
            tt("dve", A3, F3[:, :, 63:64].to_broadcast([128, nchh, 64]), F3, ALU.subtract, [Fk], [Ak])
            yield
            act(A, A, AF.Exp, [Ak], [Ak])
            yield
            tt("pool", A, E, A, ALU.mult, [Ek, Ak], [Ak])
            yield

        def head_rec(h, T, nt, VT, nsamp):
            par = h % 2
            A, Ak = slabs["A%d" % par], "slA%d" % par
            QHh, QHk = QH[par], "QH%d" % par
            EBk = "EB%d" % par
            BS = slabs["BS"]
            TB = T // 2
            nch = 2 * nt
            npc = 2 * (nt - nsamp)
            for ti in range(nt):
                b, q = next_sm4()
                tr(ps[:, b, q * 128:(q + 1) * 128], A[:, ti * 128:(ti + 1) * 128], ident, [Ak, "CSTp"], [smkey(b, q)])
                cp("act", R(KHZ[0][0:64, ti, :]), ps[0:64, b, q * 128:(q + 1) * 128], [smkey(b, q)], ["KHZ0.%d" % ti])
                cp("dve", R(KHZ[1][64:128, ti, :]), ps[64:128, b, q * 128:(q + 1) * 128], [smkey(b, q)], ["KHZ1.%d" % ti])
            yield
            for ti in range(nt):
                b, q = next_sm4()
                mm(ps[:, b, q * 128:(q + 1) * 128], R(KT[:, ti * 128:(ti + 1) * 128]), R(QHh[:, ti * 128:(ti + 1) * 128]),
                   True, True, ["KT", QHk], [smkey(b, q)])
                tt("dve", R(SCM[:, ti, :]), ps[:, b, q * 128:(q + 1) * 128], mask, ALU.mult,
                   [smkey(b, q), "CSTp"], ["SCM%d" % ti])
                yield

            def state_io(c):
                if c >= npc:
                    k = c - npc
                    return (SS[k][:, h, :], "SS%d.%d" % (k, h)), (SS[k][:, h, :], "SS%d.%d" % (k, h))
                src = (SM[:, h, :], "SM%d" % h) if c == 0 else (ST[(c - 1) % 4][:, :], "ST%d" % ((c - 1) % 4))
                dst = (SM[:, h, :], "SM%d" % h) if c == npc - 1 else (ST[c % 4][:, :], "ST%d" % (c % 4))
                return src, dst

            def emit_ds(c):
                ti, p0 = c // 2, (c % 2) * 64
                b, q = next_sm()
                mm(ps[:, b, q * 128:(q + 1) * 128], R(KHZ[c % 2][:, ti, :]), R(VT[:, ti, h * 128:(h + 1) * 128]),
                   True, True, ["KHZ%d.%d" % (c % 2, ti), "VT%d" % ti], [smkey(b, q)])
                return b, q

            def emit_stt(c, b, q):
                (src, srck), (dst, dstk) = state_io(c)
                stt(dst, src, EB[:, par, c:c + 1], ps[:, b, q * 128:(q + 1) * 128], ALU.mult, ALU.add,
                    [srck, EBk, smkey(b, q)], [dstk])

            def emit_shadow(c):
                (src, srck), _ = state_io(c)
                cp("pool", R(SHD[c % 4][:, :]), src, [srck], ["SHD%d" % (c % 4)])

            def emit_o(c):
                ti, p0 = c // 2, (c % 2) * 64
                (src, srck), _ = state_io(c)
                tb, off = (c * 64) // TB, (c * 64) % TB
                mm(ps[:, 6 + tb, off:off + 64], R(VT[:, ti, h * 128:(h + 1) * 128]),
                   R(SCM[:, ti, p0:p0 + 64]), True, False, ["VT%d" % ti, "SCM%d" % ti], [OBK[tb]])
                shi = c % 4
                mm(ps[:, 6 + tb, off:off + 64], R(SHD[shi][:, :]), R(QHh[:, c * 64:(c + 1) * 64]), False, True,
                   ["SHD%d" % shi, QHk], [OBK[tb]])

            if npc >= 4:
                for step in range(npc + 2):
                    if step < npc:
                        b, q = emit_ds(step)
                        emit_shadow(step)
                        emit_stt(step, b, q)
                    if step >= 2:
                        emit_o(step - 2)
                    yield
            else:
                for c in range(npc):
                    b, q = emit_ds(c)
                    emit_shadow(c)
                    emit_o(c)
                    emit_stt(c, b, q)
                    yield
            for c in range(npc, nch):
                b, q = emit_ds(c)
                emit_shadow(c)
                emit_o(c)
                emit_stt(c, b, q)
                yield
            act(R(slab2(SQ[0], T)), ps[:, 6:8, 0:TB], AF.Square, OBK, ["SQ0"])
            yield
            i = next_acc()
            for tb in range(2):
                mm(ps[:, 2 * i + tb, 0:TB], onesv, R(SQ[0][:, tb * TB:(tb + 1) * TB]), True, True, ["SQ0", "CST"], ["pb%d" % (2 * i + tb)])
            act(R(slab2(SQ[1], T)), accview(i, TB), AF.Ln, acckeys(i) + ["SMALL"], ["SQ1"], bias=epsc)
            yield
            act(R(SQ[1][:, 0:T]), SQ[1][:, 0:T], AF.Exp, ["SQ1"], ["SQ1"], scale=-0.5)
            yield
            stt(R(slab2(B1[:, h, :], T)), ps[:, 6:8, 0:TB], PRM[:, P_HGN:P_HGN + 1], slab2(SQ[1], T), ALU.mult, ALU.mult,
                OBK + ["SQ1", "PRM"], ["B1.%d" % h])
            yield
            view, wk = load_unit(wh[h, :, :, 256:384], 8, 128)
            i = next_acc()
            kloop(i, TB, lambda kb: R(view[:, kb, 0:128]), lambda kb, tb: R(NT[:, kb, tb * TB:(tb + 1) * TB]), 8, [wk] + NTK)
            act(slab2(BS, T), accview(i, TB), AF.Silu, acckeys(i), ["slBS"])
            yield
            tt("pool", R(B1[:, h, 0:T]), B1[:, h, 0:T], BS[:, 0:T], ALU.mult, ["B1.%d" % h, "slBS"], ["B1.%d" % h])
            yield

        def pre_head(h, T, nt, VT, slabset):
            (C, Ck), (Dd, Dk), (E, Ek), (F, Fk) = slabset
            TB = T // 2
            hoc, nhoc, f0c = DRV[:, HO + h:HO + h + 1], DRV[:, NHO + h:NHO + h + 1], DRV[:, F0 + h:F0 + h + 1]
            view, wk = load_unit(wh[h, :, :, 128:256], 8, 128)
            i = next_acc()
            kloop(i, TB, lambda kb: R(view[:, kb, 0:128]), lambda kb, tb: R(NT[:, kb, tb * TB:(tb + 1) * TB]), 8, [wk] + NTK)
            act(C[:, 0:T].rearrange("p (b t) -> p b t", b=2), accview(i, TB), AF.Tanh, acckeys(i), [Ck], scale=0.5)
            yield
            ts("dve", Dd[:, 0:T], C[:, 0:T], hoc, f0c, ALU.mult, ALU.add, [Ck, "DRV"], [Dk])
            act(E[:, 0:T], C[:, 0:T], AF.Identity, [Ck, "DRV"], [Ek], bias=hoc, scale=nhoc)
            yield
            act(Dd[:, 0:T], Dd[:, 0:T], AF.Ln, [Dk], [Dk])
            yield
            S.op("dve", lambda e: e.tensor_tensor_scan(out=F[:, T - 1::-1], data0=SMALL[:, 1:2].to_broadcast([128, T]),
                                                       data1=Dd[:, T - 1::-1], initial=0.0, op0=ALU.mult, op1=ALU.add),
                 reads=["SMALL", Dk], writes=[Fk], cost=0.06 + 2 * T / 960.0)
            yield
            act(C[:, 0:T - 1], F[:, 1:T], AF.Exp, [Fk], [Ck])
            S.op("pool", lambda e: e.memset(C[:, T - 1:T], 1.0), reads=[Ck], writes=[Ck], cost=0.1)
            act(SMALL[:, 16 + h:17 + h], F[:, 0:1], AF.Exp, [Fk], ["EBP%d" % h])
            yield
            tt("pool", C[:, 0:T], E[:, 0:T], C[:, 0:T], ALU.mult, [Ek, Ck], [Ck])
            yield
            ob = 6 + h % 2
            pc0 = h * 128 if h < NH - 1 else (h - 1) * 128
            for ti in range(nt):
                b, q = next_sm()
                tr(ps[:, b, q * 128:(q + 1) * 128], C[:, ti * 128:(ti + 1) * 128], ident, [Ck, "CSTp"], [smkey(b, q)])
                cp("act" if ti % 2 == 0 else "dve", R(SCM[:, ti, :]), ps[:, b, q * 128:(q + 1) * 128], [smkey(b, q)], ["SCM%d" % ti])
            for ti in range(nt):
                mm(ps[:, ob, 0:128], R(SCM[:, ti, :]), R(VT[:, ti, h * 128:(h + 1) * 128]), ti == 0, ti == nt - 1,
                   ["SCM%d" % ti, "VT%d" % ti], ["pb%d" % ob])
            stt(SM[:, h, :], SM[:, h, :], SMALL[:, 16 + h:17 + h], ps[:, ob, 0:128], ALU.mult, ALU.add,
                ["SM%d" % h, "EBP%d" % h, "pb%d" % ob], ["SM%d" % h])
            yield

        def interleave(gens, weights=None):
            gens = list(gens)
            weights = [1] * len(gens) if weights is None else list(weights)
            live = list(range(len(gens)))
            while live:
                for gi in list(live):
                    for _ in range(weights[gi]):
                        try:
                            next(gens[gi])
                        except StopIteration:
                            live.remove(gi)
                            break

        def gated_branch(T, wz_dram, wm_dram, srcbuf_key, first):
            TB = T // 2
            MIX = B2[:, 0:8 * T].rearrange("p (c t) -> p c t", c=8)
            tmp = [slabs["A0"], slabs["A1"]]
            tmpk = ["slA0", "slA1"]
            sg = [slabs["C0"], slabs["C1"]]
            sgk = ["slC0", "slC1"]
            for cp_ in range(4):
                vz, kz = load_unit(wz_dram[cp_], 8, 256)
                vm, km = load_unit(wm_dram[cp_], 8, 256)
                for cq in range(2):
                    cb = cp_ * 2 + cq
                    i = next_acc()
                    kloop(i, TB, lambda kb, cq=cq, vz=vz: R(vz[:, kb, cq * 128:(cq + 1) * 128]),
                          lambda kb, tb: R(NT[:, kb, tb * TB:(tb + 1) * TB]), 8, [kz] + NTK)
                    act(slab2(sg[cq], T), accview(i, TB), AF.Sigmoid, acckeys(i), [sgk[cq]])
                    i = next_acc()
                    kloop(i, TB, lambda kb, cq=cq, vm=vm: R(vm[:, kb, cq * 128:(cq + 1) * 128]),
                          lambda kb, tb: R(B1[:, kb, tb * TB:(tb + 1) * TB]), 8, [km] + ["B1.%d" % b for b in range(8)])
                    mixv = MIX[:, cb, :].rearrange("p (b t) -> p b t", b=2)
                    if first:
                        tt("dve", R(mixv), slab2(sg[cq], T), accview(i, TB), ALU.mult, [sgk[cq]] + acckeys(i), ["B2.%d" % cb])
                    else:
                        tt("dve", slab2(tmp[cq], T), slab2(sg[cq], T), accview(i, TB), ALU.mult, [sgk[cq]] + acckeys(i), [tmpk[cq]])
                        tt("pool", R(MIX[:, cb, :]), MIX[:, cb, :], tmp[cq][:, 0:T], ALU.add, ["B2.%d" % cb, tmpk[cq]], ["B2.%d" % cb])
            return MIX

        def conv_phase(T, Tp, nsamp, last):
            TB = T // 2
            CT, UE, Y = slabs["E"], slabs["F"], slabs["D"]
            for j in range(8):
                view, wk = load_unit(wc[j], 8, 384)
                w0 = PRM[:, P_CW + 0 * 8 + j:P_CW + 0 * 8 + j + 1]
                w1 = PRM[:, P_CW + 1 * 8 + j:P_CW + 1 * 8 + j + 1]
                w2 = PRM[:, P_CW + 2 * 8 + j:P_CW + 2 * 8 + j + 1]
                i = next_acc()
                kloop(i, TB, lambda kb: R(view[:, kb, 128:256]), lambda kb, tb: R(NT[:, kb, tb * TB:(tb + 1) * TB]), 8, [wk] + NTK)
                cp("act", slab2(CT, T), accview(i, TB), acckeys(i), ["slE"])
                i = next_acc()
                kloop(i, TB, lambda kb: R(view[:, kb, 256:384]), lambda kb, tb: R(NT[:, kb, tb * TB:(tb + 1) * TB]), 8, [wk] + NTK)
                tt("dve", UE[:, 2:2 + T].rearrange("p (b t) -> p b t", b=2), slab2(CT, T), accview(i, TB), ALU.mult,
                   ["slE"] + acckeys(i), ["slF"])
                cp("pool", UE[:, 0:2], UH[:, j, :], ["UH%d" % j], ["slF"])
                act(Y[:, 0:T], UE[:, 2:2 + T], AF.Copy, ["slF", "PRM"], ["slD"], scale=w2)
                stt(Y[:, 0:T], UE[:, 1:1 + T], w1, Y[:, 0:T], ALU.mult, ALU.add, ["slF", "slD", "PRM"], ["slD"])
                stt(Y[:, 0:T], UE[:, 0:T], w0, Y[:, 0:T], ALU.mult, ALU.add, ["slF", "slD", "PRM"], ["slD"])
                if nsamp:
                    cp("pool", UES[:, :, 2:66], UE[:, 2 + Tp:2 + Tp + 128].rearrange("p (s t) -> p s t", s=2), ["slF"], ["UES"])
                    cp("pool", UES[:, :, 0:2], CB[:, j, :, :], ["CB", "UES"], ["UES"])
                    act(YS[:, :, :], UES[:, :, 2:66], AF.Copy, ["UES", "PRM"], ["YS"], scale=w2)
                    stt(YS[:, :, :], UES[:, :, 1:65], w1, YS[:, :, :], ALU.mult, ALU.add, ["UES", "YS", "PRM"], ["YS"])
                    stt(Y[:, Tp:Tp + 128].rearrange("p (s t) -> p s t", s=2), UES[:, :, 0:64], w0, YS[:, :, :], ALU.mult, ALU.add,
                        ["UES", "YS", "PRM", "slD"], ["slD"])
                    cp("pool", CBO[:, j, 1:3, :], UES[:, :, 64:66], ["UES"], ["CBO"])
                cp("pool", UH[:, j, :], UE[:, Tp:Tp + 2], ["slF"], ["UH%d" % j])
                if last:
                    cp("pool", CBO[:, j, 0, :], UE[:, Tp:Tp + 2], ["slF", "CBO"], ["CBO"])
                i = next_acc()
                kloop(i, TB, lambda kb: R(view[:, kb, 0:128]), lambda kb, tb: R(NT[:, kb, tb * TB:(tb + 1) * TB]), 8, [wk] + NTK)
                tt("dve", R(B1[:, j, 0:T].rearrange("p (b t) -> p b t", b=2)), slab2(Y, T), accview(i, TB), ALU.mult,
                   ["slD"] + acckeys(i), ["B1.%d" % j])

        def resid_matmul(T, w_dram_units, nk, src_fn, src_keys, after_cb=None):
            TB = T // 2
            for cp_ in range(4):
                view, wk = load_unit(w_dram_units(cp_), nk, 256)
                for cq in range(2):
                    cb = cp_ * 2 + cq
                    i = next_acc()
                    kloop(i, TB, lambda kb, cq=cq, view=view: R(view[:, kb, cq * 128:(cq + 1) * 128]),
                          lambda kb, tb: src_fn(kb, tb), nk, [wk] + src_keys)
                    xv = XT[:, cb, 0:T].rearrange("p (b t) -> p b t", b=2)
                    tt("dve", xv, xv, accview(i, TB), ALU.add, ["XT%d" % cb] + acckeys(i), ["XT%d" % cb])
                    if after_cb is not None and cb >= 1:
                        after_cb(cb - 1)
            if after_cb is not None:
                after_cb(7)

        def ffn_phase(T):
            TB = T // 2
            GS = [slabs["A0"], slabs["A1"]]
            GSk = ["slA0", "slA1"]
            for g, (j0, nj) in enumerate(FFG):
                for jj in range(nj):
                    j = j0 + jj
                    view, wk = load_unit(wgu[j], 8, 256)
                    i = next_acc()
                    kloop(i, TB, lambda kb: R(view[:, kb, 0:128]), lambda kb, tb: R(NT[:, kb, tb * TB:(tb + 1) * TB]), 8, [wk] + NTK)
                    act(slab2(GS[jj % 2], T), accview(i, TB), AF.Silu, acckeys(i), [GSk[jj % 2]])
                    i = next_acc()
                    kloop(i, TB, lambda kb: R(view[:, kb, 128:256]), lambda kb, tb: R(NT[:, kb, tb * TB:(tb + 1) * TB]), 8, [wk] + NTK)
                    tt("dve", R(B1[:, jj, 0:T].rearrange("p (b t) -> p b t", b=2)), slab2(GS[jj % 2], T), accview(i, TB), ALU.mult,
                       [GSk[jj % 2]] + acckeys(i), ["B1.%d" % jj])
                resid_matmul(T, lambda cp_, g=g, nj=nj: wd[g * 4 + cp_, :, 0:nj, :], nj,
                             lambda kb, tb: R(B1[:, kb, tb * TB:(tb + 1) * TB]), ["B1.%d" % b for b in range(nj)],
                             after_cb=(lambda cb: norm_stats_blk(cb, T)) if g == len(FFG) - 1 else None)

        def ple_phase(T, nt, rows):
            TB = T // 2
            PT = B2[:, 0:2 * T].rearrange("p (c t) -> p c t", c=2)
            t0 = rows[0]
            for kb in range(2):
                dma(R(PT[:, kb, 0:T]), R(pmain[:, kb, t0:t0 + T]), [], ["B2.%d" % kb], "pl%d" % kb)
            sg = [slabs["C0"], slabs["C1"]]
            sgk = ["slC0", "slC1"]
            tmp = [slabs["A0"], slabs["A1"]]
            tmpk = ["slA0", "slA1"]
            for cp_ in range(4):
                vz, kz = load_unit(wpg[cp_], 8, 256)
                vm, km = load_unit(wpl[cp_], 2, 256)
                for cq in range(2):
                    cb = cp_ * 2 + cq
                    i = next_acc()
                    kloop(i, TB, lambda kb, cq=cq, vz=vz: R(vz[:, kb, cq * 128:(cq + 1) * 128]),
                          lambda kb, tb: R(NT[:, kb, tb * TB:(tb + 1) * TB]), 8, [kz] + NTK)
                    act(slab2(sg[cq], T), accview(i, TB), AF.Sigmoid, acckeys(i), [sgk[cq]])
                    i = next_acc()
                    kloop(i, TB, lambda kb, cq=cq, vm=vm: R(vm[:, kb, cq * 128:(cq + 1) * 128]),
                          lambda kb, tb: R(PT[:, kb, tb * TB:(tb + 1) * TB]), 2, [km, "B2.0", "B2.1"])
                    tt("dve", slab2(tmp[cq], T), slab2(sg[cq], T), accview(i, TB), ALU.mult, [sgk[cq]] + acckeys(i), [tmpk[cq]])
                    tt("pool", XT[:, cb, 0:T], XT[:, cb, 0:T], tmp[cq][:, 0:T], ALU.add, ["XT%d" % cb, tmpk[cq]], ["XT%d" % cb])
                    if cb >= 1:
                        norm_stats_blk(cb - 1, T)
            norm_stats_blk(7, T)

        def final_out(T, nt, rows, next_load=None):
            TB = T // 2
            act(slab2(RSTD, T), ps[:, 6:8, 0:TB], AF.Ln, OBK + ["SMALL"], ["slA0"], bias=epsc)
            act(RSTD[:, 0:T], RSTD[:, 0:T], AF.Exp, ["slA0"], ["slA0"], scale=-0.5)
            t0 = rows[0]
            ybuf = [(slabs[n], "sl" + n) for n in ("A1", "BS", "C0", "C1", "D", "E", "F")] + [(XT[:, 7, :], "XT7")]
            for blk in range(8):
                yb, yk = ybuf[blk]
                stt(yb[:, 0:T], XT[:, blk, 0:T], PRM[:, P_GFIN + blk:P_GFIN + blk + 1], RSTD[:, 0:T],
                    ALU.mult, ALU.mult, ["XT%d" % blk, "slA0", "PRM"], [yk])
                dma(yout[:, blk, t0:t0 + T], yb[:, 0:T], [yk], ["yout%d" % blk], "yo%d" % blk)
                if next_load is not None and blk < 7:
                    next_load(blk)
            if next_load is not None:
                next_load(7)

        def run_pass(tiles, main, first_main, last_main, last_pre, preloaded=False, next_tiles=None):
            nt = len(tiles)
            T = nt * 128
            state_only = not main
            src = xmain if main else xpre
            rows = [g * 128 for g in tiles]
            nsamp = 1 if (main and samp_tile in tiles) else 0
            Tp = T - 128 * nsamp

            def chunk_state(c, h):
                if nsamp and c >= 2 * (nt - 1):
                    k = c - 2 * (nt - 1)
                    return SS[k][:, h, :], "SS%d.%d" % (k, h)
                return SM[:, h, :], "SM%d" % h

            fence(["B2.%d" % b for b in range(8)], ["VT%d" % t for t in range(6)])
            if not preloaded:
                load_x_and_transpose(src, rows, nt)
            norm_to_NT(T, P_GMIX)
            VT = B2[:, 0:nt * D].rearrange("p (t c) -> p t c", t=nt)
            if state_only:
                sets = [[(slabs[n], "sl" + n) for n in ("A0", "A1", "BS", "F")],
                        [(slabs[n], "sl" + n) for n in ("C0", "C1", "D", "E")],
                        [(XT[:, b, :], "XT%d" % b) for b in range(0, 4)],
                        [(XT[:, b, :], "XT%d" % b) for b in range(4, 8)]]
                interleave([v_phase(T, nt, VT)] + [pre_head(i, T, nt, VT, sets[i]) for i in range(4)], [4, 1, 1, 1, 1])
                interleave([pre_head(4 + i, T, nt, VT, sets[i]) for i in range(4)])
            else:
                interleave([head_proj(0, T, state_only)])
                interleave([v_phase(T, nt, VT), head_chain(0, T, 0), head_chain(0, T, 1), head_proj(1, T, state_only)],
                           [3, 1, 1, 1])
            for h in (range(NH) if not state_only else ()):
                gens, wts = [head_rec(h, T, nt, VT, nsamp)], [REC_W]
                if h + 1 < NH:
                    gens += [head_chain(h + 1, T, 0), head_chain(h + 1, T, 1)]
                    wts += [2, 2]
                if h + 2 < NH:
                    gens.append(head_proj(h + 2, T, state_only))
                    wts.append(1)
                interleave(gens, wts)
            if state_only:
                if last_pre:
                    for j in range(8):
                        view, wk = load_unit(wc[j], 8, 384)
                        b, q = next_sm()
                        for kb in range(8):
                            mm(ps[:, b, q * 128:q * 128 + 2], R(view[:, kb, 128:256]), R(NT[:, kb, T - 2:T]), kb == 0, kb == 7,
                               [wk] + NTK, [smkey(b, q)])
                        cp("act", SMALL[:, 8:10], ps[:, b, q * 128:q * 128 + 2], [smkey(b, q)], ["SMu"])
                        b2, q2 = next_sm()
                        for kb in range(8):
                            mm(ps[:, b2, q2 * 128:q2 * 128 + 2], R(view[:, kb, 256:384]), R(NT[:, kb, T - 2:T]), kb == 0, kb == 7,
                               [wk] + NTK, [smkey(b2, q2)])
                        tt("dve", UH[:, j, :], SMALL[:, 8:10], ps[:, b2, q2 * 128:q2 * 128 + 2], ALU.mult,
                           ["SMu", smkey(b2, q2)], ["UH%d" % j])
                return
            fence(["VT%d" % t for t in range(6)], ["B2.%d" % b for b in range(8)])
            gated_branch(T, wza, wa, None, True)
            conv_phase(T, Tp, nsamp, last_main)
            MIX = gated_branch(T, wzb, wb, None, False)
            TB = T // 2
            resid_matmul(T, lambda cp_: wo[cp_], 8, lambda kb, tb: R(MIX[:, kb, tb * TB:(tb + 1) * TB]),
                         ["B2.%d" % b for b in range(8)], after_cb=lambda cb: norm_stats_blk(cb, T))
            norm_finish(T, P_GFFN)
            ffn_phase(T)
            norm_finish(T, P_GPLE)
            ple_phase(T, nt, rows)
            if next_tiles is not None:
                nT_, nt0 = len(next_tiles) * 128, next_tiles[0] * 128
                final_out(T, nt, rows, lambda blk: dma(XT[:, blk, 0:nT_], xmain[:, blk, nt0:nt0 + nT_], [], ["XT%d" % blk], "xl%d" % blk))
            else:
                final_out(T, nt, rows)

        for pi, tiles in enumerate(pre_passes):
            run_pass(tiles, False, False, False, pi == len(pre_passes) - 1)
        for pi, tiles in enumerate(main_passes):
            run_pass(tiles, True, pi == 0, pi == len(main_passes) - 1, False, preloaded=(pi > 0),
                     next_tiles=(main_passes[pi + 1] if pi + 1 < len(main_passes) else None))

        dma(sfin[0].rearrange("h k v -> k h v"), SM[:, :, :], ["SM%d" % h for h in range(NH)], ["sfin0"], "o0")
        for k in range(2):
            dma(sfin[1 + k].rearrange("h k v -> k h v"), SS[k][:, :, :], ["SS%d.%d" % (k, h) for h in range(NH)], ["sfin%d" % (1 + k)], "o%d" % (1 + k))
        for blk in range(8):
            b, q = next_sm()
            tr(ps[0:6, b, q * 128:(q + 1) * 128], CBO[:, blk, :, :].rearrange("p s r -> p (s r)"), ident, ["CBO", "CSTp"], [smkey(b, q)])
            cp("dve", COUT[:, blk * 128:(blk + 1) * 128], ps[0:6, b, q * 128:(q + 1) * 128], [smkey(b, q)], ["XIN0"])
        dma(cfin.rearrange("s r d -> (s r) d"), COUT[:, :], ["XIN0"], ["cfin"], "o3")

        cnt = S.resolve()
        sems = {}
        for key in cnt:
            sems[key] = es.enter_context(nc.semaphore("s_%s_%s" % key))
        block = es.enter_context(nc.Block())
        S.emit(nc, block, sems)
    return nc


def _tile_cols(W, col_lists, kb, width):
    out = np.zeros((len(col_lists), 128, kb, width), np.float32)
    Wr = W.reshape(kb, 128, W.shape[1])
    for u, cols in enumerate(col_lists):
        out[u, :, :, :len(cols)] = np.transpose(Wr[:, :, cols], (1, 0, 2))
    return out


def _fm(x):
    t, d = x.shape
    return np.ascontiguousarray(np.transpose(x.reshape(t, d // 128, 128), (2, 1, 0)))


def _pcol(v):
    return np.ascontiguousarray(v.reshape(-1, 128).T)


_NC_CACHE = {}


def _prep_shared(lower_bounds, norm_mix, w_in, conv_w, hg_norm, w_branch_a, w_branch_b, w_out, norm_ffn,
                 w_gate_up, w_down, norm_ple, w_ple, w_ple_gate, norm_final):
    f = lambda a: np.ascontiguousarray(np.asarray(a, dtype=np.float32))
    win = f(w_in)[0]
    ar = np.arange
    HF, HV, CD = 1024, 1024, 1024
    oq, of_, oi, og, oB, oC, oh, oza, ozb = 0, 1024, 2048, 3072, 4096, 5120, 6144, 7168, 8192
    wv_ = _tile_cols(win, [list(oi + ar(0, 384)), list(oi + ar(384, 768)), list(oi + ar(768, 1024))], 8, 384)
    wh_ = _tile_cols(win, [list(oq + h * 128 + ar(128)) + list(of_ + h * 128 + ar(128)) + list(og + h * 128 + ar(128))
                           for h in range(8)], 8, 384)
    wc_ = _tile_cols(win, [list(oB + j * 128 + ar(128)) + list(oC + j * 128 + ar(128)) + list(oh + j * 128 + ar(128))
                           for j in range(8)], 8, 384)
    c256 = [list(c * 256 + ar(256)) for c in range(4)]
    wza_ = _tile_cols(win[:, oza:oza + 1024], c256, 8, 256)
    wzb_ = _tile_cols(win[:, ozb:ozb + 1024], c256, 8, 256)
    wa_ = _tile_cols(f(w_branch_a)[0], c256, 8, 256)
    wb_ = _tile_cols(f(w_branch_b)[0], c256, 8, 256)
    wo_ = _tile_cols(f(w_out)[0], c256, 8, 256)
    wpg_ = _tile_cols(f(w_ple_gate)[0], c256, 8, 256)
    wgu_full = f(w_gate_up)[0]
    wgu_ = _tile_cols(wgu_full, [list(j * 128 + ar(128)) + list(DFF + j * 128 + ar(128)) for j in range(22)], 8, 256)
    wdn = f(w_down)[0]
    wd_ = np.zeros((12, 128, 8, 256), np.float32)
    for g, (j0, nj) in enumerate(FFG):
        sub = wdn[j0 * 128:(j0 + nj) * 128]
        wd_[g * 4:(g + 1) * 4, :, :nj, :] = _tile_cols(sub, c256, nj, 256)
    wpl_ = _tile_cols(f(w_ple)[0], c256, 2, 256)

    prm = np.zeros((128, 96), np.float32)
    lbs = f(lower_bounds)
    prm[:, 0:8] = _pcol(lbs[0])
    prm[:, 8:16] = _pcol(lbs[1])
    prm[:, 16:24] = _pcol(f(norm_mix)[0])
    prm[:, 24:32] = _pcol(f(norm_ffn)[0])
    prm[:, 32:40] = _pcol(f(norm_ple)[0])
    prm[:, 40] = f(hg_norm)[0]
    cw = f(conv_w)[0]
    for tap in range(3):
        prm[:, 41 + tap * 8:41 + tap * 8 + 8] = _pcol(cw[tap])
    cst = np.zeros((128, 4, 128), np.float32)
    cst[:, 0, :] = np.eye(128, dtype=np.float32)
    s_i, t_i = np.meshgrid(ar(128), ar(128), indexing="ij")
    cst[:, 1, :] = ((s_i // 64 == t_i // 64) & (s_i <= t_i)).astype(np.float32)
    cst[:, 2, :] = 1.0 / D
    cst[:, 3, :] = 1.0 / 128
    prm[:, 72:80] = _pcol(f(norm_final))

    shared = dict(prm=prm, cst=cst, wv=wv_, wh=wh_, wza=wza_, wa=wa_, wzb=wzb_, wb=wb_, wo=wo_,
                  wpg=wpg_, wc=wc_, wgu=wgu_, wd=wd_, wpl=wpl_)
    return shared


def kernel(x_prompt, x_sample, p_prompt, p_sample, state_hgrn, state_conv, lower_bounds,
           norm_mix, w_in, conv_w, hg_norm, w_branch_a, w_branch_b, w_out, norm_ffn,
           w_gate_up, w_down, norm_ple, w_ple, w_ple_gate, norm_final):
    f = lambda a: np.ascontiguousarray(np.asarray(a, dtype=np.float32))
    x_prompt, x_sample, p_prompt, p_sample = f(x_prompt), f(x_sample), f(p_prompt), f(p_sample)
    state_hgrn, state_conv = f(state_hgrn), f(state_conv)
    shared = _prep_shared(lower_bounds, norm_mix, w_in, conv_w, hg_norm, w_branch_a, w_branch_b, w_out, norm_ffn,
                          w_gate_up, w_down, norm_ple, w_ple, w_ple_gate, norm_final)
    in_maps = []
    for c in range(NCORES):
        b, half = c // 2, c % 2
        xm = _fm(np.concatenate([x_prompt[b, half * 2048:(half + 1) * 2048], x_sample[2 * c], x_sample[2 * c + 1]], axis=0))
        pm = _fm(np.concatenate([p_prompt[0, b, half * 2048:(half + 1) * 2048], p_sample[0, 2 * c], p_sample[0, 2 * c + 1]], axis=0))
        xp = _fm(x_prompt[b, 0:2048]) if half == 1 else np.zeros((128, 8, TPRE), np.float32)
        s0 = np.zeros((3, NH, 128, 128), np.float32)
        s0[1] = state_hgrn[0, 2 * c]
        s0[2] = state_hgrn[0, 2 * c + 1]
        cb = np.stack([state_conv[0, 2 * c], state_conv[0, 2 * c + 1]], axis=0)
        m = dict(shared)
        m.update(xpre=np.ascontiguousarray(xp), xmain=np.ascontiguousarray(xm), pmain=np.ascontiguousarray(pm),
                 s0=s0, cbuf=np.ascontiguousarray(cb))
        in_maps.append(m)

    if "nc" not in _NC_CACHE:
        _NC_CACHE["nc"] = build_program()
    nc = _NC_CACHE["nc"]
    res = run_bass_kernel_spmd(nc, in_maps, core_ids=list(range(NCORES)))
    R_ = res.results
    y_prompt = np.zeros((4, 4096, D), np.float32)
    y_sample = np.zeros((16, 64, D), np.float32)
    hg_p = np.zeros((1, 4, NH, 128, 128), np.float32)
    cv_p = np.zeros((1, 4, 2, D), np.float32)
    hg_s = np.zeros((1, 16, NH, 128, 128), np.float32)
    cv_s = np.zeros((1, 16, 2, D), np.float32)
    for c in range(NCORES):
        b, half = c // 2, c % 2
        y = np.ascontiguousarray(np.transpose(R_[c]["y"], (2, 1, 0))).reshape(TMAIN, D)
        y_prompt[b, half * 2048:(half + 1) * 2048] = y[0:2048]
        y_sample[2 * c] = y[2048:2112]
        y_sample[2 * c + 1] = y[2112:2176]
        sf, cf = R_[c]["sfin"], R_[c]["cfin"]
        hg_s[0, 2 * c], hg_s[0, 2 * c + 1] = sf[1], sf[2]
        cv_s[0, 2 * c], cv_s[0, 2 * c + 1] = cf[1], cf[2]
        if half == 1:
            hg_p[0, b] = sf[0]
            cv_p[0, b] = cf[0]
    return (y_prompt, y_sample, hg_p, cv_p, hg_s, cv_s)
```

```python
import numpy as np
import concourse.bass as bass
import concourse.mybir as mybir
from concourse.bass_utils import run_bass_kernel_spmd

F32 = mybir.dt.float32
F32R = mybir.dt.float32r
AF = mybir.ActivationFunctionType
ALU = mybir.AluOpType

D = 1024
NH = 8
DFF = 2816
PLE = 256
EPS = 1e-6
NCORES = 8
TPRE = 2048
TMAIN = 2176
MAIN_PASSES = [list(range(0, 6)), list(range(6, 12)), list(range(12, 17))]
PRE_PASSES = [list(range(0, 6)), list(range(6, 12)), list(range(12, 16))]
TMAX = 768
SLOTW = 3072
NSLOT = 3
FFG = [(0, 8), (8, 8), (16, 6)]
REC_W = 4
LIST_SCHED = True
PRIO_BLEVEL = False


SPLIT_KEYS = {"slA0", "slA1", "slC0", "slC1", "slD", "slE", "slF", "QH0", "QH1", "KT", "EB0", "EB1"}


class Op:
    __slots__ = ("eng", "fn", "reads", "writes", "dma", "deps", "signal", "sigval", "idx", "cost", "odeps")

    def __init__(self, eng, fn, reads, writes, dma, cost=0.3):
        self.eng, self.fn, self.reads, self.writes, self.dma = eng, fn, tuple(reads), tuple(writes), dma
        self.cost = cost
        self.odeps = ()
        self.deps = ()
        self.signal = False
        self.sigval = 0


class Sched:
    ENGS = ("pe", "act", "dve", "pool", "sp")

    def __init__(self):
        self.ops = []

    def op(self, eng, fn, reads=(), writes=(), dma=None, cost=0.3):
        def _exp(keys):
            out = []
            for k in keys:
                if k in SPLIT_KEYS:
                    out += [k + "#0", k + "#1"]
                else:
                    out.append(k)
            return out
        reads, writes = _exp(reads), _exp(writes)
        writes = list(writes) + [k for k in reads if k.startswith("pb") and k not in writes]
        o = Op(eng, fn, reads, writes, dma, cost)
        o.idx = len(self.ops)
        self.ops.append(o)
        return o

    def resolve(self):
        last_w = {}
        readers = {}
        for o in self.ops:
            deps = {}
            rset = set(o.reads)
            for k in o.reads:
                d = last_w.get(k)
                if d is not None:
                    deps[d.idx] = True
            for k in o.writes:
                d = last_w.get(k)
                if d is not None:
                    deps.setdefault(d.idx, False)
                for r in readers.get(k, ()):
                    deps.setdefault(r.idx, False)
            need = []
            o.odeps = [self.ops[di] for di in deps if di != o.idx]
            for di, raw in deps.items():
                d = self.ops[di]
                if d is o:
                    continue
                if d.dma is None and d.eng == o.eng:
                    if o.eng == "pe":
                        continue
                need.append(d)
                d.signal = True
            o.deps = need
            for k in o.reads:
                readers.setdefault(k, []).append(o)
            for k in o.writes:
                last_w[k] = o
                readers[k] = []
        self.schedule()
        cnt = {}
        for o in self.issue_order:
            if o.signal or o.dma is not None:
                key = ("dma", o.dma) if o.dma is not None else ("eng", o.eng)
                step = 16 if o.dma is not None else 1
                cnt[key] = cnt.get(key, 0) + step
                o.sigval = cnt[key]
        self.final = dict(cnt)
        return cnt

    def schedule(self):
        import heapq
        ops = self.ops
        if not LIST_SCHED:
            self.issue_order = list(ops)
            self.order = {e: [o for o in ops if o.eng == e] for e in self.ENGS}
            return
        nleft = [len(o.odeps) for o in ops]
        users = [[] for _ in ops]
        for o in ops:
            for d in o.odeps:
                users[d.idx].append(o)
        finish = [0.0] * len(ops)
        ready_t = [0.0] * len(ops)
        blevel = [0.0] * len(ops)
        for o in reversed(ops):
            m = 0.0
            for u in users[o.idx]:
                if blevel[u.idx] > m:
                    m = blevel[u.idx]
            blevel[o.idx] = o.cost + m
        pend = {e: [] for e in self.ENGS}
        avail = {e: [] for e in self.ENGS}
        free = {e: 0.0 for e in self.ENGS}
        for o in ops:
            if nleft[o.idx] == 0:
                heapq.heappush(pend[o.eng], (0.0, o.idx))
        order = {e: [] for e in self.ENGS}
        issue = []
        HOP = 0.6
        done = 0
        while done < len(ops):
            best = None
            for e in self.ENGS:
                while pend[e] and pend[e][0][0] <= free[e]:
                    pi_ = heapq.heappop(pend[e])[1]
                    heapq.heappush(avail[e], ((-blevel[pi_], pi_) if PRIO_BLEVEL else (pi_, pi_)))
                if avail[e]:
                    cand = (free[e], avail[e][0][1], e, True)
                elif pend[e]:
                    cand = (pend[e][0][0], pend[e][0][1], e, False)
                else:
                    continue
                if best is None or cand[:2] < best[:2]:
                    best = cand
            start, idx, e, from_avail = best
            if from_avail:
                heapq.heappop(avail[e])
            else:
                heapq.heappop(pend[e])
            o = ops[idx]
            issue_cost = 0.07 if o.dma is not None else o.cost
            free[e] = start + issue_cost
            finish[idx] = start + o.cost
            order[e].append(o)
            issue.append(o)
            done += 1
            for u in users[idx]:
                lat = 0.0 if (u.eng == e and o.dma is None) else HOP
                ready_t[u.idx] = max(ready_t[u.idx], finish[idx] + lat)
                nleft[u.idx] -= 1
                if nleft[u.idx] == 0:
                    heapq.heappush(pend[u.eng], (ready_t[u.idx], u.idx))
        self.order = order
        self.issue_order = issue

    def emit(self, nc, block, sems):
        engmap = {"pe": "tensor", "act": "scalar", "dve": "vector", "pool": "gpsimd", "sp": "sync"}
        sched = self

        def run(engname, e):
            waited = {}
            for o in sched.order[engname]:
                want = {}
                for d in o.deps:
                    key = ("dma", d.dma) if d.dma is not None else ("eng", d.eng)
                    if d.sigval > want.get(key, 0):
                        want[key] = d.sigval
                for key, val in want.items():
                    if val > waited.get(key, 0):
                        e.wait_ge(sems[key], val)
                        waited[key] = val
                if o.fn is None:
                    continue
                ins = o.fn(e)
                if o.dma is not None:
                    ins.then_inc(sems[("dma", o.dma)], 16)
                elif o.signal:
                    ins.then_inc(sems[("eng", o.eng)], 1)
            if engname == "sp":
                for key, val in sched.final.items():
                    if key[0] == "dma" and val > waited.get(key, 0):
                        e.wait_ge(sems[key], val)

        for engname in self.ENGS:
            getattr(block, engmap[engname])(lambda e, _n=engname: run(_n, e))


def build_program(pre_passes=None, main_passes=None, samp_tile=16):
    pre_passes = PRE_PASSES if pre_passes is None else pre_passes
    main_passes = MAIN_PASSES if main_passes is None else main_passes
    nc = bass.Bass("TRN2", target_bir_lowering=False)
    nc.dge_precook = False
    S = Sched()

    def din(name, shape):
        return nc.dram_tensor(name, list(shape), F32, kind="ExternalInput").ap()

    def dout(name, shape):
        return nc.dram_tensor(name, list(shape), F32, kind="ExternalOutput").ap()

    xpre = din("xpre", [128, 8, TPRE])
    xmain = din("xmain", [128, 8, TMAIN])
    pmain = din("pmain", [128, 2, TMAIN])
    s0 = din("s0", [3, NH, 128, 128])
    cbuf = din("cbuf", [2, 2, D])
    prm = din("prm", [128, 96])
    cst = din("cst", [128, 4, 128])
    wv = din("wv", [3, 128, 8, 384])
    wh = din("wh", [8, 128, 8, 384])
    wza = din("wza", [4, 128, 8, 256])
    wa = din("wa", [4, 128, 8, 256])
    wzb = din("wzb", [4, 128, 8, 256])
    wb = din("wb", [4, 128, 8, 256])
    wo = din("wo", [4, 128, 8, 256])
    wpg = din("wpg", [4, 128, 8, 256])
    wc = din("wc", [8, 128, 8, 384])
    wgu = din("wgu", [22, 128, 8, 256])
    wd = din("wd", [12, 128, 8, 256])
    wpl = din("wpl", [4, 128, 2, 256])
    yout = dout("y", [128, 8, TMAIN])
    sfin = dout("sfin", [3, NH, 128, 128])
    cfin = dout("cfin", [3, 2, D])

    import contextlib
    es = contextlib.ExitStack()

    def sb(name, shape):
        return es.enter_context(nc.sbuf_tensor(name, list(shape), F32))

    with es:
        XT = sb("XT", [128, 8, TMAX])
        NT = sb("NT", [128, 8, TMAX])
        B1 = sb("B1", [128, 8, TMAX])
        B2 = sb("B2", [128, 8 * TMAX])
        SLW = TMAX + 8
        slabs = {n: sb("sl" + n, [128, SLW]) for n in ("A0", "A1", "BS", "C0", "C1", "D", "E", "F")}
        QH = [sb("QH%d" % i, [128, TMAX]) for i in range(2)]
        KT = sb("KT", [128, TMAX])
        SQ = [sb("SQ%d" % i, [128, TMAX]) for i in range(2)]
        KHZ = [sb("KHZ%d" % i, [128, 6, 128]) for i in range(2)]
        SCM = sb("SCM", [128, 6, 128])
        RSTD = slabs["A0"]
        RM = sb("RM", [128, TMAX])
        WS = [sb("WS%d" % i, [128, SLOTW]) for i in range(NSLOT)]
        XIN = [sb("XIN0", [6, D])]
        SM = sb("SM", [128, NH, 128])
        SS = [sb("SS%d" % i, [128, NH, 128]) for i in range(2)]
        ST = [sb("ST%d" % i, [128, 128]) for i in range(4)]
        SHD = [sb("SHD%d" % i, [128, 128]) for i in range(4)]
        CST = sb("CST", [128, 2, 128])
        CSTB = sb("CSTB", [128, 2, 128])
        PRM = sb("PRM", [128, 96])
        DRV = sb("DRV", [128, 64])
        EB = sb("EB", [128, 2, 16])
        UH = sb("UH", [128, 8, 2])
        UES = sb("UES", [128, 2, 66])
        YS = sb("YS", [128, 2, 64])
        CB = sb("CB", [128, 8, 2, 2])
        CBO = sb("CBO", [128, 8, 3, 2])
        COUT = XIN[0][0:6, :]
        CTOK = COUT
        SMALL = sb("SMALL", [128, 32])
        FEN = sb("FEN", [128, 4])
        ps = es.enter_context(nc.psum_tensor("ps", [128, 8, 512], F32))

        ident = CST[:, 0, :]
        mask = CST[:, 1, :]
        onesd = CSTB[:, 0, :].bitcast(F32R)
        onesv = CSTB[:, 1, :].bitcast(F32R)
        P_L0, P_L1, P_GMIX, P_GFFN, P_GPLE, P_HGN, P_CW, P_GFIN = 0, 8, 16, 24, 32, 40, 41, 72
        LB, OML, NOML, HO, NHO, F0 = 0, 8, 16, 32, 40, 48
        epsc = SMALL[:, 0:1]

        dma_ctr = [0]

        def fsz(ap):
            n = 1
            for d in ap.shape[1:]:
                n *= d
            return n

        def dma(out, in_, reads, writes, key):
            S.op("sp", lambda e, o=out, i=in_: e.dma_start(out=o, in_=i), reads=reads, writes=writes, dma=key,
                 cost=2.0 + out.shape[0] * fsz(out) * 4 / 250e3)

        def ecost(eng, n):
            return {"act": 0.22 + n / 1200.0, "dve": 0.06 + n / 960.0, "pool": 0.1 + n / 420.0}[eng]

        def act(out, in_, func, reads, writes, bias=None, scale=None):
            kw = {}
            if bias is not None:
                kw["bias"] = bias
            if scale is not None:
                kw["scale"] = scale
            S.op("act", lambda e, o=out, i=in_, f=func, k=kw: e.activation(out=o, in_=i, func=f, **k),
                 reads=reads, writes=writes, cost=ecost("act", fsz(out)))

        def tt(eng, out, in0, in1, op, reads, writes):
            S.op(eng, lambda e, o=out, a=in0, b=in1, p=op: e.tensor_tensor(out=o, in0=a, in1=b, op=p),
                 reads=reads, writes=writes, cost=ecost(eng, fsz(out)))

        def ts(eng, out, in0, s1, s2, op0, op1, reads, writes):
            if op1 is None:
                S.op(eng, lambda e, o=out, a=in0, x=s1, p0=op0: e.tensor_scalar(out=o, in0=a, scalar1=x, scalar2=None, op0=p0),
                     reads=reads, writes=writes, cost=ecost(eng, fsz(out)))
            else:
                S.op(eng, lambda e, o=out, a=in0, x=s1, y=s2, p0=op0, p1=op1:
                     e.tensor_scalar(out=o, in0=a, scalar1=x, scalar2=y, op0=p0, op1=p1), reads=reads, writes=writes,
                     cost=ecost(eng, fsz(out)))

        def stt(out, in0, scalar, in1, op0, op1, reads, writes):
            S.op("dve", lambda e, o=out, a=in0, s=scalar, b=in1, p0=op0, p1=op1:
                 e.scalar_tensor_tensor(out=o, in0=a, scalar=s, in1=b, op0=p0, op1=p1), reads=reads, writes=writes,
                 cost=ecost("dve", fsz(out)))

        def cp(eng, out, in_, reads, writes):
            if eng == "act":
                act(out, in_, AF.Copy, reads, writes)
            else:
                S.op(eng, lambda e, o=out, i=in_: e.tensor_copy(out=o, in_=i), reads=reads, writes=writes, cost=ecost(eng, fsz(out)))

        def mm(out, lhsT, rhs, start, stop, reads, writes):
            nmov = fsz(rhs)
            S.op("pe", lambda e, o=out, l=lhsT, r=rhs, a=start, b=stop: e.matmul(o, l, r, start=a, stop=b),
                 reads=reads, writes=writes, cost=(nmov if nmov >= 256 else 4 * nmov) / 1900.0 + 0.02)

        def tr(out, in_, idn, reads, writes):
            S.op("pe", lambda e, o=out, i=in_, d=idn: e.transpose(o, i, d), reads=reads, writes=writes, cost=0.28)

        def fence(reads, writes):
            S.op("pool", lambda e: e.memset(FEN[:, 0:1], 0.0), reads=reads, writes=list(writes) + ["FEN"])

        def R(ap):
            return ap.bitcast(F32R)

        slot_ctr = [0]

        def load_unit(dram_ap, kb, w):
            i = slot_ctr[0] % NSLOT
            slot_ctr[0] += 1
            view = WS[i][:, 0:kb * w].rearrange("p (k w) -> p k w", k=kb)
            dma(R(view), R(dram_ap), reads=[], writes=["WS%d" % i], key="WS%d" % i)
            return view, "WS%d" % i

        accsel = [0]

        def next_acc():
            i = accsel[0] % 2
            accsel[0] += 1
            return i

        smsel = [0]

        def next_sm():
            n = smsel[0]
            smsel[0] += 1
            return 4 + n % 2, (n // 2) % 4

        sm4sel = [0]

        def next_sm4():
            n = sm4sel[0]
            sm4sel[0] += 1
            return (4, 5, 6, 7)[n % 4], (n // 4) % 4

        sm6sel = [0]

        def next_sm6():
            n = sm6sel[0]
            sm6sel[0] += 1
            return (4, 5, 0, 1, 2, 3)[n % 6], 0

        def smkey(b, q):
            return "pb%d" % b

        def acckeys(i):
            return ["pb%d" % (2 * i), "pb%d" % (2 * i + 1)]

        OBK = ["pb6", "pb7"]
        SMALLK = [smkey(4, q) for q in range(4)] + [smkey(5, q) for q in range(4)]

        dma(CST[:, 0:2, :], cst[:, 0:2, :], [], ["CSTp"], "c0")
        dma(R(CSTB[:, :, :]), R(cst[:, 2:4, :]), [], ["CST"], "c0b")
        dma(PRM[:, :], prm, [], ["PRM"], "c1")
        S.op("pool", lambda e: e.memset(SMALL[:, 1:2], 1.0), reads=[], writes=["SMALLa"])
        S.op("pool", lambda e: e.memset(SMALL[:, 0:1], EPS), reads=["SMALLa"], writes=["SMALL"])
        for i in range(2):
            ts("dve", R(KHZ[i][:, :, :]), CST[:, 0:1, :].to_broadcast([128, 6, 128]), 0.0, None, ALU.mult, None,
               ["CSTp"], ["KHZ%d.%d" % (i, t) for t in range(6)])
        S.op("pool", lambda e: e.memset(RM[:, :], 1.0), reads=[], writes=["RMa"])
        S.op("pool", lambda e: e.memset(RM[:, 0::64], 0.0), reads=["RMa"], writes=["RM"])
        tt("dve", DRV[:, 24:32], PRM[:, P_L0:P_L0 + 8], PRM[:, P_L1:P_L1 + 8], ALU.subtract, ["PRM"], ["DRVt"])
        act(DRV[:, LB:LB + 8], DRV[:, 24:32], AF.Sigmoid, ["DRVt"], ["DRVlb"])
        ts("dve", DRV[:, OML:OML + 8], DRV[:, LB:LB + 8], -1.0, 1.0, ALU.mult, ALU.add, ["DRVlb"], ["DRVo"])
        ts("dve", DRV[:, NOML:NOML + 8], DRV[:, LB:LB + 8], 1.0, -1.0, ALU.mult, ALU.add, ["DRVo", "DRVlb"], ["DRVn"])
        ts("dve", DRV[:, HO:HO + 8], DRV[:, OML:OML + 8], 0.5, None, ALU.mult, None, ["DRVn", "DRVo"], ["DRVh"])
        ts("dve", DRV[:, NHO:NHO + 8], DRV[:, OML:OML + 8], -0.5, None, ALU.mult, None, ["DRVh"], ["DRVnh"])
        tt("dve", DRV[:, F0:F0 + 8], DRV[:, LB:LB + 8], DRV[:, HO:HO + 8], ALU.add, ["DRVnh", "DRVh", "DRVlb"], ["DRV"])
        dma(SM[:, :, :], s0[0].rearrange("h k v -> k h v"), [], ["SM%d" % h for h in range(NH)], "c3")
        for k in range(2):
            dma(SS[k][:, :, :], s0[1 + k].rearrange("h k v -> k h v"), [], ["SS%d.%d" % (k, h) for h in range(NH)], "c4%d" % k)
        dma(CTOK[0:4, :], cbuf.rearrange("s r d -> (s r) d"), [], ["XIN0"], "c5")
        for blk in range(8):
            b, q = next_sm()
            tr(ps[:, b, q * 128:q * 128 + 4], CTOK[0:4, blk * 128:(blk + 1) * 128], ident[0:4, 0:4],
               ["XIN0", "CSTp"], [smkey(b, q)])
            cp("dve", CB[:, blk, :, :].rearrange("p s r -> p (s r)"), ps[:, b, q * 128:q * 128 + 4], [smkey(b, q)], ["CB"])

        def slab2(ap_slab, T):
            return ap_slab[:, 0:T].rearrange("p (b t) -> p b t", b=2)

        def accview(i, TB):
            return ps[:, 2 * i:2 * i + 2, 0:TB]

        def load_x_and_transpose(src, rows, nt):
            t0, T = rows[0], nt * 128
            for blk in range(8):
                dma(XT[:, blk, 0:T], src[:, blk, t0:t0 + T], [], ["XT%d" % blk], "xl%d" % blk)

        def norm_stats_blk(blk, T):
            TB = T // 2
            i = blk % 2
            act(R(SQ[i][:, 0:T]), XT[:, blk, 0:T], AF.Square, ["XT%d" % blk], ["SQ%d" % i])
            for tb in range(2):
                mm(ps[:, 6 + tb, 0:TB], onesd, R(SQ[i][:, tb * TB:(tb + 1) * TB]), blk == 0, blk == 7,
                   ["SQ%d" % i, "CST"], [OBK[tb]])

        def norm_finish(T, gcol):
            TB = T // 2
            act(slab2(RSTD, T), ps[:, 6:8, 0:TB], AF.Ln, OBK + ["SMALL"], ["slA0"], bias=epsc)
            act(RSTD[:, 0:T], RSTD[:, 0:T], AF.Exp, ["slA0"], ["slA0"], scale=-0.5)
            for blk in range(8):
                stt(R(NT[:, blk, 0:T]), XT[:, blk, 0:T], PRM[:, gcol + blk:gcol + blk + 1], RSTD[:, 0:T],
                    ALU.mult, ALU.mult, ["XT%d" % blk, "slA0", "PRM"], ["NT%d" % blk])

        def norm_to_NT(T, gcol):
            for blk in range(8):
                norm_stats_blk(blk, T)
            norm_finish(T, gcol)

        NTK = ["NT%d" % b for b in range(8)]

        def kloop(accv_i, TB, lhs_fn, rhs_fn, nk, reads):
            fams = ("NT", "B1.", "B2.")
            for kb in range(nk):
                rk = [k for k in reads if not k.startswith(fams)]
                for fam in fams:
                    if any(k.startswith(fam) for k in reads):
                        rk.append("%s%d" % (fam, kb))
                for tb in range(2):
                    mm(ps[:, 2 * accv_i + tb, 0:TB], lhs_fn(kb), rhs_fn(kb, tb), kb == 0, kb == nk - 1,
                       rk, ["pb%d" % (2 * accv_i + tb)])

        def v_phase(T, nt, VT):
            widths = [384, 384, 256]
            c0 = 0
            n = 0
            for cu in range(3):
                w = widths[cu]
                view, wk = load_unit(wv[cu, :, :, 0:w], 8, w)
                for ti in range(nt):
                    bank = n % 4
                    n += 1
                    for blk in range(8):
                        mm(ps[:, bank, 0:w], R(NT[:, blk, ti * 128:(ti + 1) * 128]), R(view[:, blk, :]), blk == 0, blk == 7,
                           [wk, "NT%d" % blk], ["pb%d" % bank])
                    cp("act" if n % 2 == 0 else "dve", R(VT[:, ti, c0:c0 + w]), ps[:, bank, 0:w], ["pb%d" % bank], ["VT%d" % ti])
                    yield
                c0 += w

        def seq(*gens):
            for g in gens:
                yield from g

        def head_proj(h, T, state_only):
            TB = T // 2
            par = h % 2
            A, C = slabs["A%d" % par], slabs["C%d" % par]
            Ak, Ck = "slA%d" % par, "slC%d" % par
            view, wk = load_unit(wh[h, :, :, 0:256], 8, 256)
            cols = [("q", 0), ("f", 128)]
            for name, c0 in cols:
                i = next_acc()
                kloop(i, TB, lambda kb, c0=c0: R(view[:, kb, c0:c0 + 128]),
                      lambda kb, tb: R(NT[:, kb, tb * TB:(tb + 1) * TB]), 8, [wk] + NTK)
                dst, dk, fn = {"q": (A, Ak, AF.Silu), "f": (C, Ck, AF.Tanh)}[name]
                act(slab2(dst, T), accview(i, TB), fn, acckeys(i), [dk], scale=(0.5 if name == "f" else None))
                yield

        def head_chain(h, T, hf):
            par = h % 2
            TB = T // 2
            t0, t1 = hf * TB, (hf + 1) * TB
            sfx = "#%d" % hf
            A, C = slabs["A%d" % par][:, t0:t1], slabs["C%d" % par][:, t0:t1]
            Ak, Ck = "slA%d" % par + sfx, "slC%d" % par + sfx
            Dd, E, F = slabs["D"][:, t0:t1], slabs["E"][:, t0:t1], slabs["F"][:, t0:t1]
            Dk, Ek, Fk = "slD" + sfx, "slE" + sfx, "slF" + sfx
            QHd, QHk = QH[par][:, t0:t1], "QH%d" % par + sfx
            KTd, KTk = KT[:, t0:t1], "KT" + sfx
            nchh = TB // 64
            EBd, EBk = EB[:, par, hf * nchh:(hf + 1) * nchh], "EB%d" % par + sfx
            hoc, nhoc, f0c = DRV[:, HO + h:HO + h + 1], DRV[:, NHO + h:NHO + h + 1], DRV[:, F0 + h:F0 + h + 1]
            ts("dve", Dd, C, hoc, f0c, ALU.mult, ALU.add, [Ck, "DRV"], [Dk])
            act(E, C, AF.Identity, [Ck, "DRV"], [Ek], bias=hoc, scale=nhoc)
            yield
            act(Dd, Dd, AF.Ln, [Dk], [Dk])
            yield
            S.op("dve", lambda e: e.tensor_tensor_scan(out=F, data0=RM[:, t0:t1], data1=Dd,
                                                       initial=0.0, op0=ALU.mult, op1=ALU.add),
                 reads=["RM", Dk], writes=[Fk], cost=0.06 + 2 * TB / 960.0)
            yield
            F3 = F.rearrange("p (c t) -> p c t", t=64)
            act(C, F, AF.Exp, [Fk], [Ck])
            yield
            cp("dve", EBd, C[:, 63:TB:64], [Ck], [EBk])
            act(Dd, F, AF.Exp, [Fk], [Dk], scale=-1.0)
            yield
            stt(R(QHd), A, float(128 ** -0.5), C, ALU.mult, ALU.mult, [Ak, Ck], [QHk])
            yield
            tt("dve", R(KTd), E, Dd, ALU.mult, [Ek, Dk], [KTk])
            yield
            A3 = A.rearrange("p (c t) -> p c t", t=64)
            tt("dve", A3, F3[:, :, 63:64].to_broadcast([128, nchh, 64]), F3, ALU.subtract, [Fk], [Ak])
            yield
            act(A, A, AF.Exp, [Ak], [Ak])
            yield
            tt("pool", A, E, A, ALU.mult, [Ek, Ak], [Ak])
            yield

        def head_rec(h, T, nt, VT, nsamp):
            par = h % 2
            A, Ak = slabs["A%d" % par], "slA%d" % par
            QHh, QHk = QH[par], "QH%d" % par
            EBk = "EB%d" % par
            BS = slabs["BS"]
            TB = T // 2
            nch = 2 * nt
            npc = 2 * (nt - nsamp)
            for ti in range(nt):
                b, q = next_sm4()
                tr(ps[:, b, q * 128:(q + 1) * 128], A[:, ti * 128:(ti + 1) * 128], ident, [Ak, "CSTp"], [smkey(b, q)])
                cp("act", R(KHZ[0][0:64, ti, :]), ps[0:64, b, q * 128:(q + 1) * 128], [smkey(b, q)], ["KHZ0.%d" % ti])
                cp("dve", R(KHZ[1][64:128, ti, :]), ps[64:128, b, q * 128:(q + 1) * 128], [smkey(b, q)], ["KHZ1.%d" % ti])
            yield
            for ti in range(nt):
                b, q = next_sm4()
                mm(ps[:, b, q * 128:(q + 1) * 128], R(KT[:, ti * 128:(ti + 1) * 128]), R(QHh[:, ti * 128:(ti + 1) * 128]),
                   True, True, ["KT", QHk], [smkey(b, q)])
                tt("dve", R(SCM[:, ti, :]), ps[:, b, q * 128:(q + 1) * 128], mask, ALU.mult,
                   [smkey(b, q), "CSTp"], ["SCM%d" % ti])
                yield

            def state_io(c):
                if c >= npc:
                    k = c - npc
                    return (SS[k][:, h, :], "SS%d.%d" % (k, h)), (SS[k][:, h, :], "SS%d.%d" % (k, h))
                src = (SM[:, h, :], "SM%d" % h) if c == 0 else (ST[(c - 1) % 4][:, :], "ST%d" % ((c - 1) % 4))
                dst = (SM[:, h, :], "SM%d" % h) if c == npc - 1 else (ST[c % 4][:, :], "ST%d" % (c % 4))
                return src, dst

            def emit_ds(c):
                ti, p0 = c // 2, (c % 2) * 64
                b, q = next_sm()
                mm(ps[:, b, q * 128:(q + 1) * 128], R(KHZ[c % 2][:, ti, :]), R(VT[:, ti, h * 128:(h + 1) * 128]),
                   True, True, ["KHZ%d.%d" % (c % 2, ti), "VT%d" % ti], [smkey(b, q)])
                return b, q

            def emit_stt(c, b, q):
                (src, srck), (dst, dstk) = state_io(c)
                stt(dst, src, EB[:, par, c:c + 1], ps[:, b, q * 128:(q + 1) * 128], ALU.mult, ALU.add,
                    [srck, EBk, smkey(b, q)], [dstk])

            def emit_shadow(c):
                (src, srck), _ = state_io(c)
                cp("pool", R(SHD[c % 4][:, :]), src, [srck], ["SHD%d" % (c % 4)])

            def emit_o(c):
                ti, p0 = c // 2, (c % 2) * 64
                (src, srck), _ = state_io(c)
                tb, off = (c * 64) // TB, (c * 64) % TB
                mm(ps[:, 6 + tb, off:off + 64], R(VT[:, ti, h * 128:(h + 1) * 128]),
                   R(SCM[:, ti, p0:p0 + 64]), True, False, ["VT%d" % ti, "SCM%d" % ti], [OBK[tb]])
                shi = c % 4
                mm(ps[:, 6 + tb, off:off + 64], R(SHD[shi][:, :]), R(QHh[:, c * 64:(c + 1) * 64]), False, True,
                   ["SHD%d" % shi, QHk], [OBK[tb]])

            if npc >= 4:
                for step in range(npc + 2):
                    if step < npc:
                        b, q = emit_ds(step)
                        emit_shadow(step)
                        emit_stt(step, b, q)
                    if step >= 2:
                        emit_o(step - 2)
                    yield
            else:
                for c in range(npc):
                    b, q = emit_ds(c)
                    emit_shadow(c)
                    emit_o(c)
                    emit_stt(c, b, q)
                    yield
            for c in range(npc, nch):
                b, q = emit_ds(c)
                emit_shadow(c)
                emit_o(c)
                emit_stt(c, b, q)
                yield
            act(R(slab2(SQ[0], T)), ps[:, 6:8, 0:TB], AF.Square, OBK, ["SQ0"])
            yield
            i = next_acc()
            for tb in range(2):
                mm(ps[:, 2 * i + tb, 0:TB], onesv, R(SQ[0][:, tb * TB:(tb + 1) * TB]), True, True, ["SQ0", "CST"], ["pb%d" % (2 * i + tb)])
            act(R(slab2(SQ[1], T)), accview(i, TB), AF.Ln, acckeys(i) + ["SMALL"], ["SQ1"], bias=epsc)
            yield
            act(R(SQ[1][:, 0:T]), SQ[1][:, 0:T], AF.Exp, ["SQ1"], ["SQ1"], scale=-0.5)
            yield
            stt(R(slab2(B1[:, h, :], T)), ps[:, 6:8, 0:TB], PRM[:, P_HGN:P_HGN + 1], slab2(SQ[1], T), ALU.mult, ALU.mult,
                OBK + ["SQ1", "PRM"], ["B1.%d" % h])
            yield
            view, wk = load_unit(wh[h, :, :, 256:384], 8, 128)
            i = next_acc()
            kloop(i, TB, lambda kb: R(view[:, kb, 0:128]), lambda kb, tb: R(NT[:, kb, tb * TB:(tb + 1) * TB]), 8, [wk] + NTK)
            act(slab2(BS, T), accview(i, TB), AF.Silu, acckeys(i), ["slBS"])
            yield
            tt("pool", R(B1[:, h, 0:T]), B1[:, h, 0:T], BS[:, 0:T], ALU.mult, ["B1.%d" % h, "slBS"], ["B1.%d" % h])
            yield

        def pre_head(h, T, nt, VT, slabset):
            (C, Ck), (Dd, Dk), (E, Ek), (F, Fk) = slabset
            TB = T // 2
            hoc, nhoc, f0c = DRV[:, HO + h:HO + h + 1], DRV[:, NHO + h:NHO + h + 1], DRV[:, F0 + h:F0 + h + 1]
            view, wk = load_unit(wh[h, :, :, 128:256], 8, 128)
            i = next_acc()
            kloop(i, TB, lambda kb: R(view[:, kb, 0:128]), lambda kb, tb: R(NT[:, kb, tb * TB:(tb + 1) * TB]), 8, [wk] + NTK)
            act(C[:, 0:T].rearrange("p (b t) -> p b t", b=2), accview(i, TB), AF.Tanh, acckeys(i), [Ck], scale=0.5)
            yield
            ts("dve", Dd[:, 0:T], C[:, 0:T], hoc, f0c, ALU.mult, ALU.add, [Ck, "DRV"], [Dk])
            act(E[:, 0:T], C[:, 0:T], AF.Identity, [Ck, "DRV"], [Ek], bias=hoc, scale=nhoc)
            yield
            act(Dd[:, 0:T], Dd[:, 0:T], AF.Ln, [Dk], [Dk])
            yield
            S.op("dve", lambda e: e.tensor_tensor_scan(out=F[:, T - 1::-1], data0=SMALL[:, 1:2].to_broadcast([128, T]),
                                                       data1=Dd[:, T - 1::-1], initial=0.0, op0=ALU.mult, op1=ALU.add),
                 reads=["SMALL", Dk], writes=[Fk], cost=0.06 + 2 * T / 960.0)
            yield
            act(C[:, 0:T - 1], F[:, 1:T], AF.Exp, [Fk], [Ck])
            S.op("pool", lambda e: e.memset(C[:, T - 1:T], 1.0), reads=[Ck], writes=[Ck], cost=0.1)
            act(SMALL[:, 16 + h:17 + h], F[:, 0:1], AF.Exp, [Fk], ["EBP%d" % h])
            yield
            tt("pool", C[:, 0:T], E[:, 0:T], C[:, 0:T], ALU.mult, [Ek, Ck], [Ck])
            yield
            ob = 6 + h % 2
            pc0 = h * 128 if h < NH - 1 else (h - 1) * 128
            for ti in range(nt):
                b, q = next_sm()
                tr(ps[:, b, q * 128:(q + 1) * 128], C[:, ti * 128:(ti + 1) * 128], ident, [Ck, "CSTp"], [smkey(b, q)])
                cp("act" if ti % 2 == 0 else "dve", R(SCM[:, ti, :]), ps[:, b, q * 128:(q + 1) * 128], [smkey(b, q)], ["SCM%d" % ti])
            for ti in range(nt):
                mm(ps[:, ob, 0:128], R(SCM[:, ti, :]), R(VT[:, ti, h * 128:(h + 1) * 128]), ti == 0, ti == nt - 1,
                   ["SCM%d" % ti, "VT%d" % ti], ["pb%d" % ob])
            stt(SM[:, h, :], SM[:, h, :], SMALL[:, 16 + h:17 + h], ps[:, ob, 0:128], ALU.mult, ALU.add,
                ["SM%d" % h, "EBP%d" % h, "pb%d" % ob], ["SM%d" % h])
            yield

        def interleave(gens, weights=None):
            gens = list(gens)
            weights = [1] * len(gens) if weights is None else list(weights)
            live = list(range(len(gens)))
            while live:
                for gi in list(live):
                    for _ in range(weights[gi]):
                        try:
                            next(gens[gi])
                        except StopIteration:
                            live.remove(gi)
                            break

        def gated_branch(T, wz_dram, wm_dram, srcbuf_key, first):
            TB = T // 2
            MIX = B2[:, 0:8 * T].rearrange("p (c t) -> p c t", c=8)
            tmp = [slabs["A0"], slabs["A1"]]
            tmpk = ["slA0", "slA1"]
            sg = [slabs["C0"], slabs["C1"]]
            sgk = ["slC0", "slC1"]
            for cp_ in range(4):
                vz, kz = load_unit(wz_dram[cp_], 8, 256)
                vm, km = load_unit(wm_dram[cp_], 8, 256)
                for cq in range(2):
                    cb = cp_ * 2 + cq
                    i = next_acc()
                    kloop(i, TB, lambda kb, cq=cq, vz=vz: R(vz[:, kb, cq * 128:(cq + 1) * 128]),
                          lambda kb, tb: R(NT[:, kb, tb * TB:(tb + 1) * TB]), 8, [kz] + NTK)
                    act(slab2(sg[cq], T), accview(i, TB), AF.Sigmoid, acckeys(i), [sgk[cq]])
                    i = next_acc()
                    kloop(i, TB, lambda kb, cq=cq, vm=vm: R(vm[:, kb, cq * 128:(cq + 1) * 128]),
                          lambda kb, tb: R(B1[:, kb, tb * TB:(tb + 1) * TB]), 8, [km] + ["B1.%d" % b for b in range(8)])
                    mixv = MIX[:, cb, :].rearrange("p (b t) -> p b t", b=2)
                    if first:
                        tt("dve", R(mixv), slab2(sg[cq], T), accview(i, TB), ALU.mult, [sgk[cq]] + acckeys(i), ["B2.%d" % cb])
                    else:
                        tt("dve", slab2(tmp[cq], T), slab2(sg[cq], T), accview(i, TB), ALU.mult, [sgk[cq]] + acckeys(i), [tmpk[cq]])
                        tt("pool", R(MIX[:, cb, :]), MIX[:, cb, :], tmp[cq][:, 0:T], ALU.add, ["B2.%d" % cb, tmpk[cq]], ["B2.%d" % cb])
            return MIX

        def conv_phase(T, Tp, nsamp, last):
            TB = T // 2
            CT, UE, Y = slabs["E"], slabs["F"], slabs["D"]
            for j in range(8):
                view, wk = load_unit(wc[j], 8, 384)
                w0 = PRM[:, P_CW + 0 * 8 + j:P_CW + 0 * 8 + j + 1]
                w1 = PRM[:, P_CW + 1 * 8 + j:P_CW + 1 * 8 + j + 1]
                w2 = PRM[:, P_CW + 2 * 8 + j:P_CW + 2 * 8 + j + 1]
                i = next_acc()
                kloop(i, TB, lambda kb: R(view[:, kb, 128:256]), lambda kb, tb: R(NT[:, kb, tb * TB:(tb + 1) * TB]), 8, [wk] + NTK)
                cp("act", slab2(CT, T), accview(i, TB), acckeys(i), ["slE"])
                i = next_acc()
                kloop(i, TB, lambda kb: R(view[:, kb, 256:384]), lambda kb, tb: R(NT[:, kb, tb * TB:(tb + 1) * TB]), 8, [wk] + NTK)
                tt("dve", UE[:, 2:2 + T].rearrange("p (b t) -> p b t", b=2), slab2(CT, T), accview(i, TB), ALU.mult,
                   ["slE"] + acckeys(i), ["slF"])
                cp("pool", UE[:, 0:2], UH[:, j, :], ["UH%d" % j], ["slF"])
                act(Y[:, 0:T], UE[:, 2:2 + T], AF.Copy, ["slF", "PRM"], ["slD"], scale=w2)
                stt(Y[:, 0:T], UE[:, 1:1 + T], w1, Y[:, 0:T], ALU.mult, ALU.add, ["slF", "slD", "PRM"], ["slD"])
                stt(Y[:, 0:T], UE[:, 0:T], w0, Y[:, 0:T], ALU.mult, ALU.add, ["slF", "slD", "PRM"], ["slD"])
                if nsamp:
                    cp("pool", UES[:, :, 2:66], UE[:, 2 + Tp:2 + Tp + 128].rearrange("p (s t) -> p s t", s=2), ["slF"], ["UES"])
                    cp("pool", UES[:, :, 0:2], CB[:, j, :, :], ["CB", "UES"], ["UES"])
                    act(YS[:, :, :], UES[:, :, 2:66], AF.Copy, ["UES", "PRM"], ["YS"], scale=w2)
                    stt(YS[:, :, :], UES[:, :, 1:65], w1, YS[:, :, :], ALU.mult, ALU.add, ["UES", "YS", "PRM"], ["YS"])
                    stt(Y[:, Tp:Tp + 128].rearrange("p (s t) -> p s t", s=2), UES[:, :, 0:64], w0, YS[:, :, :], ALU.mult, ALU.add,
                        ["UES", "YS", "PRM", "slD"], ["slD"])
                    cp("pool", CBO[:, j, 1:3, :], UES[:, :, 64:66], ["UES"], ["CBO"])
                cp("pool", UH[:, j, :], UE[:, Tp:Tp + 2], ["slF"], ["UH%d" % j])
                if last:
                    cp("pool", CBO[:, j, 0, :], UE[:, Tp:Tp + 2], ["slF", "CBO"], ["CBO"])
                i = next_acc()
                kloop(i, TB, lambda kb: R(view[:, kb, 0:128]), lambda kb, tb: R(NT[:, kb, tb * TB:(tb + 1) * TB]), 8, [wk] + NTK)
                tt("dve", R(B1[:, j, 0:T].rearrange("p (b t) -> p b t", b=2)), slab2(Y, T), accview(i, TB), ALU.mult,
                   ["slD"] + acckeys(i), ["B1.%d" % j])

        def resid_matmul(T, w_dram_units, nk, src_fn, src_keys, after_cb=None):
            TB = T // 2
            for cp_ in range(4):
                view, wk = load_unit(w_dram_units(cp_), nk, 256)
                for cq in range(2):
                    cb = cp_ * 2 + cq
                    i = next_acc()
                    kloop(i, TB, lambda kb, cq=cq, view=view: R(view[:, kb, cq * 128:(cq + 1) * 128]),
                          lambda kb, tb: src_fn(kb, tb), nk, [wk] + src_keys)
                    xv = XT[:, cb, 0:T].rearrange("p (b t) -> p b t", b=2)
                    tt("dve", xv, xv, accview(i, TB), ALU.add, ["XT%d" % cb] + acckeys(i), ["XT%d" % cb])
                    if after_cb is not None and cb >= 1:
                        after_cb(cb - 1)
            if after_cb is not None:
                after_cb(7)

        def ffn_phase(T):
            TB = T // 2
            GS = [slabs["A0"], slabs["A1"]]
            GSk = ["slA0", "slA1"]
            for g, (j0, nj) in enumerate(FFG):
                for jj in range(nj):
                    j = j0 + jj
                    view, wk = load_unit(wgu[j], 8, 256)
                    i = next_acc()
                    kloop(i, TB, lambda kb: R(view[:, kb, 0:128]), lambda kb, tb: R(NT[:, kb, tb * TB:(tb + 1) * TB]), 8, [wk] + NTK)
                    act(slab2(GS[jj % 2], T), accview(i, TB), AF.Silu, acckeys(i), [GSk[jj % 2]])
                    i = next_acc()
                    kloop(i, TB, lambda kb: R(view[:, kb, 128:256]), lambda kb, tb: R(NT[:, kb, tb * TB:(tb + 1) * TB]), 8, [wk] + NTK)
                    tt("dve", R(B1[:, jj, 0:T].rearrange("p (b t) -> p b t", b=2)), slab2(GS[jj % 2], T), accview(i, TB), ALU.mult,
                       [GSk[jj % 2]] + acckeys(i), ["B1.%d" % jj])
                resid_matmul(T, lambda cp_, g=g, nj=nj: wd[g * 4 + cp_, :, 0:nj, :], nj,
                             lambda kb, tb: R(B1[:, kb, tb * TB:(tb + 1) * TB]), ["B1.%d" % b for b in range(nj)],
                             after_cb=(lambda cb: norm_stats_blk(cb, T)) if g == len(FFG) - 1 else None)

        def ple_phase(T, nt, rows):
            TB = T // 2
            PT = B2[:, 0:2 * T].rearrange("p (c t) -> p c t", c=2)
            t0 = rows[0]
            for kb in range(2):
                dma(R(PT[:, kb, 0:T]), R(pmain[:, kb, t0:t0 + T]), [], ["B2.%d" % kb], "pl%d" % kb)
            sg = [slabs["C0"], slabs["C1"]]
            sgk = ["slC0", "slC1"]
            tmp = [slabs["A0"], slabs["A1"]]
            tmpk = ["slA0", "slA1"]
            for cp_ in range(4):
                vz, kz = load_unit(wpg[cp_], 8, 256)
                vm, km = load_unit(wpl[cp_], 2, 256)
                for cq in range(2):
                    cb = cp_ * 2 + cq
                    i = next_acc()
                    kloop(i, TB, lambda kb, cq=cq, vz=vz: R(vz[:, kb, cq * 128:(cq + 1) * 128]),
                          lambda kb, tb: R(NT[:, kb, tb * TB:(tb + 1) * TB]), 8, [kz] + NTK)
                    act(slab2(sg[cq], T), accview(i, TB), AF.Sigmoid, acckeys(i), [sgk[cq]])
                    i = next_acc()
                    kloop(i, TB, lambda kb, cq=cq, vm=vm: R(vm[:, kb, cq * 128:(cq + 1) * 128]),
                          lambda kb, tb: R(PT[:, kb, tb * TB:(tb + 1) * TB]), 2, [km, "B2.0", "B2.1"])
                    tt("dve", slab2(tmp[cq], T), slab2(sg[cq], T), accview(i, TB), ALU.mult, [sgk[cq]] + acckeys(i), [tmpk[cq]])
                    tt("pool", XT[:, cb, 0:T], XT[:, cb, 0:T], tmp[cq][:, 0:T], ALU.add, ["XT%d" % cb, tmpk[cq]], ["XT%d" % cb])
                    if cb >= 1:
                        norm_stats_blk(cb - 1, T)
            norm_stats_blk(7, T)

        def final_out(T, nt, rows, next_load=None):
            TB = T // 2
            act(slab2(RSTD, T), ps[:, 6:8, 0:TB], AF.Ln, OBK + ["SMALL"], ["slA0"], bias=epsc)
            act(RSTD[:, 0:T], RSTD[:, 0:T], AF.Exp, ["slA0"], ["slA0"], scale=-0.5)
            t0 = rows[0]
            ybuf = [(slabs[n], "sl" + n) for n in ("A1", "BS", "C0", "C1", "D", "E", "F")] + [(XT[:, 7, :], "XT7")]
            for blk in range(8):
                yb, yk = ybuf[blk]
                stt(yb[:, 0:T], XT[:, blk, 0:T], PRM[:, P_GFIN + blk:P_GFIN + blk + 1], RSTD[:, 0:T],
                    ALU.mult, ALU.mult, ["XT%d" % blk, "slA0", "PRM"], [yk])
                dma(yout[:, blk, t0:t0 + T], yb[:, 0:T], [yk], ["yout%d" % blk], "yo%d" % blk)
                if next_load is not None and blk < 7:
                    next_load(blk)
            if next_load is not None:
                next_load(7)

        def run_pass(tiles, main, first_main, last_main, last_pre, preloaded=False, next_tiles=None):
            nt = len(tiles)
            T = nt * 128
            state_only = not main
            src = xmain if main else xpre
            rows = [g * 128 for g in tiles]
            nsamp = 1 if (main and samp_tile in tiles) else 0
            Tp = T - 128 * nsamp

            def chunk_state(c, h):
                if nsamp and c >= 2 * (nt - 1):
                    k = c - 2 * (nt - 1)
                    return SS[k][:, h, :], "SS%d.%d" % (k, h)
                return SM[:, h, :], "SM%d" % h

            fence(["B2.%d" % b for b in range(8)], ["VT%d" % t for t in range(6)])
            if not preloaded:
                load_x_and_transpose(src, rows, nt)
            norm_to_NT(T, P_GMIX)
            VT = B2[:, 0:nt * D].rearrange("p (t c) -> p t c", t=nt)
            if state_only:
                sets = [[(slabs[n], "sl" + n) for n in ("A0", "A1", "BS", "F")],
                        [(slabs[n], "sl" + n) for n in ("C0", "C1", "D", "E")],
                        [(XT[:, b, :], "XT%d" % b) for b in range(0, 4)],
                        [(XT[:, b, :], "XT%d" % b) for b in range(4, 8)]]
                interleave([v_phase(T, nt, VT)] + [pre_head(i, T, nt, VT, sets[i]) for i in range(4)], [4, 1, 1, 1, 1])
                interleave([pre_head(4 + i, T, nt, VT, sets[i]) for i in range(4)])
            else:
                interleave([head_proj(0, T, state_only)])
                interleave([v_phase(T, nt, VT), head_chain(0, T, 0), head_chain(0, T, 1), head_proj(1, T, state_only)],
                           [3, 1, 1, 1])
            for h in (range(NH) if not state_only else ()):
                gens, wts = [head_rec(h, T, nt, VT, nsamp)], [REC_W]
                if h + 1 < NH:
                    gens += [head_chain(h + 1, T, 0), head_chain(h + 1, T, 1)]
                    wts += [2, 2]
                if h + 2 < NH:
                    gens.append(head_proj(h + 2, T, state_only))
                    wts.append(1)
                interleave(gens, wts)
            if state_only:
                if last_pre:
                    for j in range(8):
                        view, wk = load_unit(wc[j], 8, 384)
                        b, q = next_sm()
                        for kb in range(8):
                            mm(ps[:, b, q * 128:q * 128 + 2], R(view[:, kb, 128:256]), R(NT[:, kb, T - 2:T]), kb == 0, kb == 7,
                               [wk] + NTK, [smkey(b, q)])
                        cp("act", SMALL[:, 8:10], ps[:, b, q * 128:q * 128 + 2], [smkey(b, q)], ["SMu"])
                        b2, q2 = next_sm()
                        for kb in range(8):
                            mm(ps[:, b2, q2 * 128:q2 * 128 + 2], R(view[:, kb, 256:384]), R(NT[:, kb, T - 2:T]), kb == 0, kb == 7,
                               [wk] + NTK, [smkey(b2, q2)])
                        tt("dve", UH[:, j, :], SMALL[:, 8:10], ps[:, b2, q2 * 128:q2 * 128 + 2], ALU.mult,
                           ["SMu", smkey(b2, q2)], ["UH%d" % j])
                return
            fence(["VT%d" % t for t in range(6)], ["B2.%d" % b for b in range(8)])
            gated_branch(T, wza, wa, None, True)
            conv_phase(T, Tp, nsamp, last_main)
            MIX = gated_branch(T, wzb, wb, None, False)
            TB = T // 2
            resid_matmul(T, lambda cp_: wo[cp_], 8, lambda kb, tb: R(MIX[:, kb, tb * TB:(tb + 1) * TB]),
                         ["B2.%d" % b for b in range(8)], after_cb=lambda cb: norm_stats_blk(cb, T))
            norm_finish(T, P_GFFN)
            ffn_phase(T)
            norm_finish(T, P_GPLE)
            ple_phase(T, nt, rows)
            if next_tiles is not None:
                nT_, nt0 = len(next_tiles) * 128, next_tiles[0] * 128
                final_out(T, nt, rows, lambda blk: dma(XT[:, blk, 0:nT_], xmain[:, blk, nt0:nt0 + nT_], [], ["XT%d" % blk], "xl%d" % blk))
            else:
                final_out(T, nt, rows)

        for pi, tiles in enumerate(pre_passes):
            run_pass(tiles, False, False, False, pi == len(pre_passes) - 1)
        for pi, tiles in enumerate(main_passes):
            run_pass(tiles, True, pi == 0, pi == len(main_passes) - 1, False, preloaded=(pi > 0),
                     next_tiles=(main_passes[pi + 1] if pi + 1 < len(main_passes) else None))

        dma(sfin[0].rearrange("h k v -> k h v"), SM[:, :, :], ["SM%d" % h for h in range(NH)], ["sfin0"], "o0")
        for k in range(2):
            dma(sfin[1 + k].rearrange("h k v -> k h v"), SS[k][:, :, :], ["SS%d.%d" % (k, h) for h in range(NH)], ["sfin%d" % (1 + k)], "o%d" % (1 + k))
        for blk in range(8):
            b, q = next_sm()
            tr(ps[0:6, b, q * 128:(q + 1) * 128], CBO[:, blk, :, :].rearrange("p s r -> p (s r)"), ident, ["CBO", "CSTp"], [smkey(b, q)])
            cp("dve", COUT[:, blk * 128:(blk + 1) * 128], ps[0:6, b, q * 128:(q + 1) * 128], [smkey(b, q)], ["XIN0"])
        dma(cfin.rearrange("s r d -> (s r) d"), COUT[:, :], ["XIN0"], ["cfin"], "o3")

        cnt = S.resolve()
        sems = {}
        for key in cnt:
            sems[key] = es.enter_context(nc.semaphore("s_%s_%s" % key))
        block = es.enter_context(nc.Block())
        S.emit(nc, block, sems)
    return nc


def _tile_cols(W, col_lists, kb, width):
    out = np.zeros((len(col_lists), 128, kb, width), np.float32)
    Wr = W.reshape(kb, 128, W.shape[1])
    for u, cols in enumerate(col_lists):
        out[u, :, :, :len(cols)] = np.transpose(Wr[:, :, cols], (1, 0, 2))
    return out


def _fm(x):
    t, d = x.shape
    return np.ascontiguousarray(np.transpose(x.reshape(t, d // 128, 128), (2, 1, 0)))


def _pcol(v):
    return np.ascontiguousarray(v.reshape(-1, 128).T)


_NC_CACHE = {}


def _prep_shared(lower_bounds, norm_mix, w_in, conv_w, hg_norm, w_branch_a, w_branch_b, w_out, norm_ffn,
                 w_gate_up, w_down, norm_ple, w_ple, w_ple_gate, norm_final):
    f = lambda a: np.ascontiguousarray(np.asarray(a, dtype=np.float32))
    win = f(w_in)[0]
    ar = np.arange
    HF, HV, CD = 1024, 1024, 1024
    oq, of_, oi, og, oB, oC, oh, oza, ozb = 0, 1024, 2048, 3072, 4096, 5120, 6144, 7168, 8192
    wv_ = _tile_cols(win, [list(oi + ar(0, 384)), list(oi + ar(384, 768)), list(oi + ar(768, 1024))], 8, 384)
    wh_ = _tile_cols(win, [list(oq + h * 128 + ar(128)) + list(of_ + h * 128 + ar(128)) + list(og + h * 128 + ar(128))
                           for h in range(8)], 8, 384)
    wc_ = _tile_cols(win, [list(oB + j * 128 + ar(128)) + list(oC + j * 128 + ar(128)) + list(oh + j * 128 + ar(128))
                           for j in range(8)], 8, 384)
    c256 = [list(c * 256 + ar(256)) for c in range(4)]
    wza_ = _tile_cols(win[:, oza:oza + 1024], c256, 8, 256)
    wzb_ = _tile_cols(win[:, ozb:ozb + 1024], c256, 8, 256)
    wa_ = _tile_cols(f(w_branch_a)[0], c256, 8, 256)
    wb_ = _tile_cols(f(w_branch_b)[0], c256, 8, 256)
    wo_ = _tile_cols(f(w_out)[0], c256, 8, 256)
    wpg_ = _tile_cols(f(w_ple_gate)[0], c256, 8, 256)
    wgu_full = f(w_gate_up)[0]
    wgu_ = _tile_cols(wgu_full, [list(j * 128 + ar(128)) + list(DFF + j * 128 + ar(128)) for j in range(22)], 8, 256)
    wdn = f(w_down)[0]
    wd_ = np.zeros((12, 128, 8, 256), np.float32)
    for g, (j0, nj) in enumerate(FFG):
        sub = wdn[j0 * 128:(j0 + nj) * 128]
        wd_[g * 4:(g + 1) * 4, :, :nj, :] = _tile_cols(sub, c256, nj, 256)
    wpl_ = _tile_cols(f(w_ple)[0], c256, 2, 256)

    prm = np.zeros((128, 96), np.float32)
    lbs = f(lower_bounds)
    prm[:, 0:8] = _pcol(lbs[0])
    prm[:, 8:16] = _pcol(lbs[1])
    prm[:, 16:24] = _pcol(f(norm_mix)[0])
    prm[:, 24:32] = _pcol(f(norm_ffn)[0])
    prm[:, 32:40] = _pcol(f(norm_ple)[0])
    prm[:, 40] = f(hg_norm)[0]
    cw = f(conv_w)[0]
    for tap in range(3):
        prm[:, 41 + tap * 8:41 + tap * 8 + 8] = _pcol(cw[tap])
    cst = np.zeros((128, 4, 128), np.float32)
    cst[:, 0, :] = np.eye(128, dtype=np.float32)
    s_i, t_i = np.meshgrid(ar(128), ar(128), indexing="ij")
    cst[:, 1, :] = ((s_i // 64 == t_i // 64) & (s_i <= t_i)).astype(np.float32)
    cst[:, 2, :] = 1.0 / D
    cst[:, 3, :] = 1.0 / 128
    prm[:, 72:80] = _pcol(f(norm_final))

    shared = dict(prm=prm, cst=cst, wv=wv_, wh=wh_, wza=wza_, wa=wa_, wzb=wzb_, wb=wb_, wo=wo_,
                  wpg=wpg_, wc=wc_, wgu=wgu_, wd=wd_, wpl=wpl_)
    return shared


def kernel(x_prompt, x_sample, p_prompt, p_sample, state_hgrn, state_conv, lower_bounds,
           norm_mix, w_in, conv_w, hg_norm, w_branch_a, w_branch_b, w_out, norm_ffn,
           w_gate_up, w_down, norm_ple, w_ple, w_ple_gate, norm_final):
    f = lambda a: np.ascontiguousarray(np.asarray(a, dtype=np.float32))
    x_prompt, x_sample, p_prompt, p_sample = f(x_prompt), f(x_sample), f(p_prompt), f(p_sample)
    state_hgrn, state_conv = f(state_hgrn), f(state_conv)
    shared = _prep_shared(lower_bounds, norm_mix, w_in, conv_w, hg_norm, w_branch_a, w_branch_b, w_out, norm_ffn,
                          w_gate_up, w_down, norm_ple, w_ple, w_ple_gate, norm_final)
    in_maps = []
    for c in range(NCORES):
        b, half = c // 2, c % 2
        xm = _fm(np.concatenate([x_prompt[b, half * 2048:(half + 1) * 2048], x_sample[2 * c], x_sample[2 * c + 1]], axis=0))
        pm = _fm(np.concatenate([p_prompt[0, b, half * 2048:(half + 1) * 2048], p_sample[0, 2 * c], p_sample[0, 2 * c + 1]], axis=0))
        xp = _fm(x_prompt[b, 0:2048]) if half == 1 else np.zeros((128, 8, TPRE), np.float32)
        s0 = np.zeros((3, NH, 128, 128), np.float32)
        s0[1] = state_hgrn[0, 2 * c]
        s0[2] = state_hgrn[0, 2 * c + 1]
        cb = np.stack([state_conv[0, 2 * c], state_conv[0, 2 * c + 1]], axis=0)
        m = dict(shared)
        m.update(xpre=np.ascontiguousarray(xp), xmain=np.ascontiguousarray(xm), pmain=np.ascontiguousarray(pm),
                 s0=s0, cbuf=np.ascontiguousarray(cb))
        in_maps.append(m)

    if "nc" not in _NC_CACHE:
        _NC_CACHE["nc"] = build_program()
    nc = _NC_CACHE["nc"]
    res = run_bass_kernel_spmd(nc, in_maps, core_ids=list(range(NCORES)))
    R_ = res.results
    y_prompt = np.zeros((4, 4096, D), np.float32)
    y_sample = np.zeros((16, 64, D), np.float32)
    hg_p = np.zeros((1, 4, NH, 128, 128), np.float32)
    cv_p = np.zeros((1, 4, 2, D), np.float32)
    hg_s = np.zeros((1, 16, NH, 128, 128), np.float32)
    cv_s = np.zeros((1, 16, 2, D), np.float32)
    for c in range(NCORES):
        b, half = c // 2, c % 2
        y = np.ascontiguousarray(np.transpose(R_[c]["y"], (2, 1, 0))).reshape(TMAIN, D)
        y_prompt[b, half * 2048:(half + 1) * 2048] = y[0:2048]
        y_sample[2 * c] = y[2048:2112]
        y_sample[2 * c + 1] = y[2112:2176]
        sf, cf = R_[c]["sfin"], R_[c]["cfin"]
        hg_s[0, 2 * c], hg_s[0, 2 * c + 1] = sf[1], sf[2]
        cv_s[0, 2 * c], cv_s[0, 2 * c + 1] = cf[1], cf[2]
        if half == 1:
            hg_p[0, b] = sf[0]
            cv_p[0, b] = cf[0]
    return (y_prompt, y_sample, hg_p, cv_p, hg_s, cv_s)
```

```python
import numpy as np
import concourse.bass as bass
import concourse.mybir as mybir
from concourse.bass_utils import run_bass_kernel_spmd

F32 = mybir.dt.float32
F32R = mybir.dt.float32r
AF = mybir.ActivationFunctionType
ALU = mybir.AluOpType

D = 1024
NH = 8
DFF = 2816
PLE = 256
EPS = 1e-6
NCORES = 8
TPRE = 2048
TMAIN = 2176
MAIN_PASSES = [list(range(0, 6)), list(range(6, 12)), list(range(12, 17))]
PRE_PASSES = [list(range(0, 6)), list(range(6, 12)), list(range(12, 16))]
TMAX = 768
SLOTW = 3072
NSLOT = 3
FFG = [(0, 8), (8, 8), (16, 6)]
REC_W = 4
LIST_SCHED = True
PRIO_BLEVEL = False
TABLE_AWARE = True


SPLIT_KEYS = {"slA0", "slA1", "slC0", "slC1", "slD", "slE", "slF", "QH0", "QH1", "KT", "EB0", "EB1"}


class Op:
    __slots__ = ("eng", "fn", "reads", "writes", "dma", "deps", "signal", "sigval", "idx", "cost", "odeps", "tset")

    def __init__(self, eng, fn, reads, writes, dma, cost=0.3):
        self.eng, self.fn, self.reads, self.writes, self.dma = eng, fn, tuple(reads), tuple(writes), dma
        self.cost = cost
        self.tset = -1
        self.odeps = ()
        self.deps = ()
        self.signal = False
        self.sigval = 0


class Sched:
    ENGS = ("pe", "act", "dve", "pool", "sp")

    def __init__(self):
        self.ops = []

    def op(self, eng, fn, reads=(), writes=(), dma=None, cost=0.3):
        def _exp(keys):
            out = []
            for k in keys:
                if k in SPLIT_KEYS:
                    out += [k + "#0", k + "#1"]
                else:
                    out.append(k)
            return out
        reads, writes = _exp(reads), _exp(writes)
        writes = list(writes) + [k for k in reads if k.startswith("pb") and k not in writes]
        o = Op(eng, fn, reads, writes, dma, cost)
        o.idx = len(self.ops)
        self.ops.append(o)
        return o

    def resolve(self):
        last_w = {}
        readers = {}
        for o in self.ops:
            deps = {}
            rset = set(o.reads)
            for k in o.reads:
                d = last_w.get(k)
                if d is not None:
                    deps[d.idx] = True
            for k in o.writes:
                d = last_w.get(k)
                if d is not None:
                    deps.setdefault(d.idx, False)
                for r in readers.get(k, ()):
                    deps.setdefault(r.idx, False)
            need = []
            o.odeps = [self.ops[di] for di in deps if di != o.idx]
            for di, raw in deps.items():
                d = self.ops[di]
                if d is o:
                    continue
                if d.dma is None and d.eng == o.eng:
                    if o.eng == "pe":
                        continue
                need.append(d)
                d.signal = True
            o.deps = need
            for k in o.reads:
                readers.setdefault(k, []).append(o)
            for k in o.writes:
                last_w[k] = o
                readers[k] = []
        self.schedule()
        cnt = {}
        for o in self.issue_order:
            if o.signal or o.dma is not None:
                key = ("dma", o.dma) if o.dma is not None else ("eng", o.eng)
                step = 16 if o.dma is not None else 1
                cnt[key] = cnt.get(key, 0) + step
                o.sigval = cnt[key]
        self.final = dict(cnt)
        return cnt

    def schedule(self):
        import heapq
        ops = self.ops
        if not LIST_SCHED:
            self.issue_order = list(ops)
            self.order = {e: [o for o in ops if o.eng == e] for e in self.ENGS}
            return
        nleft = [len(o.odeps) for o in ops]
        users = [[] for _ in ops]
        for o in ops:
            for d in o.odeps:
                users[d.idx].append(o)
        finish = [0.0] * len(ops)
        ready_t = [0.0] * len(ops)
        blevel = [0.0] * len(ops)
        for o in reversed(ops):
            m = 0.0
            for u in users[o.idx]:
                if blevel[u.idx] > m:
                    m = blevel[u.idx]
            blevel[o.idx] = o.cost + m
        pend = {e: [] for e in self.ENGS}
        avail = {e: [] for e in self.ENGS}
        free = {e: 0.0 for e in self.ENGS}
        for o in ops:
            if nleft[o.idx] == 0:
                heapq.heappush(pend[o.eng], (0.0, o.idx))
        order = {e: [] for e in self.ENGS}
        issue = []
        HOP = 0.7
        cur_tset = [-1]
        done = 0
        while done < len(ops):
            best = None
            for e in self.ENGS:
                while pend[e] and pend[e][0][0] <= free[e]:
                    pi_ = heapq.heappop(pend[e])[1]
                    heapq.heappush(avail[e], ((-blevel[pi_], pi_) if PRIO_BLEVEL else (pi_, pi_)))
                if avail[e]:
                    cand = (free[e], avail[e][0][1], e, True)
                elif pend[e]:
                    cand = (pend[e][0][0], pend[e][0][1], e, False)
                else:
                    continue
                if best is None or cand[:2] < best[:2]:
                    best = cand
            start, idx, e, from_avail = best
            if from_avail:
                if e == "act" and TABLE_AWARE and len(avail[e]) > 1:
                    cands = heapq.nsmallest(3, avail[e])
                    pick = None
                    for c_ in cands:
                        ts_ = ops[c_[1]].tset
                        if ts_ == -1 or ts_ == cur_tset[0]:
                            pick = c_
                            break
                    if pick is not None and pick is not cands[0]:
                        avail[e].remove(pick)
                        heapq.heapify(avail[e])
                        idx = pick[1]
                    else:
                        heapq.heappop(avail[e])
                else:
                    heapq.heappop(avail[e])
            else:
                heapq.heappop(pend[e])
            o = ops[idx]
            if e == "act" and o.tset != -1:
                if o.tset != cur_tset[0]:
                    start += 1.28
                cur_tset[0] = o.tset
            issue_cost = 0.07 if o.dma is not None else o.cost
            free[e] = start + issue_cost
            finish[idx] = start + o.cost
            order[e].append(o)
            issue.append(o)
            done += 1
            for u in users[idx]:
                lat = 0.0 if (u.eng == e and o.dma is None) else HOP
                ready_t[u.idx] = max(ready_t[u.idx], finish[idx] + lat)
                nleft[u.idx] -= 1
                if nleft[u.idx] == 0:
                    heapq.heappush(pend[u.eng], (ready_t[u.idx], u.idx))
        self.order = order
        self.issue_order = issue

    def emit(self, nc, block, sems):
        engmap = {"pe": "tensor", "act": "scalar", "dve": "vector", "pool": "gpsimd", "sp": "sync"}
        sched = self

        def run(engname, e):
            waited = {}
            for o in sched.order[engname]:
                want = {}
                for d in o.deps:
                    key = ("dma", d.dma) if d.dma is not None else ("eng", d.eng)
                    if d.sigval > want.get(key, 0):
                        want[key] = d.sigval
                for key, val in want.items():
                    if val > waited.get(key, 0):
                        e.wait_ge(sems[key], val)
                        waited[key] = val
                if o.fn is None:
                    continue
                ins = o.fn(e)
                if o.dma is not None:
                    ins.then_inc(sems[("dma", o.dma)], 16)
                elif o.signal:
                    ins.then_inc(sems[("eng", o.eng)], 1)
            if engname == "sp":
                for key, val in sched.final.items():
                    if key[0] == "dma" and val > waited.get(key, 0):
                        e.wait_ge(sems[key], val)

        for engname in self.ENGS:
            getattr(block, engmap[engname])(lambda e, _n=engname: run(_n, e))


def build_program(pre_passes=None, main_passes=None, samp_tile=16):
    pre_passes = PRE_PASSES if pre_passes is None else pre_passes
    main_passes = MAIN_PASSES if main_passes is None else main_passes
    nc = bass.Bass("TRN2", target_bir_lowering=False)
    nc.dge_precook = False
    S = Sched()

    def din(name, shape):
        return nc.dram_tensor(name, list(shape), F32, kind="ExternalInput").ap()

    def dout(name, shape):
        return nc.dram_tensor(name, list(shape), F32, kind="ExternalOutput").ap()

    xpre = din("xpre", [128, 8, TPRE])
    xmain = din("xmain", [128, 8, TMAIN])
    pmain = din("pmain", [128, 2, TMAIN])
    s0 = din("s0", [3, NH, 128, 128])
    cbuf = din("cbuf", [2, 2, D])
    prm = din("prm", [128, 96])
    cst = din("cst", [128, 4, 128])
    wv = din("wv", [3, 128, 8, 384])
    wh = din("wh", [8, 128, 8, 384])
    wza = din("wza", [4, 128, 8, 256])
    wa = din("wa", [4, 128, 8, 256])
    wzb = din("wzb", [4, 128, 8, 256])
    wb = din("wb", [4, 128, 8, 256])
    wo = din("wo", [4, 128, 8, 256])
    wpg = din("wpg", [4, 128, 8, 256])
    wc = din("wc", [8, 128, 8, 384])
    wgu = din("wgu", [22, 128, 8, 256])
    wd = din("wd", [12, 128, 8, 256])
    wpl = din("wpl", [4, 128, 2, 256])
    yout = dout("y", [128, 8, TMAIN])
    sfin = dout("sfin", [3, NH, 128, 128])
    cfin = dout("cfin", [3, 2, D])

    import contextlib
    es = contextlib.ExitStack()

    def sb(name, shape):
        return es.enter_context(nc.sbuf_tensor(name, list(shape), F32))

    with es:
        XT = sb("XT", [128, 8, TMAX])
        NT = sb("NT", [128, 8, TMAX])
        B1 = sb("B1", [128, 8, TMAX])
        B2 = sb("B2", [128, 8 * TMAX])
        SLW = TMAX + 8
        slabs = {n: sb("sl" + n, [128, SLW]) for n in ("A0", "A1", "BS", "C0", "C1", "D", "E", "F")}
        QH = [sb("QH%d" % i, [128, TMAX]) for i in range(2)]
        KT = sb("KT", [128, TMAX])
        SQ = [sb("SQ%d" % i, [128, TMAX]) for i in range(2)]
        KHZ = [sb("KHZ%d" % i, [128, 6, 128]) for i in range(2)]
        SCM = sb("SCM", [128, 6, 128])
        RSTD = slabs["A0"]
        RM = sb("RM", [128, TMAX])
        WS = [sb("WS%d" % i, [128, SLOTW]) for i in range(NSLOT)]
        XIN = [sb("XIN0", [6, D])]
        SM = sb("SM", [128, NH, 128])
        SS = [sb("SS%d" % i, [128, NH, 128]) for i in range(2)]
        ST = [sb("ST%d" % i, [128, 128]) for i in range(4)]
        SHD = [sb("SHD%d" % i, [128, 128]) for i in range(4)]
        CST = sb("CST", [128, 2, 128])
        CSTB = sb("CSTB", [128, 2, 128])
        PRM = sb("PRM", [128, 96])
        DRV = sb("DRV", [128, 64])
        EB = sb("EB", [128, 2, 16])
        UH = sb("UH", [128, 8, 2])
        UES = sb("UES", [128, 2, 66])
        YS = sb("YS", [128, 2, 64])
        CB = sb("CB", [128, 8, 2, 2])
        CBO = sb("CBO", [128, 8, 3, 2])
        COUT = XIN[0][0:6, :]
        CTOK = COUT
        SMALL = sb("SMALL", [128, 32])
        FEN = sb("FEN", [128, 4])
        ps = es.enter_context(nc.psum_tensor("ps", [128, 8, 512], F32))

        ident = CST[:, 0, :]
        mask = CST[:, 1, :]
        onesd = CSTB[:, 0, :].bitcast(F32R)
        onesv = CSTB[:, 1, :].bitcast(F32R)
        P_L0, P_L1, P_GMIX, P_GFFN, P_GPLE, P_HGN, P_CW, P_GFIN = 0, 8, 16, 24, 32, 40, 41, 72
        LB, OML, NOML, HO, NHO, F0 = 0, 8, 16, 32, 40, 48
        epsc = SMALL[:, 0:1]

        dma_ctr = [0]

        def fsz(ap):
            n = 1
            for d in ap.shape[1:]:
                n *= d
            return n

        def dma(out, in_, reads, writes, key):
            S.op("sp", lambda e, o=out, i=in_: e.dma_start(out=o, in_=i), reads=reads, writes=writes, dma=key,
                 cost=2.0 + out.shape[0] * fsz(out) * 4 / 250e3)

        def ecost(eng, n):
            return {"act": 0.22 + n / 1200.0, "dve": 0.06 + n / 960.0, "pool": 0.1 + n / 420.0}[eng]

        def act(out, in_, func, reads, writes, bias=None, scale=None):
            kw = {}
            if bias is not None:
                kw["bias"] = bias
            if scale is not None:
                kw["scale"] = scale
            o_ = S.op("act", lambda e, o=out, i=in_, f=func, k=kw: e.activation(out=o, in_=i, func=f, **k),
                      reads=reads, writes=writes, cost=ecost("act", fsz(out)))
            o_.tset = {AF.Silu: 18, AF.Tanh: 18, AF.Sigmoid: 2, AF.Ln: 6, AF.Exp: 6}.get(func, -1)

        def tt(eng, out, in0, in1, op, reads, writes):
            S.op(eng, lambda e, o=out, a=in0, b=in1, p=op: e.tensor_tensor(out=o, in0=a, in1=b, op=p),
                 reads=reads, writes=writes, cost=ecost(eng, fsz(out)))

        def ts(eng, out, in0, s1, s2, op0, op1, reads, writes):
            if op1 is None:
                S.op(eng, lambda e, o=out, a=in0, x=s1, p0=op0: e.tensor_scalar(out=o, in0=a, scalar1=x, scalar2=None, op0=p0),
                     reads=reads, writes=writes, cost=ecost(eng, fsz(out)))
            else:
                S.op(eng, lambda e, o=out, a=in0, x=s1, y=s2, p0=op0, p1=op1:
                     e.tensor_scalar(out=o, in0=a, scalar1=x, scalar2=y, op0=p0, op1=p1), reads=reads, writes=writes,
                     cost=ecost(eng, fsz(out)))

        def stt(out, in0, scalar, in1, op0, op1, reads, writes):
            S.op("dve", lambda e, o=out, a=in0, s=scalar, b=in1, p0=op0, p1=op1:
                 e.scalar_tensor_tensor(out=o, in0=a, scalar=s, in1=b, op0=p0, op1=p1), reads=reads, writes=writes,
                 cost=ecost("dve", fsz(out)))

        def cp(eng, out, in_, reads, writes):
            if eng == "act":
                act(out, in_, AF.Copy, reads, writes)
            else:
                S.op(eng, lambda e, o=out, i=in_: e.tensor_copy(out=o, in_=i), reads=reads, writes=writes, cost=ecost(eng, fsz(out)))

        def mm(out, lhsT, rhs, start, stop, reads, writes):
            nmov = fsz(rhs)
            S.op("pe", lambda e, o=out, l=lhsT, r=rhs, a=start, b=stop: e.matmul(o, l, r, start=a, stop=b),
                 reads=reads, writes=writes, cost=(nmov if nmov >= 256 else 4 * nmov) / 1900.0 + 0.02)

        def tr(out, in_, idn, reads, writes):
            S.op("pe", lambda e, o=out, i=in_, d=idn: e.transpose(o, i, d), reads=reads, writes=writes, cost=0.28)

        def fence(reads, writes):
            S.op("pool", lambda e: e.memset(FEN[:, 0:1], 0.0), reads=reads, writes=list(writes) + ["FEN"])

        def R(ap):
            return ap.bitcast(F32R)

        slot_ctr = [0]

        def load_unit(dram_ap, kb, w):
            i = slot_ctr[0] % NSLOT
            slot_ctr[0] += 1
            view = WS[i][:, 0:kb * w].rearrange("p (k w) -> p k w", k=kb)
            dma(R(view), R(dram_ap), reads=[], writes=["WS%d" % i], key="WS%d" % i)
            return view, "WS%d" % i

        accsel = [0]

        def next_acc():
            i = accsel[0] % 2
            accsel[0] += 1
            return i

        smsel = [0]

        def next_sm():
            n = smsel[0]
            smsel[0] += 1
            return 4 + n % 2, (n // 2) % 4

        sm4sel = [0]

        def next_sm4():
            n = sm4sel[0]
            sm4sel[0] += 1
            return (4, 5, 6, 7)[n % 4], (n // 4) % 4

        sm6sel = [0]

        def next_sm6():
            n = sm6sel[0]
            sm6sel[0] += 1
            return (4, 5, 0, 1, 2, 3)[n % 6], 0

        def smkey(b, q):
            return "pb%d" % b

        def acckeys(i):
            return ["pb%d" % (2 * i), "pb%d" % (2 * i + 1)]

        OBK = ["pb6", "pb7"]
        SMALLK = [smkey(4, q) for q in range(4)] + [smkey(5, q) for q in range(4)]

        dma(CST[:, 0:2, :], cst[:, 0:2, :], [], ["CSTp"], "c0")
        dma(R(CSTB[:, :, :]), R(cst[:, 2:4, :]), [], ["CST"], "c0b")
        dma(PRM[:, :], prm, [], ["PRM"], "c1")
        S.op("pool", lambda e: e.memset(SMALL[:, 1:2], 1.0), reads=[], writes=["SMALLa"])
        S.op("pool", lambda e: e.memset(SMALL[:, 0:1], EPS), reads=["SMALLa"], writes=["SMALL"])
        for i in range(2):
            ts("dve", R(KHZ[i][:, :, :]), CST[:, 0:1, :].to_broadcast([128, 6, 128]), 0.0, None, ALU.mult, None,
               ["CSTp"], ["KHZ%d.%d" % (i, t) for t in range(6)])
        S.op("pool", lambda e: e.memset(RM[:, :], 1.0), reads=[], writes=["RMa"])
        S.op("pool", lambda e: e.memset(RM[:, 0::64], 0.0), reads=["RMa"], writes=["RM"])
        tt("dve", DRV[:, 24:32], PRM[:, P_L0:P_L0 + 8], PRM[:, P_L1:P_L1 + 8], ALU.subtract, ["PRM"], ["DRVt"])
        act(DRV[:, LB:LB + 8], DRV[:, 24:32], AF.Sigmoid, ["DRVt"], ["DRVlb"])
        ts("dve", DRV[:, OML:OML + 8], DRV[:, LB:LB + 8], -1.0, 1.0, ALU.mult, ALU.add, ["DRVlb"], ["DRVo"])
        ts("dve", DRV[:, NOML:NOML + 8], DRV[:, LB:LB + 8], 1.0, -1.0, ALU.mult, ALU.add, ["DRVo", "DRVlb"], ["DRVn"])
        ts("dve", DRV[:, HO:HO + 8], DRV[:, OML:OML + 8], 0.5, None, ALU.mult, None, ["DRVn", "DRVo"], ["DRVh"])
        ts("dve", DRV[:, NHO:NHO + 8], DRV[:, OML:OML + 8], -0.5, None, ALU.mult, None, ["DRVh"], ["DRVnh"])
        tt("dve", DRV[:, F0:F0 + 8], DRV[:, LB:LB + 8], DRV[:, HO:HO + 8], ALU.add, ["DRVnh", "DRVh", "DRVlb"], ["DRV"])
        dma(SM[:, :, :], s0[0].rearrange("h k v -> k h v"), [], ["SM%d" % h for h in range(NH)], "c3")
        for k in range(2):
            dma(SS[k][:, :, :], s0[1 + k].rearrange("h k v -> k h v"), [], ["SS%d.%d" % (k, h) for h in range(NH)], "c4%d" % k)
        dma(CTOK[0:4, :], cbuf.rearrange("s r d -> (s r) d"), [], ["XIN0"], "c5")
        for blk in range(8):
            b, q = next_sm()
            tr(ps[:, b, q * 128:q * 128 + 4], CTOK[0:4, blk * 128:(blk + 1) * 128], ident[0:4, 0:4],
               ["XIN0", "CSTp"], [smkey(b, q)])
            cp("dve", CB[:, blk, :, :].rearrange("p s r -> p (s r)"), ps[:, b, q * 128:q * 128 + 4], [smkey(b, q)], ["CB"])

        def slab2(ap_slab, T):
            return ap_slab[:, 0:T].rearrange("p (b t) -> p b t", b=2)

        def accview(i, TB):
            return ps[:, 2 * i:2 * i + 2, 0:TB]

        def load_x_and_transpose(src, rows, nt):
            t0, T = rows[0], nt * 128
            for blk in range(8):
                dma(XT[:, blk, 0:T], src[:, blk, t0:t0 + T], [], ["XT%d" % blk], "xl%d" % blk)

        def norm_stats_blk(blk, T):
            TB = T // 2
            i = blk % 2
            act(R(SQ[i][:, 0:T]), XT[:, blk, 0:T], AF.Square, ["XT%d" % blk], ["SQ%d" % i])
            for tb in range(2):
                mm(ps[:, 6 + tb, 0:TB], onesd, R(SQ[i][:, tb * TB:(tb + 1) * TB]), blk == 0, blk == 7,
                   ["SQ%d" % i, "CST"], [OBK[tb]])

        def norm_finish(T, gcol):
            TB = T // 2
            act(slab2(RSTD, T), ps[:, 6:8, 0:TB], AF.Ln, OBK + ["SMALL"], ["slA0"], bias=epsc)
            act(RSTD[:, 0:T], RSTD[:, 0:T], AF.Exp, ["slA0"], ["slA0"], scale=-0.5)
            for blk in range(8):
                stt(R(NT[:, blk, 0:T]), XT[:, blk, 0:T], PRM[:, gcol + blk:gcol + blk + 1], RSTD[:, 0:T],
                    ALU.mult, ALU.mult, ["XT%d" % blk, "slA0", "PRM"], ["NT%d" % blk])

        def norm_to_NT(T, gcol):
            for blk in range(8):
                norm_stats_blk(blk, T)
            norm_finish(T, gcol)

        NTK = ["NT%d" % b for b in range(8)]

        def kloop(accv_i, TB, lhs_fn, rhs_fn, nk, reads):
            fams = ("NT", "B1.", "B2.")
            for kb in range(nk):
                rk = [k for k in reads if not k.startswith(fams)]
                for fam in fams:
                    if any(k.startswith(fam) for k in reads):
                        rk.append("%s%d" % (fam, kb))
                for tb in range(2):
                    mm(ps[:, 2 * accv_i + tb, 0:TB], lhs_fn(kb), rhs_fn(kb, tb), kb == 0, kb == nk - 1,
                       rk, ["pb%d" % (2 * accv_i + tb)])

        def v_phase(T, nt, VT):
            widths = [384, 384, 256]
            c0 = 0
            n = 0
            for cu in range(3):
                w = widths[cu]
                view, wk = load_unit(wv[cu, :, :, 0:w], 8, w)
                for ti in range(nt):
                    bank = n % 4
                    n += 1
                    for blk in range(8):
                        mm(ps[:, bank, 0:w], R(NT[:, blk, ti * 128:(ti + 1) * 128]), R(view[:, blk, :]), blk == 0, blk == 7,
                           [wk, "NT%d" % blk], ["pb%d" % bank])
                    cp("act" if n % 2 == 0 else "dve", R(VT[:, ti, c0:c0 + w]), ps[:, bank, 0:w], ["pb%d" % bank], ["VT%d" % ti])
                    yield
                c0 += w

        def seq(*gens):
            for g in gens:
                yield from g

        def head_proj(h, T, state_only):
            TB = T // 2
            par = h % 2
            A, C = slabs["A%d" % par], slabs["C%d" % par]
            Ak, Ck = "slA%d" % par, "slC%d" % par
            view, wk = load_unit(wh[h, :, :, 0:256], 8, 256)
            cols = [("q", 0), ("f", 128)]
            for name, c0 in cols:
                i = next_acc()
                kloop(i, TB, lambda kb, c0=c0: R(view[:, kb, c0:c0 + 128]),
                      lambda kb, tb: R(NT[:, kb, tb * TB:(tb + 1) * TB]), 8, [wk] + NTK)
                dst, dk, fn = {"q": (A, Ak, AF.Silu), "f": (C, Ck, AF.Tanh)}[name]
                act(slab2(dst, T), accview(i, TB), fn, acckeys(i), [dk], scale=(0.5 if name == "f" else None))
                yield

        def head_chain(h, T, hf):
            par = h % 2
            TB = T // 2
            t0, t1 = hf * TB, (hf + 1) * TB
            sfx = "#%d" % hf
            A, C = slabs["A%d" % par][:, t0:t1], slabs["C%d" % par][:, t0:t1]
            Ak, Ck = "slA%d" % par + sfx, "slC%d" % par + sfx
            Dd, E, F = slabs["D"][:, t0:t1], slabs["E"][:, t0:t1], slabs["F"][:, t0:t1]
            Dk, Ek, Fk = "slD" + sfx, "slE" + sfx, "slF" + sfx
            QHd, QHk = QH[par][:, t0:t1], "QH%d" % par + sfx
            KTd, KTk = KT[:, t0:t1], "KT" + sfx
            nchh = TB // 64
            EBd, EBk = EB[:, par, hf * nchh:(hf + 1) * nchh], "EB%d" % par + sfx
            hoc, nhoc, f0c = DRV[:, HO + h:HO + h + 1], DRV[:, NHO + h:NHO + h + 1], DRV[:, F0 + h:F0 + h + 1]
            ts("dve", Dd, C, hoc, f0c, ALU.mult, ALU.add, [Ck, "DRV"], [Dk])
            act(E, C, AF.Identity, [Ck, "DRV"], [Ek], bias=hoc, scale=nhoc)
            yield
            act(Dd, Dd, AF.Ln, [Dk], [Dk])
            yield
            S.op("dve", lambda e: e.tensor_tensor_scan(out=F, data0=RM[:, t0:t1], data1=Dd,
                                                       initial=0.0, op0=ALU.mult, op1=ALU.add),
                 reads=["RM", Dk], writes=[Fk], cost=0.06 + 2 * TB / 960.0)
            yield
            F3 = F.rearrange("p (c t) -> p c t", t=64)
            act(C, F, AF.Exp, [Fk], [Ck])
            yield
            cp("dve", EBd, C[:, 63:TB:64], [Ck], [EBk])
            act(Dd, F, AF.Exp, [Fk], [Dk], scale=-1.0)
            yield
            stt(R(QHd), A, float(128 ** -0.5), C, ALU.mult, ALU.mult, [Ak, Ck], [QHk])
            yield
            tt("dve", R(KTd), E, Dd, ALU.mult, [Ek, Dk], [KTk])
            yield
            A3 = A.rearrange("p (c t) -> p c t", t=64)
            tt("dve", A3, F3[:, :, 63:64].to_broadcast([128, nchh, 64]), F3, ALU.subtract, [Fk], [Ak])
            yield
            act(A, A, AF.Exp, [Ak], [Ak])
            yield
            tt("pool", A, E, A, ALU.mult, [Ek, Ak], [Ak])
            yield

        def head_rec(h, T, nt, VT, nsamp):
            par = h % 2
            A, Ak = slabs["A%d" % par], "slA%d" % par
            QHh, QHk = QH[par], "QH%d" % par
            EBk = "EB%d" % par
            BS = slabs["BS"]
            TB = T // 2
            nch = 2 * nt
            npc = 2 * (nt - nsamp)
            for ti in range(nt):
                b, q = next_sm4()
                tr(ps[:, b, q * 128:(q + 1) * 128], A[:, ti * 128:(ti + 1) * 128], ident, [Ak, "CSTp"], [smkey(b, q)])
                cp("act", R(KHZ[0][0:64, ti, :]), ps[0:64, b, q * 128:(q + 1) * 128], [smkey(b, q)], ["KHZ0.%d" % ti])
                cp("dve", R(KHZ[1][64:128, ti, :]), ps[64:128, b, q * 128:(q + 1) * 128], [smkey(b, q)], ["KHZ1.%d" % ti])
            yield
            for ti in range(nt):
                b, q = next_sm4()
                mm(ps[:, b, q * 128:(q + 1) * 128], R(KT[:, ti * 128:(ti + 1) * 128]), R(QHh[:, ti * 128:(ti + 1) * 128]),
                   True, True, ["KT", QHk], [smkey(b, q)])
                tt("dve", R(SCM[:, ti, :]), ps[:, b, q * 128:(q + 1) * 128], mask, ALU.mult,
                   [smkey(b, q), "CSTp"], ["SCM%d" % ti])
                yield

            def state_io(c):
                if c >= npc:
                    k = c - npc
                    return (SS[k][:, h, :], "SS%d.%d" % (k, h)), (SS[k][:, h, :], "SS%d.%d" % (k, h))
                src = (SM[:, h, :], "SM%d" % h) if c == 0 else (ST[(c - 1) % 4][:, :], "ST%d" % ((c - 1) % 4))
                dst = (SM[:, h, :], "SM%d" % h) if c == npc - 1 else (ST[c % 4][:, :], "ST%d" % (c % 4))
                return src, dst

            def emit_ds(c):
                ti, p0 = c // 2, (c % 2) * 64
                b, q = next_sm()
                mm(ps[:, b, q * 128:(q + 1) * 128], R(KHZ[c % 2][:, ti, :]), R(VT[:, ti, h * 128:(h + 1) * 128]),
                   True, True, ["KHZ%d.%d" % (c % 2, ti), "VT%d" % ti], [smkey(b, q)])
                return b, q

            def emit_stt(c, b, q):
                (src, srck), (dst, dstk) = state_io(c)
                stt(dst, src, EB[:, par, c:c + 1], ps[:, b, q * 128:(q + 1) * 128], ALU.mult, ALU.add,
                    [srck, EBk, smkey(b, q)], [dstk])

            def emit_shadow(c):
                (src, srck), _ = state_io(c)
                cp("pool", R(SHD[c % 4][:, :]), src, [srck], ["SHD%d" % (c % 4)])

            def emit_o(c):
                ti, p0 = c // 2, (c % 2) * 64
                (src, srck), _ = state_io(c)
                tb, off = (c * 64) // TB, (c * 64) % TB
                mm(ps[:, 6 + tb, off:off + 64], R(VT[:, ti, h * 128:(h + 1) * 128]),
                   R(SCM[:, ti, p0:p0 + 64]), True, False, ["VT%d" % ti, "SCM%d" % ti], [OBK[tb]])
                shi = c % 4
                mm(ps[:, 6 + tb, off:off + 64], R(SHD[shi][:, :]), R(QHh[:, c * 64:(c + 1) * 64]), False, True,
                   ["SHD%d" % shi, QHk], [OBK[tb]])

            if npc >= 4:
                for step in range(npc + 2):
                    if step < npc:
                        b, q = emit_ds(step)
                        emit_shadow(step)
                        emit_stt(step, b, q)
                    if step >= 2:
                        emit_o(step - 2)
                    yield
            else:
                for c in range(npc):
                    b, q = emit_ds(c)
                    emit_shadow(c)
                    emit_o(c)
                    emit_stt(c, b, q)
                    yield
            for c in range(npc, nch):
                b, q = emit_ds(c)
                emit_shadow(c)
                emit_o(c)
                emit_stt(c, b, q)
                yield
            act(R(slab2(SQ[0], T)), ps[:, 6:8, 0:TB], AF.Square, OBK, ["SQ0"])
            yield
            i = next_acc()
            for tb in range(2):
                mm(ps[:, 2 * i + tb, 0:TB], onesv, R(SQ[0][:, tb * TB:(tb + 1) * TB]), True, True, ["SQ0", "CST"], ["pb%d" % (2 * i + tb)])
            act(R(slab2(SQ[1], T)), accview(i, TB), AF.Ln, acckeys(i) + ["SMALL"], ["SQ1"], bias=epsc)
            yield
            act(R(SQ[1][:, 0:T]), SQ[1][:, 0:T], AF.Exp, ["SQ1"], ["SQ1"], scale=-0.5)
            yield
            stt(R(slab2(B1[:, h, :], T)), ps[:, 6:8, 0:TB], PRM[:, P_HGN:P_HGN + 1], slab2(SQ[1], T), ALU.mult, ALU.mult,
                OBK + ["SQ1", "PRM"], ["B1.%d" % h])
            yield
            view, wk = load_unit(wh[h, :, :, 256:384], 8, 128)
            i = next_acc()
            kloop(i, TB, lambda kb: R(view[:, kb, 0:128]), lambda kb, tb: R(NT[:, kb, tb * TB:(tb + 1) * TB]), 8, [wk] + NTK)
            act(slab2(BS, T), accview(i, TB), AF.Silu, acckeys(i), ["slBS"])
            yield
            tt("pool", R(B1[:, h, 0:T]), B1[:, h, 0:T], BS[:, 0:T], ALU.mult, ["B1.%d" % h, "slBS"], ["B1.%d" % h])
            yield

        def pre_head(h, T, nt, VT, slabset):
            (C, Ck), (Dd, Dk), (E, Ek), (F, Fk) = slabset
            TB = T // 2
            hoc, nhoc, f0c = DRV[:, HO + h:HO + h + 1], DRV[:, NHO + h:NHO + h + 1], DRV[:, F0 + h:F0 + h + 1]
            view, wk = load_unit(wh[h, :, :, 128:256], 8, 128)
            i = next_acc()
            kloop(i, TB, lambda kb: R(view[:, kb, 0:128]), lambda kb, tb: R(NT[:, kb, tb * TB:(tb + 1) * TB]), 8, [wk] + NTK)
            act(C[:, 0:T].rearrange("p (b t) -> p b t", b=2), accview(i, TB), AF.Tanh, acckeys(i), [Ck], scale=0.5)
            yield
            ts("dve", Dd[:, 0:T], C[:, 0:T], hoc, f0c, ALU.mult, ALU.add, [Ck, "DRV"], [Dk])
            act(E[:, 0:T], C[:, 0:T], AF.Identity, [Ck, "DRV"], [Ek], bias=hoc, scale=nhoc)
            yield
            act(Dd[:, 0:T], Dd[:, 0:T], AF.Ln, [Dk], [Dk])
            yield
            S.op("dve", lambda e: e.tensor_tensor_scan(out=F[:, T - 1::-1], data0=SMALL[:, 1:2].to_broadcast([128, T]),
                                                       data1=Dd[:, T - 1::-1], initial=0.0, op0=ALU.mult, op1=ALU.add),
                 reads=["SMALL", Dk], writes=[Fk], cost=0.06 + 2 * T / 960.0)
            yield
            act(C[:, 0:T - 1], F[:, 1:T], AF.Exp, [Fk], [Ck])
            S.op("pool", lambda e: e.memset(C[:, T - 1:T], 1.0), reads=[Ck], writes=[Ck], cost=0.1)
            act(SMALL[:, 16 + h:17 + h], F[:, 0:1], AF.Exp, [Fk], ["EBP%d" % h])
            yield
            tt("pool", C[:, 0:T], E[:, 0:T], C[:, 0:T], ALU.mult, [Ek, Ck], [Ck])
            yield
            ob = 6 + h % 2
            pc0 = h * 128 if h < NH - 1 else (h - 1) * 128
            for ti in range(nt):
                b, q = next_sm()
                tr(ps[:, b, q * 128:(q + 1) * 128], C[:, ti * 128:(ti + 1) * 128], ident, [Ck, "CSTp"], [smkey(b, q)])
                cp("act" if ti % 2 == 0 else "dve", R(SCM[:, ti, :]), ps[:, b, q * 128:(q + 1) * 128], [smkey(b, q)], ["SCM%d" % ti])
            for ti in range(nt):
                mm(ps[:, ob, 0:128], R(SCM[:, ti, :]), R(VT[:, ti, h * 128:(h + 1) * 128]), ti == 0, ti == nt - 1,
                   ["SCM%d" % ti, "VT%d" % ti], ["pb%d" % ob])
            stt(SM[:, h, :], SM[:, h, :], SMALL[:, 16 + h:17 + h], ps[:, ob, 0:128], ALU.mult, ALU.add,
                ["SM%d" % h, "EBP%d" % h, "pb%d" % ob], ["SM%d" % h])
            yield

        def interleave(gens, weights=None):
            gens = list(gens)
            weights = [1] * len(gens) if weights is None else list(weights)
            live = list(range(len(gens)))
            while live:
                for gi in list(live):
                    for _ in range(weights[gi]):
                        try:
                            next(gens[gi])
                        except StopIteration:
                            live.remove(gi)
                            break

        def gated_branch(T, wz_dram, wm_dram, srcbuf_key, first):
            TB = T // 2
            MIX = B2[:, 0:8 * T].rearrange("p (c t) -> p c t", c=8)
            tmp = [slabs["A0"], slabs["A1"]]
            tmpk = ["slA0", "slA1"]
            sg = [slabs["C0"], slabs["C1"]]
            sgk = ["slC0", "slC1"]
            for cp_ in range(4):
                vz, kz = load_unit(wz_dram[cp_], 8, 256)
                vm, km = load_unit(wm_dram[cp_], 8, 256)
                for cq in range(2):
                    cb = cp_ * 2 + cq
                    i = next_acc()
                    kloop(i, TB, lambda kb, cq=cq, vz=vz: R(vz[:, kb, cq * 128:(cq + 1) * 128]),
                          lambda kb, tb: R(NT[:, kb, tb * TB:(tb + 1) * TB]), 8, [kz] + NTK)
                    act(slab2(sg[cq], T), accview(i, TB), AF.Sigmoid, acckeys(i), [sgk[cq]])
                    i = next_acc()
                    kloop(i, TB, lambda kb, cq=cq, vm=vm: R(vm[:, kb, cq * 128:(cq + 1) * 128]),
                          lambda kb, tb: R(B1[:, kb, tb * TB:(tb + 1) * TB]), 8, [km] + ["B1.%d" % b for b in range(8)])
                    mixv = MIX[:, cb, :].rearrange("p (b t) -> p b t", b=2)
                    if first:
                        tt("dve", R(mixv), slab2(sg[cq], T), accview(i, TB), ALU.mult, [sgk[cq]] + acckeys(i), ["B2.%d" % cb])
                    else:
                        tt("dve", slab2(tmp[cq], T), slab2(sg[cq], T), accview(i, TB), ALU.mult, [sgk[cq]] + acckeys(i), [tmpk[cq]])
                        tt("pool", R(MIX[:, cb, :]), MIX[:, cb, :], tmp[cq][:, 0:T], ALU.add, ["B2.%d" % cb, tmpk[cq]], ["B2.%d" % cb])
            return MIX

        def conv_phase(T, Tp, nsamp, last):
            TB = T // 2
            CT, UE, Y = slabs["E"], slabs["F"], slabs["D"]
            for j in range(8):
                view, wk = load_unit(wc[j], 8, 384)
                w0 = PRM[:, P_CW + 0 * 8 + j:P_CW + 0 * 8 + j + 1]
                w1 = PRM[:, P_CW + 1 * 8 + j:P_CW + 1 * 8 + j + 1]
                w2 = PRM[:, P_CW + 2 * 8 + j:P_CW + 2 * 8 + j + 1]
                i = next_acc()
                kloop(i, TB, lambda kb: R(view[:, kb, 128:256]), lambda kb, tb: R(NT[:, kb, tb * TB:(tb + 1) * TB]), 8, [wk] + NTK)
                cp("act", slab2(CT, T), accview(i, TB), acckeys(i), ["slE"])
                i = next_acc()
                kloop(i, TB, lambda kb: R(view[:, kb, 256:384]), lambda kb, tb: R(NT[:, kb, tb * TB:(tb + 1) * TB]), 8, [wk] + NTK)
                tt("dve", UE[:, 2:2 + T].rearrange("p (b t) -> p b t", b=2), slab2(CT, T), accview(i, TB), ALU.mult,
                   ["slE"] + acckeys(i), ["slF"])
                cp("pool", UE[:, 0:2], UH[:, j, :], ["UH%d" % j], ["slF"])
                act(Y[:, 0:T], UE[:, 2:2 + T], AF.Copy, ["slF", "PRM"], ["slD"], scale=w2)
                stt(Y[:, 0:T], UE[:, 1:1 + T], w1, Y[:, 0:T], ALU.mult, ALU.add, ["slF", "slD", "PRM"], ["slD"])
                stt(Y[:, 0:T], UE[:, 0:T], w0, Y[:, 0:T], ALU.mult, ALU.add, ["slF", "slD", "PRM"], ["slD"])
                if nsamp:
                    cp("pool", UES[:, :, 2:66], UE[:, 2 + Tp:2 + Tp + 128].rearrange("p (s t) -> p s t", s=2), ["slF"], ["UES"])
                    cp("pool", UES[:, :, 0:2], CB[:, j, :, :], ["CB", "UES"], ["UES"])
                    act(YS[:, :, :], UES[:, :, 2:66], AF.Copy, ["UES", "PRM"], ["YS"], scale=w2)
                    stt(YS[:, :, :], UES[:, :, 1:65], w1, YS[:, :, :], ALU.mult, ALU.add, ["UES", "YS", "PRM"], ["YS"])
                    stt(Y[:, Tp:Tp + 128].rearrange("p (s t) -> p s t", s=2), UES[:, :, 0:64], w0, YS[:, :, :], ALU.mult, ALU.add,
                        ["UES", "YS", "PRM", "slD"], ["slD"])
                    cp("pool", CBO[:, j, 1:3, :], UES[:, :, 64:66], ["UES"], ["CBO"])
                cp("pool", UH[:, j, :], UE[:, Tp:Tp + 2], ["slF"], ["UH%d" % j])
                if last:
                    cp("pool", CBO[:, j, 0, :], UE[:, Tp:Tp + 2], ["slF", "CBO"], ["CBO"])
                i = next_acc()
                kloop(i, TB, lambda kb: R(view[:, kb, 0:128]), lambda kb, tb: R(NT[:, kb, tb * TB:(tb + 1) * TB]), 8, [wk] + NTK)
                tt("dve", R(B1[:, j, 0:T].rearrange("p (b t) -> p b t", b=2)), slab2(Y, T), accview(i, TB), ALU.mult,
                   ["slD"] + acckeys(i), ["B1.%d" % j])

        def resid_matmul(T, w_dram_units, nk, src_fn, src_keys, after_cb=None):
            TB = T // 2
            for cp_ in range(4):
                view, wk = load_unit(w_dram_units(cp_), nk, 256)
                for cq in range(2):
                    cb = cp_ * 2 + cq
                    i = next_acc()
                    kloop(i, TB, lambda kb, cq=cq, view=view: R(view[:, kb, cq * 128:(cq + 1) * 128]),
                          lambda kb, tb: src_fn(kb, tb), nk, [wk] + src_keys)
                    xv = XT[:, cb, 0:T].rearrange("p (b t) -> p b t", b=2)
                    tt("dve", xv, xv, accview(i, TB), ALU.add, ["XT%d" % cb] + acckeys(i), ["XT%d" % cb])
                    if after_cb is not None and cb >= 1:
                        after_cb(cb - 1)
            if after_cb is not None:
                after_cb(7)

        def ffn_phase(T):
            TB = T // 2
            GS = [slabs["A0"], slabs["A1"]]
            GSk = ["slA0", "slA1"]
            for g, (j0, nj) in enumerate(FFG):
                for jj in range(nj):
                    j = j0 + jj
                    view, wk = load_unit(wgu[j], 8, 256)
                    i = next_acc()
                    kloop(i, TB, lambda kb: R(view[:, kb, 0:128]), lambda kb, tb: R(NT[:, kb, tb * TB:(tb + 1) * TB]), 8, [wk] + NTK)
                    act(slab2(GS[jj % 2], T), accview(i, TB), AF.Silu, acckeys(i), [GSk[jj % 2]])
                    i = next_acc()
                    kloop(i, TB, lambda kb: R(view[:, kb, 128:256]), lambda kb, tb: R(NT[:, kb, tb * TB:(tb + 1) * TB]), 8, [wk] + NTK)
                    tt("dve", R(B1[:, jj, 0:T].rearrange("p (b t) -> p b t", b=2)), slab2(GS[jj % 2], T), accview(i, TB), ALU.mult,
                       [GSk[jj % 2]] + acckeys(i), ["B1.%d" % jj])
                resid_matmul(T, lambda cp_, g=g, nj=nj: wd[g * 4 + cp_, :, 0:nj, :], nj,
                             lambda kb, tb: R(B1[:, kb, tb * TB:(tb + 1) * TB]), ["B1.%d" % b for b in range(nj)],
                             after_cb=(lambda cb: norm_stats_blk(cb, T)) if g == len(FFG) - 1 else None)

        def ple_phase(T, nt, rows):
            TB = T // 2
            PT = B2[:, 0:2 * T].rearrange("p (c t) -> p c t", c=2)
            t0 = rows[0]
            for kb in range(2):
                dma(R(PT[:, kb, 0:T]), R(pmain[:, kb, t0:t0 + T]), [], ["B2.%d" % kb], "pl%d" % kb)
            sg = [slabs["C0"], slabs["C1"]]
            sgk = ["slC0", "slC1"]
            tmp = [slabs["A0"], slabs["A1"]]
            tmpk = ["slA0", "slA1"]
            for cp_ in range(4):
                vz, kz = load_unit(wpg[cp_], 8, 256)
                vm, km = load_unit(wpl[cp_], 2, 256)
                for cq in range(2):
                    cb = cp_ * 2 + cq
                    i = next_acc()
                    kloop(i, TB, lambda kb, cq=cq, vz=vz: R(vz[:, kb, cq * 128:(cq + 1) * 128]),
                          lambda kb, tb: R(NT[:, kb, tb * TB:(tb + 1) * TB]), 8, [kz] + NTK)
                    act(slab2(sg[cq], T), accview(i, TB), AF.Sigmoid, acckeys(i), [sgk[cq]])
                    i = next_acc()
                    kloop(i, TB, lambda kb, cq=cq, vm=vm: R(vm[:, kb, cq * 128:(cq + 1) * 128]),
                          lambda kb, tb: R(PT[:, kb, tb * TB:(tb + 1) * TB]), 2, [km, "B2.0", "B2.1"])
                    tt("dve", slab2(tmp[cq], T), slab2(sg[cq], T), accview(i, TB), ALU.mult, [sgk[cq]] + acckeys(i), [tmpk[cq]])
                    tt("pool", XT[:, cb, 0:T], XT[:, cb, 0:T], tmp[cq][:, 0:T], ALU.add, ["XT%d" % cb, tmpk[cq]], ["XT%d" % cb])
                    if cb >= 1:
                        norm_stats_blk(cb - 1, T)
            norm_stats_blk(7, T)

        def final_out(T, nt, rows, next_load=None):
            TB = T // 2
            act(slab2(RSTD, T), ps[:, 6:8, 0:TB], AF.Ln, OBK + ["SMALL"], ["slA0"], bias=epsc)
            act(RSTD[:, 0:T], RSTD[:, 0:T], AF.Exp, ["slA0"], ["slA0"], scale=-0.5)
            t0 = rows[0]
            ybuf = [(slabs[n], "sl" + n) for n in ("A1", "BS", "C0", "C1", "D", "E", "F")] + [(XT[:, 7, :], "XT7")]
            for blk in range(8):
                yb, yk = ybuf[blk]
                stt(yb[:, 0:T], XT[:, blk, 0:T], PRM[:, P_GFIN + blk:P_GFIN + blk + 1], RSTD[:, 0:T],
                    ALU.mult, ALU.mult, ["XT%d" % blk, "slA0", "PRM"], [yk])
                dma(yout[:, blk, t0:t0 + T], yb[:, 0:T], [yk], ["yout%d" % blk], "yo%d" % blk)
                if next_load is not None and blk < 7:
                    next_load(blk)
            if next_load is not None:
                next_load(7)

        def run_pass(tiles, main, first_main, last_main, last_pre, preloaded=False, next_tiles=None):
            nt = len(tiles)
            T = nt * 128
            state_only = not main
            src = xmain if main else xpre
            rows = [g * 128 for g in tiles]
            nsamp = 1 if (main and samp_tile in tiles) else 0
            Tp = T - 128 * nsamp

            def chunk_state(c, h):
                if nsamp and c >= 2 * (nt - 1):
                    k = c - 2 * (nt - 1)
                    return SS[k][:, h, :], "SS%d.%d" % (k, h)
                return SM[:, h, :], "SM%d" % h

            fence(["B2.%d" % b for b in range(8)], ["VT%d" % t for t in range(6)])
            if not preloaded:
                load_x_and_transpose(src, rows, nt)
            norm_to_NT(T, P_GMIX)
            VT = B2[:, 0:nt * D].rearrange("p (t c) -> p t c", t=nt)
            if state_only:
                sets = [[(slabs[n], "sl" + n) for n in ("A0", "A1", "BS", "F")],
                        [(slabs[n], "sl" + n) for n in ("C0", "C1", "D", "E")],
                        [(XT[:, b, :], "XT%d" % b) for b in range(0, 4)],
                        [(XT[:, b, :], "XT%d" % b) for b in range(4, 8)]]
                interleave([v_phase(T, nt, VT)] + [pre_head(i, T, nt, VT, sets[i]) for i in range(4)], [4, 1, 1, 1, 1])
                interleave([pre_head(4 + i, T, nt, VT, sets[i]) for i in range(4)])
            else:
                interleave([head_proj(0, T, state_only)])
                interleave([v_phase(T, nt, VT), head_chain(0, T, 0), head_chain(0, T, 1), head_proj(1, T, state_only)],
                           [3, 1, 1, 1])
            for h in (range(NH) if not state_only else ()):
                gens, wts = [head_rec(h, T, nt, VT, nsamp)], [REC_W]
                if h + 1 < NH:
                    gens += [head_chain(h + 1, T, 0), head_chain(h + 1, T, 1)]
                    wts += [2, 2]
                if h + 2 < NH:
                    gens.append(head_proj(h + 2, T, state_only))
                    wts.append(1)
                interleave(gens, wts)
            if state_only:
                if last_pre:
                    for j in range(8):
                        view, wk = load_unit(wc[j], 8, 384)
                        b, q = next_sm()
                        for kb in range(8):
                            mm(ps[:, b, q * 128:q * 128 + 2], R(view[:, kb, 128:256]), R(NT[:, kb, T - 2:T]), kb == 0, kb == 7,
                               [wk] + NTK, [smkey(b, q)])
                        cp("act", SMALL[:, 8:10], ps[:, b, q * 128:q * 128 + 2], [smkey(b, q)], ["SMu"])
                        b2, q2 = next_sm()
                        for kb in range(8):
                            mm(ps[:, b2, q2 * 128:q2 * 128 + 2], R(view[:, kb, 256:384]), R(NT[:, kb, T - 2:T]), kb == 0, kb == 7,
                               [wk] + NTK, [smkey(b2, q2)])
                        tt("dve", UH[:, j, :], SMALL[:, 8:10], ps[:, b2, q2 * 128:q2 * 128 + 2], ALU.mult,
                           ["SMu", smkey(b2, q2)], ["UH%d" % j])
                return
            fence(["VT%d" % t for t in range(6)], ["B2.%d" % b for b in range(8)])
            gated_branch(T, wza, wa, None, True)
            conv_phase(T, Tp, nsamp, last_main)
            MIX = gated_branch(T, wzb, wb, None, False)
            TB = T // 2
            resid_matmul(T, lambda cp_: wo[cp_], 8, lambda kb, tb: R(MIX[:, kb, tb * TB:(tb + 1) * TB]),
                         ["B2.%d" % b for b in range(8)], after_cb=lambda cb: norm_stats_blk(cb, T))
            norm_finish(T, P_GFFN)
            ffn_phase(T)
            norm_finish(T, P_GPLE)
            ple_phase(T, nt, rows)
            if next_tiles is not None:
                nT_, nt0 = len(next_tiles) * 128, next_tiles[0] * 128
                final_out(T, nt, rows, lambda blk: dma(XT[:, blk, 0:nT_], xmain[:, blk, nt0:nt0 + nT_], [], ["XT%d" % blk], "xl%d" % blk))
            else:
                final_out(T, nt, rows)

        for pi, tiles in enumerate(pre_passes):
            run_pass(tiles, False, False, False, pi == len(pre_passes) - 1)
        for pi, tiles in enumerate(main_passes):
            run_pass(tiles, True, pi == 0, pi == len(main_passes) - 1, False, preloaded=(pi > 0),
                     next_tiles=(main_passes[pi + 1] if pi + 1 < len(main_passes) else None))

        dma(sfin[0].rearrange("h k v -> k h v"), SM[:, :, :], ["SM%d" % h for h in range(NH)], ["sfin0"], "o0")
        for k in range(2):
            dma(sfin[1 + k].rearrange("h k v -> k h v"), SS[k][:, :, :], ["SS%d.%d" % (k, h) for h in range(NH)], ["sfin%d" % (1 + k)], "o%d" % (1 + k))
        for blk in range(8):
            b, q = next_sm()
            tr(ps[0:6, b, q * 128:(q + 1) * 128], CBO[:, blk, :, :].rearrange("p s r -> p (s r)"), ident, ["CBO", "CSTp"], [smkey(b, q)])
            cp("dve", COUT[:, blk * 128:(blk + 1) * 128], ps[0:6, b, q * 128:(q + 1) * 128], [smkey(b, q)], ["XIN0"])
        dma(cfin.rearrange("s r d -> (s r) d"), COUT[:, :], ["XIN0"], ["cfin"], "o3")

        cnt = S.resolve()
        sems = {}
        for key in cnt:
            sems[key] = es.enter_context(nc.semaphore("s_%s_%s" % key))
        block = es.enter_context(nc.Block())
        S.emit(nc, block, sems)
    return nc


def _tile_cols(W, col_lists, kb, width):
    out = np.zeros((len(col_lists), 128, kb, width), np.float32)
    Wr = W.reshape(kb, 128, W.shape[1])
    for u, cols in enumerate(col_lists):
        out[u, :, :, :len(cols)] = np.transpose(Wr[:, :, cols], (1, 0, 2))
    return out


def _fm(x):
    t, d = x.shape
    return np.ascontiguousarray(np.transpose(x.reshape(t, d // 128, 128), (2, 1, 0)))


def _pcol(v):
    return np.ascontiguousarray(v.reshape(-1, 128).T)


_NC_CACHE = {}


def _prep_shared(lower_bounds, norm_mix, w_in, conv_w, hg_norm, w_branch_a, w_branch_b, w_out, norm_ffn,
                 w_gate_up, w_down, norm_ple, w_ple, w_ple_gate, norm_final):
    f = lambda a: np.ascontiguousarray(np.asarray(a, dtype=np.float32))
    win = f(w_in)[0]
    ar = np.arange
    HF, HV, CD = 1024, 1024, 1024
    oq, of_, oi, og, oB, oC, oh, oza, ozb = 0, 1024, 2048, 3072, 4096, 5120, 6144, 7168, 8192
    wv_ = _tile_cols(win, [list(oi + ar(0, 384)), list(oi + ar(384, 768)), list(oi + ar(768, 1024))], 8, 384)
    wh_ = _tile_cols(win, [list(oq + h * 128 + ar(128)) + list(of_ + h * 128 + ar(128)) + list(og + h * 128 + ar(128))
                           for h in range(8)], 8, 384)
    wc_ = _tile_cols(win, [list(oB + j * 128 + ar(128)) + list(oC + j * 128 + ar(128)) + list(oh + j * 128 + ar(128))
                           for j in range(8)], 8, 384)
    c256 = [list(c * 256 + ar(256)) for c in range(4)]
    wza_ = _tile_cols(win[:, oza:oza + 1024], c256, 8, 256)
    wzb_ = _tile_cols(win[:, ozb:ozb + 1024], c256, 8, 256)
    wa_ = _tile_cols(f(w_branch_a)[0], c256, 8, 256)
    wb_ = _tile_cols(f(w_branch_b)[0], c256, 8, 256)
    wo_ = _tile_cols(f(w_out)[0], c256, 8, 256)
    wpg_ = _tile_cols(f(w_ple_gate)[0], c256, 8, 256)
    wgu_full = f(w_gate_up)[0]
    wgu_ = _tile_cols(wgu_full, [list(j * 128 + ar(128)) + list(DFF + j * 128 + ar(128)) for j in range(22)], 8, 256)
    wdn = f(w_down)[0]
    wd_ = np.zeros((12, 128, 8, 256), np.float32)
    for g, (j0, nj) in enumerate(FFG):
        sub = wdn[j0 * 128:(j0 + nj) * 128]
        wd_[g * 4:(g + 1) * 4, :, :nj, :] = _tile_cols(sub, c256, nj, 256)
    wpl_ = _tile_cols(f(w_ple)[0], c256, 2, 256)

    prm = np.zeros((128, 96), np.float32)
    lbs = f(lower_bounds)
    prm[:, 0:8] = _pcol(lbs[0])
    prm[:, 8:16] = _pcol(lbs[1])
    prm[:, 16:24] = _pcol(f(norm_mix)[0])
    prm[:, 24:32] = _pcol(f(norm_ffn)[0])
    prm[:, 32:40] = _pcol(f(norm_ple)[0])
    prm[:, 40] = f(hg_norm)[0]
    cw = f(conv_w)[0]
    for tap in range(3):
        prm[:, 41 + tap * 8:41 + tap * 8 + 8] = _pcol(cw[tap])
    cst = np.zeros((128, 4, 128), np.float32)
    cst[:, 0, :] = np.eye(128, dtype=np.float32)
    s_i, t_i = np.meshgrid(ar(128), ar(128), indexing="ij")
    cst[:, 1, :] = ((s_i // 64 == t_i // 64) & (s_i <= t_i)).astype(np.float32)
    cst[:, 2, :] = 1.0 / D
    cst[:, 3, :] = 1.0 / 128
    prm[:, 72:80] = _pcol(f(norm_final))

    shared = dict(prm=prm, cst=cst, wv=wv_, wh=wh_, wza=wza_, wa=wa_, wzb=wzb_, wb=wb_, wo=wo_,
                  wpg=wpg_, wc=wc_, wgu=wgu_, wd=wd_, wpl=wpl_)
    return shared


def kernel(x_prompt, x_sample, p_prompt, p_sample, state_hgrn, state_conv, lower_bounds,
           norm_mix, w_in, conv_w, hg_norm, w_branch_a, w_branch_b, w_out, norm_ffn,
           w_gate_up, w_down, norm_ple, w_ple, w_ple_gate, norm_final):
    f = lambda a: np.ascontiguousarray(np.asarray(a, dtype=np.float32))
    x_prompt, x_sample, p_prompt, p_sample = f(x_prompt), f(x_sample), f(p_prompt), f(p_sample)
    state_hgrn, state_conv = f(state_hgrn), f(state_conv)
    shared = _prep_shared(lower_bounds, norm_mix, w_in, conv_w, hg_norm, w_branch_a, w_branch_b, w_out, norm_ffn,
                          w_gate_up, w_down, norm_ple, w_ple, w_ple_gate, norm_final)
    in_maps = []
    for c in range(NCORES):
        b, half = c // 2, c % 2
        xm = _fm(np.concatenate([x_prompt[b, half * 2048:(half + 1) * 2048], x_sample[2 * c], x_sample[2 * c + 1]], axis=0))
        pm = _fm(np.concatenate([p_prompt[0, b, half * 2048:(half + 1) * 2048], p_sample[0, 2 * c], p_sample[0, 2 * c + 1]], axis=0))
        xp = _fm(x_prompt[b, 0:2048]) if half == 1 else np.zeros((128, 8, TPRE), np.float32)
        s0 = np.zeros((3, NH, 128, 128), np.float32)
        s0[1] = state_hgrn[0, 2 * c]
        s0[2] = state_hgrn[0, 2 * c + 1]
        cb = np.stack([state_conv[0, 2 * c], state_conv[0, 2 * c + 1]], axis=0)
        m = dict(shared)
        m.update(xpre=np.ascontiguousarray(xp), xmain=np.ascontiguousarray(xm), pmain=np.ascontiguousarray(pm),
                 s0=s0, cbuf=np.ascontiguousarray(cb))
        in_maps.append(m)

    if "nc" not in _NC_CACHE:
        _NC_CACHE["nc"] = build_program()
    nc = _NC_CACHE["nc"]
    res = run_bass_kernel_spmd(nc, in_maps, core_ids=list(range(NCORES)))
    R_ = res.results
    y_prompt = np.zeros((4, 4096, D), np.float32)
    y_sample = np.zeros((16, 64, D), np.float32)
    hg_p = np.zeros((1, 4, NH, 128, 128), np.float32)
    cv_p = np.zeros((1, 4, 2, D), np.float32)
    hg_s = np.zeros((1, 16, NH, 128, 128), np.float32)
    cv_s = np.zeros((1, 16, 2, D), np.float32)
    for c in range(NCORES):
        b, half = c // 2, c % 2
        y = np.ascontiguousarray(np.transpose(R_[c]["y"], (2, 1, 0))).reshape(TMAIN, D)
        y_prompt[b, half * 2048:(half + 1) * 2048] = y[0:2048]
        y_sample[2 * c] = y[2048:2112]
        y_sample[2 * c + 1] = y[2112:2176]
        sf, cf = R_[c]["sfin"], R_[c]["cfin"]
        hg_s[0, 2 * c], hg_s[0, 2 * c + 1] = sf[1], sf[2]
        cv_s[0, 2 * c], cv_s[0, 2 * c + 1] = cf[1], cf[2]
        if half == 1:
            hg_p[0, b] = sf[0]
            cv_p[0, b] = cf[0]
    return (y_prompt, y_sample, hg_p, cv_p, hg_s, cv_s)
```

```python
import numpy as np
import concourse.bass as bass
import concourse.mybir as mybir
from concourse.bass_utils import run_bass_kernel_spmd

F32 = mybir.dt.float32
F32R = mybir.dt.float32r
AF = mybir.ActivationFunctionType
ALU = mybir.AluOpType

D = 1024
NH = 8
DFF = 2816
PLE = 256
EPS = 1e-6
NCORES = 8
TPRE = 2048
TMAIN = 2176
MAIN_PASSES = [list(range(0, 6)), list(range(6, 12)), list(range(12, 17))]
PRE_PASSES = [list(range(0, 6)), list(range(6, 12)), list(range(12, 16))]
TMAX = 768
SLOTW = 3072
NSLOT = 3
FFG = [(0, 8), (8, 8), (16, 6)]
REC_W = 4
LIST_SCHED = True
PRIO_BLEVEL = False
TABLE_AWARE = True


SPLIT_KEYS = {"slA0", "slA1", "slC0", "slC1", "slD", "slE", "slF", "QH0", "QH1", "KT", "EB0", "EB1"}


class Op:
    __slots__ = ("eng", "fn", "reads", "writes", "dma", "deps", "signal", "sigval", "idx", "cost", "odeps", "tset")

    def __init__(self, eng, fn, reads, writes, dma, cost=0.3):
        self.eng, self.fn, self.reads, self.writes, self.dma = eng, fn, tuple(reads), tuple(writes), dma
        self.cost = cost
        self.tset = -1
        self.odeps = ()
        self.deps = ()
        self.signal = False
        self.sigval = 0


class Sched:
    ENGS = ("pe", "act", "dve", "pool", "sp")

    def __init__(self):
        self.ops = []

    def op(self, eng, fn, reads=(), writes=(), dma=None, cost=0.3):
        def _exp(keys):
            out = []
            for k in keys:
                if k in SPLIT_KEYS:
                    out += [k + "#0", k + "#1"]
                else:
                    out.append(k)
            return out
        reads, writes = _exp(reads), _exp(writes)
        writes = list(writes) + [k for k in reads if k.startswith("pb") and k not in writes]
        o = Op(eng, fn, reads, writes, dma, cost)
        o.idx = len(self.ops)
        self.ops.append(o)
        return o

    def resolve(self):
        last_w = {}
        readers = {}
        for o in self.ops:
            deps = {}
            rset = set(o.reads)
            for k in o.reads:
                d = last_w.get(k)
                if d is not None:
                    deps[d.idx] = True
            for k in o.writes:
                d = last_w.get(k)
                if d is not None:
                    deps.setdefault(d.idx, False)
                for r in readers.get(k, ()):
                    deps.setdefault(r.idx, False)
            need = []
            o.odeps = [self.ops[di] for di in deps if di != o.idx]
            for di, raw in deps.items():
                d = self.ops[di]
                if d is o:
                    continue
                if d.dma is None and d.eng == o.eng:
                    if o.eng == "pe":
                        continue
                need.append(d)
                d.signal = True
            o.deps = need
            for k in o.reads:
                readers.setdefault(k, []).append(o)
            for k in o.writes:
                last_w[k] = o
                readers[k] = []
        self.schedule()
        cnt = {}
        for o in self.issue_order:
            if o.signal or o.dma is not None:
                key = ("dma", o.dma) if o.dma is not None else ("eng", o.eng)
                step = 16 if o.dma is not None else 1
                cnt[key] = cnt.get(key, 0) + step
                o.sigval = cnt[key]
        self.final = dict(cnt)
        return cnt

    def schedule(self):
        import heapq
        ops = self.ops
        if not LIST_SCHED:
            self.issue_order = list(ops)
            self.order = {e: [o for o in ops if o.eng == e] for e in self.ENGS}
            return
        nleft = [len(o.odeps) for o in ops]
        users = [[] for _ in ops]
        for o in ops:
            for d in o.odeps:
                users[d.idx].append(o)
        finish = [0.0] * len(ops)
        ready_t = [0.0] * len(ops)
        blevel = [0.0] * len(ops)
        for o in reversed(ops):
            m = 0.0
            for u in users[o.idx]:
                if blevel[u.idx] > m:
                    m = blevel[u.idx]
            blevel[o.idx] = o.cost + m
        pend = {e: [] for e in self.ENGS}
        avail = {e: [] for e in self.ENGS}
        free = {e: 0.0 for e in self.ENGS}
        for o in ops:
            if nleft[o.idx] == 0:
                heapq.heappush(pend[o.eng], (0.0, o.idx))
        order = {e: [] for e in self.ENGS}
        issue = []
        HOP = 0.7
        cur_tset = [-1]
        done = 0
        while done < len(ops):
            best = None
            for e in self.ENGS:
                while pend[e] and pend[e][0][0] <= free[e]:
                    pi_ = heapq.heappop(pend[e])[1]
                    heapq.heappush(avail[e], ((-blevel[pi_], pi_) if PRIO_BLEVEL else (pi_, pi_)))
                if avail[e]:
                    cand = (free[e], avail[e][0][1], e, True)
                elif pend[e]:
                    cand = (pend[e][0][0], pend[e][0][1], e, False)
                else:
                    continue
                if best is None or cand[:2] < best[:2]:
                    best = cand
            start, idx, e, from_avail = best
            if from_avail:
                if e == "act" and TABLE_AWARE and len(avail[e]) > 1:
                    cands = heapq.nsmallest(6, avail[e])
                    pick = None
                    for c_ in cands:
                        ts_ = ops[c_[1]].tset
                        if ts_ == -1 or ts_ == cur_tset[0]:
                            pick = c_
                            break
                    if pick is not None and pick is not cands[0]:
                        avail[e].remove(pick)
                        heapq.heapify(avail[e])
                        idx = pick[1]
                    else:
                        heapq.heappop(avail[e])
                else:
                    heapq.heappop(avail[e])
            else:
                heapq.heappop(pend[e])
            o = ops[idx]
            if e == "act" and o.tset != -1:
                if o.tset != cur_tset[0]:
                    start += 1.28
                cur_tset[0] = o.tset
            issue_cost = 0.07 if o.dma is not None else o.cost
            free[e] = start + issue_cost
            finish[idx] = start + o.cost
            order[e].append(o)
            issue.append(o)
            done += 1
            for u in users[idx]:
                lat = 0.0 if (u.eng == e and o.dma is None) else HOP
                ready_t[u.idx] = max(ready_t[u.idx], finish[idx] + lat)
                nleft[u.idx] -= 1
                if nleft[u.idx] == 0:
                    heapq.heappush(pend[u.eng], (ready_t[u.idx], u.idx))
        self.order = order
        self.issue_order = issue

    def emit(self, nc, block, sems):
        engmap = {"pe": "tensor", "act": "scalar", "dve": "vector", "pool": "gpsimd", "sp": "sync"}
        sched = self

        def run(engname, e):
            waited = {}
            for o in sched.order[engname]:
                want = {}
                for d in o.deps:
                    key = ("dma", d.dma) if d.dma is not None else ("eng", d.eng)
                    if d.sigval > want.get(key, 0):
                        want[key] = d.sigval
                for key, val in want.items():
                    if val > waited.get(key, 0):
                        e.wait_ge(sems[key], val)
                        waited[key] = val
                if o.fn is None:
                    continue
                ins = o.fn(e)
                if o.dma is not None:
                    ins.then_inc(sems[("dma", o.dma)], 16)
                elif o.signal:
                    ins.then_inc(sems[("eng", o.eng)], 1)
            if engname == "sp":
                for key, val in sched.final.items():
                    if key[0] == "dma" and val > waited.get(key, 0):
                        e.wait_ge(sems[key], val)

        for engname in self.ENGS:
            getattr(block, engmap[engname])(lambda e, _n=engname: run(_n, e))


def build_program(pre_passes=None, main_passes=None, samp_tile=16):
    pre_passes = PRE_PASSES if pre_passes is None else pre_passes
    main_passes = MAIN_PASSES if main_passes is None else main_passes
    nc = bass.Bass("TRN2", target_bir_lowering=False)
    nc.dge_precook = False
    S = Sched()

    def din(name, shape):
        return nc.dram_tensor(name, list(shape), F32, kind="ExternalInput").ap()

    def dout(name, shape):
        return nc.dram_tensor(name, list(shape), F32, kind="ExternalOutput").ap()

    xpre = din("xpre", [128, 8, TPRE])
    xmain = din("xmain", [128, 8, TMAIN])
    pmain = din("pmain", [128, 2, TMAIN])
    s0 = din("s0", [3, NH, 128, 128])
    cbuf = din("cbuf", [2, 2, D])
    prm = din("prm", [128, 96])
    cst = din("cst", [128, 4, 128])
    wv = din("wv", [3, 128, 8, 384])
    wh = din("wh", [8, 128, 8, 384])
    wza = din("wza", [4, 128, 8, 256])
    wa = din("wa", [4, 128, 8, 256])
    wzb = din("wzb", [4, 128, 8, 256])
    wb = din("wb", [4, 128, 8, 256])
    wo = din("wo", [4, 128, 8, 256])
    wpg = din("wpg", [4, 128, 8, 256])
    wc = din("wc", [8, 128, 8, 384])
    wgu = din("wgu", [22, 128, 8, 256])
    wd = din("wd", [12, 128, 8, 256])
    wpl = din("wpl", [4, 128, 2, 256])
    yout = dout("y", [128, 8, TMAIN])
    sfin = dout("sfin", [3, NH, 128, 128])
    cfin = dout("cfin", [3, 2, D])

    import contextlib
    es = contextlib.ExitStack()

    def sb(name, shape):
        return es.enter_context(nc.sbuf_tensor(name, list(shape), F32))

    with es:
        XT = sb("XT", [128, 8, TMAX])
        NT = sb("NT", [128, 8, TMAX])
        B1 = sb("B1", [128, 8, TMAX])
        B2 = sb("B2", [128, 8 * TMAX])
        SLW = TMAX + 8
        slabs = {n: sb("sl" + n, [128, SLW]) for n in ("A0", "A1", "BS", "C0", "C1", "D", "E", "F")}
        QH = [sb("QH%d" % i, [128, TMAX]) for i in range(2)]
        KT = sb("KT", [128, TMAX])
        SQ = [sb("SQ%d" % i, [128, TMAX]) for i in range(2)]
        KHZ = [sb("KHZ%d" % i, [128, 6, 128]) for i in range(2)]
        SCM = sb("SCM", [128, 6, 128])
        RSTD = slabs["A0"]
        RM = sb("RM", [128, TMAX])
        WS = [sb("WS%d" % i, [128, SLOTW]) for i in range(NSLOT)]
        XIN = [sb("XIN0", [6, D])]
        SM = sb("SM", [128, NH, 128])
        SS = [sb("SS%d" % i, [128, NH, 128]) for i in range(2)]
        ST = [sb("ST%d" % i, [128, 128]) for i in range(4)]
        SHD = [sb("SHD%d" % i, [128, 128]) for i in range(4)]
        CST = sb("CST", [128, 2, 128])
        CSTB = sb("CSTB", [128, 2, 128])
        PRM = sb("PRM", [128, 96])
        DRV = sb("DRV", [128, 64])
        EB = sb("EB", [128, 2, 16])
        UH = sb("UH", [128, 8, 2])
        UES = sb("UES", [128, 2, 66])
        YS = sb("YS", [128, 2, 64])
        CB = sb("CB", [128, 8, 2, 2])
        CBO = sb("CBO", [128, 8, 3, 2])
        COUT = XIN[0][0:6, :]
        CTOK = COUT
        SMALL = sb("SMALL", [128, 32])
        FEN = sb("FEN", [128, 4])
        ps = es.enter_context(nc.psum_tensor("ps", [128, 8, 512], F32))

        ident = CST[:, 0, :]
        mask = CST[:, 1, :]
        onesd = CSTB[:, 0, :].bitcast(F32R)
        onesv = CSTB[:, 1, :].bitcast(F32R)
        P_L0, P_L1, P_GMIX, P_GFFN, P_GPLE, P_HGN, P_CW, P_GFIN = 0, 8, 16, 24, 32, 40, 41, 72
        LB, OML, NOML, HO, NHO, F0 = 0, 8, 16, 32, 40, 48
        epsc = SMALL[:, 0:1]

        dma_ctr = [0]

        def fsz(ap):
            n = 1
            for d in ap.shape[1:]:
                n *= d
            return n

        def dma(out, in_, reads, writes, key):
            S.op("sp", lambda e, o=out, i=in_: e.dma_start(out=o, in_=i), reads=reads, writes=writes, dma=key,
                 cost=2.0 + out.shape[0] * fsz(out) * 4 / 250e3)

        def ecost(eng, n):
            return {"act": 0.22 + n / 1200.0, "dve": 0.06 + n / 960.0, "pool": 0.1 + n / 420.0}[eng]

        def act(out, in_, func, reads, writes, bias=None, scale=None):
            kw = {}
            if bias is not None:
                kw["bias"] = bias
            if scale is not None:
                kw["scale"] = scale
            o_ = S.op("act", lambda e, o=out, i=in_, f=func, k=kw: e.activation(out=o, in_=i, func=f, **k),
                      reads=reads, writes=writes, cost=ecost("act", fsz(out)))
            o_.tset = {AF.Silu: 18, AF.Tanh: 18, AF.Sigmoid: 2, AF.Ln: 6, AF.Exp: 6}.get(func, -1)

        def tt(eng, out, in0, in1, op, reads, writes):
            S.op(eng, lambda e, o=out, a=in0, b=in1, p=op: e.tensor_tensor(out=o, in0=a, in1=b, op=p),
                 reads=reads, writes=writes, cost=ecost(eng, fsz(out)))

        def ts(eng, out, in0, s1, s2, op0, op1, reads, writes):
            if op1 is None:
                S.op(eng, lambda e, o=out, a=in0, x=s1, p0=op0: e.tensor_scalar(out=o, in0=a, scalar1=x, scalar2=None, op0=p0),
                     reads=reads, writes=writes, cost=ecost(eng, fsz(out)))
            else:
                S.op(eng, lambda e, o=out, a=in0, x=s1, y=s2, p0=op0, p1=op1:
                     e.tensor_scalar(out=o, in0=a, scalar1=x, scalar2=y, op0=p0, op1=p1), reads=reads, writes=writes,
                     cost=ecost(eng, fsz(out)))

        def stt(out, in0, scalar, in1, op0, op1, reads, writes):
            S.op("dve", lambda e, o=out, a=in0, s=scalar, b=in1, p0=op0, p1=op1:
                 e.scalar_tensor_tensor(out=o, in0=a, scalar=s, in1=b, op0=p0, op1=p1), reads=reads, writes=writes,
                 cost=ecost("dve", fsz(out)))

        def cp(eng, out, in_, reads, writes):
            if eng == "act":
                act(out, in_, AF.Copy, reads, writes)
            else:
                S.op(eng, lambda e, o=out, i=in_: e.tensor_copy(out=o, in_=i), reads=reads, writes=writes, cost=ecost(eng, fsz(out)))

        def mm(out, lhsT, rhs, start, stop, reads, writes):
            nmov = fsz(rhs)
            S.op("pe", lambda e, o=out, l=lhsT, r=rhs, a=start, b=stop: e.matmul(o, l, r, start=a, stop=b),
                 reads=reads, writes=writes, cost=(nmov if nmov >= 256 else 4 * nmov) / 1900.0 + 0.02)

        def tr(out, in_, idn, reads, writes):
            S.op("pe", lambda e, o=out, i=in_, d=idn: e.transpose(o, i, d), reads=reads, writes=writes, cost=0.28)

        def fence(reads, writes):
            S.op("pool", lambda e: e.memset(FEN[:, 0:1], 0.0), reads=reads, writes=list(writes) + ["FEN"])

        def R(ap):
            return ap.bitcast(F32R)

        slot_ctr = [0]

        def load_unit(dram_ap, kb, w):
            i = slot_ctr[0] % NSLOT
            slot_ctr[0] += 1
            view = WS[i][:, 0:kb * w].rearrange("p (k w) -> p k w", k=kb)
            dma(R(view), R(dram_ap), reads=[], writes=["WS%d" % i], key="WS%d" % i)
            return view, "WS%d" % i

        accsel = [0]

        def next_acc():
            i = accsel[0] % 2
            accsel[0] += 1
            return i

        smsel = [0]

        def next_sm():
            n = smsel[0]
            smsel[0] += 1
            return 4 + n % 2, (n // 2) % 4

        sm4sel = [0]

        def next_sm4():
            n = sm4sel[0]
            sm4sel[0] += 1
            return (4, 5, 6, 7)[n % 4], (n // 4) % 4

        sm6sel = [0]

        def next_sm6():
            n = sm6sel[0]
            sm6sel[0] += 1
            return (4, 5, 0, 1, 2, 3)[n % 6], 0

        def smkey(b, q):
            return "pb%d" % b

        def acckeys(i):
            return ["pb%d" % (2 * i), "pb%d" % (2 * i + 1)]

        OBK = ["pb6", "pb7"]
        SMALLK = [smkey(4, q) for q in range(4)] + [smkey(5, q) for q in range(4)]

        dma(CST[:, 0:2, :], cst[:, 0:2, :], [], ["CSTp"], "c0")
        dma(R(CSTB[:, :, :]), R(cst[:, 2:4, :]), [], ["CST"], "c0b")
        dma(PRM[:, :], prm, [], ["PRM"], "c1")
        S.op("pool", lambda e: e.memset(SMALL[:, 1:2], 1.0), reads=[], writes=["SMALLa"])
        S.op("pool", lambda e: e.memset(SMALL[:, 0:1], EPS), reads=["SMALLa"], writes=["SMALL"])
        for i in range(2):
            ts("dve", R(KHZ[i][:, :, :]), CST[:, 0:1, :].to_broadcast([128, 6, 128]), 0.0, None, ALU.mult, None,
               ["CSTp"], ["KHZ%d.%d" % (i, t) for t in range(6)])
        S.op("pool", lambda e: e.memset(RM[:, :], 1.0), reads=[], writes=["RMa"])
        S.op("pool", lambda e: e.memset(RM[:, 0::64], 0.0), reads=["RMa"], writes=["RM"])
        tt("dve", DRV[:, 24:32], PRM[:, P_L0:P_L0 + 8], PRM[:, P_L1:P_L1 + 8], ALU.subtract, ["PRM"], ["DRVt"])
        act(DRV[:, LB:LB + 8], DRV[:, 24:32], AF.Sigmoid, ["DRVt"], ["DRVlb"])
        ts("dve", DRV[:, OML:OML + 8], DRV[:, LB:LB + 8], -1.0, 1.0, ALU.mult, ALU.add, ["DRVlb"], ["DRVo"])
        ts("dve", DRV[:, NOML:NOML + 8], DRV[:, LB:LB + 8], 1.0, -1.0, ALU.mult, ALU.add, ["DRVo", "DRVlb"], ["DRVn"])
        ts("dve", DRV[:, HO:HO + 8], DRV[:, OML:OML + 8], 0.5, None, ALU.mult, None, ["DRVn", "DRVo"], ["DRVh"])
        ts("dve", DRV[:, NHO:NHO + 8], DRV[:, OML:OML + 8], -0.5, None, ALU.mult, None, ["DRVh"], ["DRVnh"])
        tt("dve", DRV[:, F0:F0 + 8], DRV[:, LB:LB + 8], DRV[:, HO:HO + 8], ALU.add, ["DRVnh", "DRVh", "DRVlb"], ["DRV"])
        dma(SM[:, :, :], s0[0].rearrange("h k v -> k h v"), [], ["SM%d" % h for h in range(NH)], "c3")
        for k in range(2):
            dma(SS[k][:, :, :], s0[1 + k].rearrange("h k v -> k h v"), [], ["SS%d.%d" % (k, h) for h in range(NH)], "c4%d" % k)
        dma(CTOK[0:4, :], cbuf.rearrange("s r d -> (s r) d"), [], ["XIN0"], "c5")
        for blk in range(8):
            b, q = next_sm()
            tr(ps[:, b, q * 128:q * 128 + 4], CTOK[0:4, blk * 128:(blk + 1) * 128], ident[0:4, 0:4],
               ["XIN0", "CSTp"], [smkey(b, q)])
            cp("dve", CB[:, blk, :, :].rearrange("p s r -> p (s r)"), ps[:, b, q * 128:q * 128 + 4], [smkey(b, q)], ["CB"])

        def slab2(ap_slab, T):
            return ap_slab[:, 0:T].rearrange("p (b t) -> p b t", b=2)

        def accview(i, TB):
            return ps[:, 2 * i:2 * i + 2, 0:TB]

        def load_x_and_transpose(src, rows, nt):
            t0, T = rows[0], nt * 128
            for blk in range(8):
                dma(XT[:, blk, 0:T], src[:, blk, t0:t0 + T], [], ["XT%d" % blk], "xl%d" % blk)

        def norm_stats_blk(blk, T):
            TB = T // 2
            i = blk % 2
            act(R(SQ[i][:, 0:T]), XT[:, blk, 0:T], AF.Square, ["XT%d" % blk], ["SQ%d" % i])
            for tb in range(2):
                mm(ps[:, 6 + tb, 0:TB], onesd, R(SQ[i][:, tb * TB:(tb + 1) * TB]), blk == 0, blk == 7,
                   ["SQ%d" % i, "CST"], [OBK[tb]])

        def norm_finish(T, gcol):
            TB = T // 2
            act(slab2(RSTD, T), ps[:, 6:8, 0:TB], AF.Ln, OBK + ["SMALL"], ["slA0"], bias=epsc)
            act(RSTD[:, 0:T], RSTD[:, 0:T], AF.Exp, ["slA0"], ["slA0"], scale=-0.5)
            for blk in range(8):
                stt(R(NT[:, blk, 0:T]), XT[:, blk, 0:T], PRM[:, gcol + blk:gcol + blk + 1], RSTD[:, 0:T],
                    ALU.mult, ALU.mult, ["XT%d" % blk, "slA0", "PRM"], ["NT%d" % blk])

        def norm_to_NT(T, gcol):
            for blk in range(8):
                norm_stats_blk(blk, T)
            norm_finish(T, gcol)

        NTK = ["NT%d" % b for b in range(8)]

        def kloop(accv_i, TB, lhs_fn, rhs_fn, nk, reads):
            fams = ("NT", "B1.", "B2.")
            for kb in range(nk):
                rk = [k for k in reads if not k.startswith(fams)]
                for fam in fams:
                    if any(k.startswith(fam) for k in reads):
                        rk.append("%s%d" % (fam, kb))
                for tb in range(2):
                    mm(ps[:, 2 * accv_i + tb, 0:TB], lhs_fn(kb), rhs_fn(kb, tb), kb == 0, kb == nk - 1,
                       rk, ["pb%d" % (2 * accv_i + tb)])

        def v_phase(T, nt, VT):
            widths = [384, 384, 256]
            c0 = 0
            n = 0
            for cu in range(3):
                w = widths[cu]
                view, wk = load_unit(wv[cu, :, :, 0:w], 8, w)
                for ti in range(nt):
                    bank = n % 4
                    n += 1
                    for blk in range(8):
                        mm(ps[:, bank, 0:w], R(NT[:, blk, ti * 128:(ti + 1) * 128]), R(view[:, blk, :]), blk == 0, blk == 7,
                           [wk, "NT%d" % blk], ["pb%d" % bank])
                    cp("act" if n % 2 == 0 else "dve", R(VT[:, ti, c0:c0 + w]), ps[:, bank, 0:w], ["pb%d" % bank], ["VT%d" % ti])
                    yield
                c0 += w

        def seq(*gens):
            for g in gens:
                yield from g

        def head_proj(h, T, state_only):
            TB = T // 2
            par = h % 2
            A, C = slabs["A%d" % par], slabs["C%d" % par]
            Ak, Ck = "slA%d" % par, "slC%d" % par
            view, wk = load_unit(wh[h, :, :, 0:256], 8, 256)
            cols = [("q", 0), ("f", 128)]
            for name, c0 in cols:
                i = next_acc()
                kloop(i, TB, lambda kb, c0=c0: R(view[:, kb, c0:c0 + 128]),
                      lambda kb, tb: R(NT[:, kb, tb * TB:(tb + 1) * TB]), 8, [wk] + NTK)
                dst, dk, fn = {"q": (A, Ak, AF.Silu), "f": (C, Ck, AF.Tanh)}[name]
                act(slab2(dst, T), accview(i, TB), fn, acckeys(i), [dk], scale=(0.5 if name == "f" else None))
                yield

        def head_chain(h, T, hf):
            par = h % 2
            TB = T // 2
            t0, t1 = hf * TB, (hf + 1) * TB
            sfx = "#%d" % hf
            A, C = slabs["A%d" % par][:, t0:t1], slabs["C%d" % par][:, t0:t1]
            Ak, Ck = "slA%d" % par + sfx, "slC%d" % par + sfx
            Dd, E, F = slabs["D"][:, t0:t1], slabs["E"][:, t0:t1], slabs["F"][:, t0:t1]
            Dk, Ek, Fk = "slD" + sfx, "slE" + sfx, "slF" + sfx
            QHd, QHk = QH[par][:, t0:t1], "QH%d" % par + sfx
            KTd, KTk = KT[:, t0:t1], "KT" + sfx
            nchh = TB // 64
            EBd, EBk = EB[:, par, hf * nchh:(hf + 1) * nchh], "EB%d" % par + sfx
            hoc, nhoc, f0c = DRV[:, HO + h:HO + h + 1], DRV[:, NHO + h:NHO + h + 1], DRV[:, F0 + h:F0 + h + 1]
            ts("dve", Dd, C, hoc, f0c, ALU.mult, ALU.add, [Ck, "DRV"], [Dk])
            act(E, C, AF.Identity, [Ck, "DRV"], [Ek], bias=hoc, scale=nhoc)
            yield
            act(Dd, Dd, AF.Ln, [Dk], [Dk])
            yield
            S.op("dve", lambda e: e.tensor_tensor_scan(out=F, data0=RM[:, t0:t1], data1=Dd,
                                                       initial=0.0, op0=ALU.mult, op1=ALU.add),
                 reads=["RM", Dk], writes=[Fk], cost=0.06 + 2 * TB / 960.0)
            yield
            F3 = F.rearrange("p (c t) -> p c t", t=64)
            act(C, F, AF.Exp, [Fk], [Ck])
            yield
            cp("dve", EBd, C[:, 63:TB:64], [Ck], [EBk])
            act(Dd, F, AF.Exp, [Fk], [Dk], scale=-1.0)
            yield
            stt(R(QHd), A, float(128 ** -0.5), C, ALU.mult, ALU.mult, [Ak, Ck], [QHk])
            yield
            tt("dve", R(KTd), E, Dd, ALU.mult, [Ek, Dk], [KTk])
            yield
            A3 = A.rearrange("p (c t) -> p c t", t=64)
            tt("dve", A3, F3[:, :, 63:64].to_broadcast([128, nchh, 64]), F3, ALU.subtract, [Fk], [Ak])
            yield
            act(A, A, AF.Exp, [Ak], [Ak])
            yield
            tt("pool", A, E, A, ALU.mult, [Ek, Ak], [Ak])
            yield

        def head_rec(h, T, nt, VT, nsamp):
            par = h % 2
            A, Ak = slabs["A%d" % par], "slA%d" % par
            QHh, QHk = QH[par], "QH%d" % par
            EBk = "EB%d" % par
            BS = slabs["BS"]
            TB = T // 2
            nch = 2 * nt
            npc = 2 * (nt - nsamp)
            for ti in range(nt):
                b, q = next_sm4()
                tr(ps[:, b, q * 128:(q + 1) * 128], A[:, ti * 128:(ti + 1) * 128], ident, [Ak, "CSTp"], [smkey(b, q)])
                cp("act", R(KHZ[0][0:64, ti, :]), ps[0:64, b, q * 128:(q + 1) * 128], [smkey(b, q)], ["KHZ0.%d" % ti])
                cp("dve", R(KHZ[1][64:128, ti, :]), ps[64:128, b, q * 128:(q + 1) * 128], [smkey(b, q)], ["KHZ1.%d" % ti])
            yield
            for ti in range(nt):
                b, q = next_sm4()
                mm(ps[:, b, q * 128:(q + 1) * 128], R(KT[:, ti * 128:(ti + 1) * 128]), R(QHh[:, ti * 128:(ti + 1) * 128]),
                   True, True, ["KT", QHk], [smkey(b, q)])
                tt("dve", R(SCM[:, ti, :]), ps[:, b, q * 128:(q + 1) * 128], mask, ALU.mult,
                   [smkey(b, q), "CSTp"], ["SCM%d" % ti])
                yield

            def state_io(c):
                if c >= npc:
                    k = c - npc
                    return (SS[k][:, h, :], "SS%d.%d" % (k, h)), (SS[k][:, h, :], "SS%d.%d" % (k, h))
                src = (SM[:, h, :], "SM%d" % h) if c == 0 else (ST[(c - 1) % 4][:, :], "ST%d" % ((c - 1) % 4))
                dst = (SM[:, h, :], "SM%d" % h) if c == npc - 1 else (ST[c % 4][:, :], "ST%d" % (c % 4))
                return src, dst

            def emit_ds(c):
                ti, p0 = c // 2, (c % 2) * 64
                b, q = next_sm()
                mm(ps[:, b, q * 128:(q + 1) * 128], R(KHZ[c % 2][:, ti, :]), R(VT[:, ti, h * 128:(h + 1) * 128]),
                   True, True, ["KHZ%d.%d" % (c % 2, ti), "VT%d" % ti], [smkey(b, q)])
                return b, q

            def emit_stt(c, b, q):
                (src, srck), (dst, dstk) = state_io(c)
                stt(dst, src, EB[:, par, c:c + 1], ps[:, b, q * 128:(q + 1) * 128], ALU.mult, ALU.add,
                    [srck, EBk, smkey(b, q)], [dstk])

            def emit_shadow(c):
                (src, srck), _ = state_io(c)
                cp("pool", R(SHD[c % 4][:, :]), src, [srck], ["SHD%d" % (c % 4)])

            def emit_o(c):
                ti, p0 = c // 2, (c % 2) * 64
                (src, srck), _ = state_io(c)
                tb, off = (c * 64) // TB, (c * 64) % TB
                mm(ps[:, 6 + tb, off:off + 64], R(VT[:, ti, h * 128:(h + 1) * 128]),
                   R(SCM[:, ti, p0:p0 + 64]), True, False, ["VT%d" % ti, "SCM%d" % ti], [OBK[tb]])
                shi = c % 4
                mm(ps[:, 6 + tb, off:off + 64], R(SHD[shi][:, :]), R(QHh[:, c * 64:(c + 1) * 64]), False, True,
                   ["SHD%d" % shi, QHk], [OBK[tb]])

            if npc >= 4:
                for step in range(npc + 2):
                    if step < npc:
                        b, q = emit_ds(step)
                        emit_shadow(step)
                        emit_stt(step, b, q)
                    if step >= 2:
                        emit_o(step - 2)
                    yield
            else:
                for c in range(npc):
                    b, q = emit_ds(c)
                    emit_shadow(c)
                    emit_o(c)
                    emit_stt(c, b, q)
                    yield
            for c in range(npc, nch):
                b, q = emit_ds(c)
                emit_shadow(c)
                emit_o(c)
                emit_stt(c, b, q)
                yield
            act(R(slab2(SQ[0], T)), ps[:, 6:8, 0:TB], AF.Square, OBK, ["SQ0"])
            yield
            i = next_acc()
            for tb in range(2):
                mm(ps[:, 2 * i + tb, 0:TB], onesv, R(SQ[0][:, tb * TB:(tb + 1) * TB]), True, True, ["SQ0", "CST"], ["pb%d" % (2 * i + tb)])
            act(R(slab2(SQ[1], T)), accview(i, TB), AF.Ln, acckeys(i) + ["SMALL"], ["SQ1"], bias=epsc)
            yield
            act(R(SQ[1][:, 0:T]), SQ[1][:, 0:T], AF.Exp, ["SQ1"], ["SQ1"], scale=-0.5)
            yield
            stt(R(slab2(B1[:, h, :], T)), ps[:, 6:8, 0:TB], PRM[:, P_HGN:P_HGN + 1], slab2(SQ[1], T), ALU.mult, ALU.mult,
                OBK + ["SQ1", "PRM"], ["B1.%d" % h])
            yield
            view, wk = load_unit(wh[h, :, :, 256:384], 8, 128)
            i = next_acc()
            kloop(i, TB, lambda kb: R(view[:, kb, 0:128]), lambda kb, tb: R(NT[:, kb, tb * TB:(tb + 1) * TB]), 8, [wk] + NTK)
            act(slab2(BS, T), accview(i, TB), AF.Silu, acckeys(i), ["slBS"])
            yield
            tt("pool", R(B1[:, h, 0:T]), B1[:, h, 0:T], BS[:, 0:T], ALU.mult, ["B1.%d" % h, "slBS"], ["B1.%d" % h])
            yield

        def pre_head(h, T, nt, VT, slabset):
            (C, Ck), (Dd, Dk), (E, Ek), (F, Fk) = slabset
            TB = T // 2
            hoc, nhoc, f0c = DRV[:, HO + h:HO + h + 1], DRV[:, NHO + h:NHO + h + 1], DRV[:, F0 + h:F0 + h + 1]
            view, wk = load_unit(wh[h, :, :, 128:256], 8, 128)
            i = next_acc()
            kloop(i, TB, lambda kb: R(view[:, kb, 0:128]), lambda kb, tb: R(NT[:, kb, tb * TB:(tb + 1) * TB]), 8, [wk] + NTK)
            act(C[:, 0:T].rearrange("p (b t) -> p b t", b=2), accview(i, TB), AF.Tanh, acckeys(i), [Ck], scale=0.5)
            yield
            ts("dve", Dd[:, 0:T], C[:, 0:T], hoc, f0c, ALU.mult, ALU.add, [Ck, "DRV"], [Dk])
            act(E[:, 0:T], C[:, 0:T], AF.Identity, [Ck, "DRV"], [Ek], bias=hoc, scale=nhoc)
            yield
            act(Dd[:, 0:T], Dd[:, 0:T], AF.Ln, [Dk], [Dk])
            yield
            S.op("dve", lambda e: e.tensor_tensor_scan(out=F[:, T - 1::-1], data0=SMALL[:, 1:2].to_broadcast([128, T]),
                                                       data1=Dd[:, T - 1::-1], initial=0.0, op0=ALU.mult, op1=ALU.add),
                 reads=["SMALL", Dk], writes=[Fk], cost=0.06 + 2 * T / 960.0)
            yield
            act(C[:, 0:T - 1], F[:, 1:T], AF.Exp, [Fk], [Ck])
            S.op("pool", lambda e: e.memset(C[:, T - 1:T], 1.0), reads=[Ck], writes=[Ck], cost=0.1)
            act(SMALL[:, 16 + h:17 + h], F[:, 0:1], AF.Exp, [Fk], ["EBP%d" % h])
            yield
            tt("pool", C[:, 0:T], E[:, 0:T], C[:, 0:T], ALU.mult, [Ek, Ck], [Ck])
            yield
            ob = 6 + h % 2
            pc0 = h * 128 if h < NH - 1 else (h - 1) * 128
            for ti in range(nt):
                b, q = next_sm()
                tr(ps[:, b, q * 128:(q + 1) * 128], C[:, ti * 128:(ti + 1) * 128], ident, [Ck, "CSTp"], [smkey(b, q)])
                cp("act" if ti % 2 == 0 else "dve", R(SCM[:, ti, :]), ps[:, b, q * 128:(q + 1) * 128], [smkey(b, q)], ["SCM%d" % ti])
            for ti in range(nt):
                mm(ps[:, ob, 0:128], R(SCM[:, ti, :]), R(VT[:, ti, h * 128:(h + 1) * 128]), ti == 0, ti == nt - 1,
                   ["SCM%d" % ti, "VT%d" % ti], ["pb%d" % ob])
            stt(SM[:, h, :], SM[:, h, :], SMALL[:, 16 + h:17 + h], ps[:, ob, 0:128], ALU.mult, ALU.add,
                ["SM%d" % h, "EBP%d" % h, "pb%d" % ob], ["SM%d" % h])
            yield

        def interleave(gens, weights=None):
            gens = list(gens)
            weights = [1] * len(gens) if weights is None else list(weights)
            live = list(range(len(gens)))
            while live:
                for gi in list(live):
                    for _ in range(weights[gi]):
                        try:
                            next(gens[gi])
                        except StopIteration:
                            live.remove(gi)
                            break

        def gated_branch(T, wz_dram, wm_dram, srcbuf_key, first):
            TB = T // 2
            MIX = B2[:, 0:8 * T].rearrange("p (c t) -> p c t", c=8)
            tmp = [slabs["A0"], slabs["A1"]]
            tmpk = ["slA0", "slA1"]
            sg = [slabs["C0"], slabs["C1"]]
            sgk = ["slC0", "slC1"]
            for cp_ in range(4):
                vz, kz = load_unit(wz_dram[cp_], 8, 256)
                vm, km = load_unit(wm_dram[cp_], 8, 256)
                for cq in range(2):
                    cb = cp_ * 2 + cq
                    i = next_acc()
                    kloop(i, TB, lambda kb, cq=cq, vz=vz: R(vz[:, kb, cq * 128:(cq + 1) * 128]),
                          lambda kb, tb: R(NT[:, kb, tb * TB:(tb + 1) * TB]), 8, [kz] + NTK)
                    act(slab2(sg[cq], T), accview(i, TB), AF.Sigmoid, acckeys(i), [sgk[cq]])
                    i = next_acc()
                    kloop(i, TB, lambda kb, cq=cq, vm=vm: R(vm[:, kb, cq * 128:(cq + 1) * 128]),
                          lambda kb, tb: R(B1[:, kb, tb * TB:(tb + 1) * TB]), 8, [km] + ["B1.%d" % b for b in range(8)])
                    mixv = MIX[:, cb, :].rearrange("p (b t) -> p b t", b=2)
                    if first:
                        tt("dve", R(mixv), slab2(sg[cq], T), accview(i, TB), ALU.mult, [sgk[cq]] + acckeys(i), ["B2.%d" % cb])
                    else:
                        tt("dve", slab2(tmp[cq], T), slab2(sg[cq], T), accview(i, TB), ALU.mult, [sgk[cq]] + acckeys(i), [tmpk[cq]])
                        tt("pool", R(MIX[:, cb, :]), MIX[:, cb, :], tmp[cq][:, 0:T], ALU.add, ["B2.%d" % cb, tmpk[cq]], ["B2.%d" % cb])
            return MIX

        def conv_phase(T, Tp, nsamp, last):
            TB = T // 2
            CT, UE, Y = slabs["E"], slabs["F"], slabs["D"]
            for j in range(8):
                view, wk = load_unit(wc[j], 8, 384)
                w0 = PRM[:, P_CW + 0 * 8 + j:P_CW + 0 * 8 + j + 1]
                w1 = PRM[:, P_CW + 1 * 8 + j:P_CW + 1 * 8 + j + 1]
                w2 = PRM[:, P_CW + 2 * 8 + j:P_CW + 2 * 8 + j + 1]
                i = next_acc()
                kloop(i, TB, lambda kb: R(view[:, kb, 128:256]), lambda kb, tb: R(NT[:, kb, tb * TB:(tb + 1) * TB]), 8, [wk] + NTK)
                cp("act", slab2(CT, T), accview(i, TB), acckeys(i), ["slE"])
                i = next_acc()
                kloop(i, TB, lambda kb: R(view[:, kb, 256:384]), lambda kb, tb: R(NT[:, kb, tb * TB:(tb + 1) * TB]), 8, [wk] + NTK)
                tt("dve", UE[:, 2:2 + T].rearrange("p (b t) -> p b t", b=2), slab2(CT, T), accview(i, TB), ALU.mult,
                   ["slE"] + acckeys(i), ["slF"])
                cp("pool", UE[:, 0:2], UH[:, j, :], ["UH%d" % j], ["slF"])
                act(Y[:, 0:T], UE[:, 2:2 + T], AF.Copy, ["slF", "PRM"], ["slD"], scale=w2)
                stt(Y[:, 0:T], UE[:, 1:1 + T], w1, Y[:, 0:T], ALU.mult, ALU.add, ["slF", "slD", "PRM"], ["slD"])
                stt(Y[:, 0:T], UE[:, 0:T], w0, Y[:, 0:T], ALU.mult, ALU.add, ["slF", "slD", "PRM"], ["slD"])
                if nsamp:
                    cp("pool", UES[:, :, 2:66], UE[:, 2 + Tp:2 + Tp + 128].rearrange("p (s t) -> p s t", s=2), ["slF"], ["UES"])
                    cp("pool", UES[:, :, 0:2], CB[:, j, :, :], ["CB", "UES"], ["UES"])
                    act(YS[:, :, :], UES[:, :, 2:66], AF.Copy, ["UES", "PRM"], ["YS"], scale=w2)
                    stt(YS[:, :, :], UES[:, :, 1:65], w1, YS[:, :, :], ALU.mult, ALU.add, ["UES", "YS", "PRM"], ["YS"])
                    stt(Y[:, Tp:Tp + 128].rearrange("p (s t) -> p s t", s=2), UES[:, :, 0:64], w0, YS[:, :, :], ALU.mult, ALU.add,
                        ["UES", "YS", "PRM", "slD"], ["slD"])
                    cp("pool", CBO[:, j, 1:3, :], UES[:, :, 64:66], ["UES"], ["CBO"])
                cp("pool", UH[:, j, :], UE[:, Tp:Tp + 2], ["slF"], ["UH%d" % j])
                if last:
                    cp("pool", CBO[:, j, 0, :], UE[:, Tp:Tp + 2], ["slF", "CBO"], ["CBO"])
                i = next_acc()
                kloop(i, TB, lambda kb: R(view[:, kb, 0:128]), lambda kb, tb: R(NT[:, kb, tb * TB:(tb + 1) * TB]), 8, [wk] + NTK)
                tt("dve", R(B1[:, j, 0:T].rearrange("p (b t) -> p b t", b=2)), slab2(Y, T), accview(i, TB), ALU.mult,
                   ["slD"] + acckeys(i), ["B1.%d" % j])

        def resid_matmul(T, w_dram_units, nk, src_fn, src_keys, after_cb=None):
            TB = T // 2
            for cp_ in range(4):
                view, wk = load_unit(w_dram_units(cp_), nk, 256)
                for cq in range(2):
                    cb = cp_ * 2 + cq
                    i = next_acc()
                    kloop(i, TB, lambda kb, cq=cq, view=view: R(view[:, kb, cq * 128:(cq + 1) * 128]),
                          lambda kb, tb: src_fn(kb, tb), nk, [wk] + src_keys)
                    xv = XT[:, cb, 0:T].rearrange("p (b t) -> p b t", b=2)
                    tt("dve", xv, xv, accview(i, TB), ALU.add, ["XT%d" % cb] + acckeys(i), ["XT%d" % cb])
                    if after_cb is not None and cb >= 1:
                        after_cb(cb - 1)
            if after_cb is not None:
                after_cb(7)

        def ffn_phase(T):
            TB = T // 2
            GS = [slabs["A0"], slabs["A1"]]
            GSk = ["slA0", "slA1"]
            for g, (j0, nj) in enumerate(FFG):
                for jj in range(nj):
                    j = j0 + jj
                    view, wk = load_unit(wgu[j], 8, 256)
                    i = next_acc()
                    kloop(i, TB, lambda kb: R(view[:, kb, 0:128]), lambda kb, tb: R(NT[:, kb, tb * TB:(tb + 1) * TB]), 8, [wk] + NTK)
                    act(slab2(GS[jj % 2], T), accview(i, TB), AF.Silu, acckeys(i), [GSk[jj % 2]])
                    i = next_acc()
                    kloop(i, TB, lambda kb: R(view[:, kb, 128:256]), lambda kb, tb: R(NT[:, kb, tb * TB:(tb + 1) * TB]), 8, [wk] + NTK)
                    tt("dve", R(B1[:, jj, 0:T].rearrange("p (b t) -> p b t", b=2)), slab2(GS[jj % 2], T), accview(i, TB), ALU.mult,
                       [GSk[jj % 2]] + acckeys(i), ["B1.%d" % jj])
                resid_matmul(T, lambda cp_, g=g, nj=nj: wd[g * 4 + cp_, :, 0:nj, :], nj,
                             lambda kb, tb: R(B1[:, kb, tb * TB:(tb + 1) * TB]), ["B1.%d" % b for b in range(nj)],
                             after_cb=(lambda cb: norm_stats_blk(cb, T)) if g == len(FFG) - 1 else None)

        def ple_phase(T, nt, rows):
            TB = T // 2
            PT = B2[:, 0:2 * T].rearrange("p (c t) -> p c t", c=2)
            t0 = rows[0]
            for kb in range(2):
                dma(R(PT[:, kb, 0:T]), R(pmain[:, kb, t0:t0 + T]), [], ["B2.%d" % kb], "pl%d" % kb)
            sg = [slabs["C0"], slabs["C1"]]
            sgk = ["slC0", "slC1"]
            tmp = [slabs["A0"], slabs["A1"]]
            tmpk = ["slA0", "slA1"]
            for cp_ in range(4):
                vz, kz = load_unit(wpg[cp_], 8, 256)
                vm, km = load_unit(wpl[cp_], 2, 256)
                for cq in range(2):
                    cb = cp_ * 2 + cq
                    i = next_acc()
                    kloop(i, TB, lambda kb, cq=cq, vz=vz: R(vz[:, kb, cq * 128:(cq + 1) * 128]),
                          lambda kb, tb: R(NT[:, kb, tb * TB:(tb + 1) * TB]), 8, [kz] + NTK)
                    act(slab2(sg[cq], T), accview(i, TB), AF.Sigmoid, acckeys(i), [sgk[cq]])
                    i = next_acc()
                    kloop(i, TB, lambda kb, cq=cq, vm=vm: R(vm[:, kb, cq * 128:(cq + 1) * 128]),
                          lambda kb, tb: R(PT[:, kb, tb * TB:(tb + 1) * TB]), 2, [km, "B2.0", "B2.1"])
                    tt("dve", slab2(tmp[cq], T), slab2(sg[cq], T), accview(i, TB), ALU.mult, [sgk[cq]] + acckeys(i), [tmpk[cq]])
                    tt("pool", XT[:, cb, 0:T], XT[:, cb, 0:T], tmp[cq][:, 0:T], ALU.add, ["XT%d" % cb, tmpk[cq]], ["XT%d" % cb])
                    if cb >= 1:
                        norm_stats_blk(cb - 1, T)
            norm_stats_blk(7, T)

        def final_out(T, nt, rows, next_load=None):
            TB = T // 2
            act(slab2(RSTD, T), ps[:, 6:8, 0:TB], AF.Ln, OBK + ["SMALL"], ["slA0"], bias=epsc)
            act(RSTD[:, 0:T], RSTD[:, 0:T], AF.Exp, ["slA0"], ["slA0"], scale=-0.5)
            t0 = rows[0]
            ybuf = [(slabs[n], "sl" + n) for n in ("A1", "BS", "C0", "C1", "D", "E", "F")] + [(XT[:, 7, :], "XT7")]
            for blk in range(8):
                yb, yk = ybuf[blk]
                stt(yb[:, 0:T], XT[:, blk, 0:T], PRM[:, P_GFIN + blk:P_GFIN + blk + 1], RSTD[:, 0:T],
                    ALU.mult, ALU.mult, ["XT%d" % blk, "slA0", "PRM"], [yk])
                dma(yout[:, blk, t0:t0 + T], yb[:, 0:T], [yk], ["yout%d" % blk], "yo%d" % blk)
                if next_load is not None and blk < 7:
                    next_load(blk)
            if next_load is not None:
                next_load(7)

        def run_pass(tiles, main, first_main, last_main, last_pre, preloaded=False, next_tiles=None):
            nt = len(tiles)
            T = nt * 128
            state_only = not main
            src = xmain if main else xpre
            rows = [g * 128 for g in tiles]
            nsamp = 1 if (main and samp_tile in tiles) else 0
            Tp = T - 128 * nsamp

            def chunk_state(c, h):
                if nsamp and c >= 2 * (nt - 1):
                    k = c - 2 * (nt - 1)
                    return SS[k][:, h, :], "SS%d.%d" % (k, h)
                return SM[:, h, :], "SM%d" % h

            fence(["B2.%d" % b for b in range(8)], ["VT%d" % t for t in range(6)])
            if not preloaded:
                load_x_and_transpose(src, rows, nt)
            norm_to_NT(T, P_GMIX)
            VT = B2[:, 0:nt * D].rearrange("p (t c) -> p t c", t=nt)
            if state_only:
                sets = [[(slabs[n], "sl" + n) for n in ("A0", "A1", "BS", "F")],
                        [(slabs[n], "sl" + n) for n in ("C0", "C1", "D", "E")],
                        [(XT[:, b, :], "XT%d" % b) for b in range(0, 4)],
                        [(XT[:, b, :], "XT%d" % b) for b in range(4, 8)]]
                interleave([v_phase(T, nt, VT)] + [pre_head(i, T, nt, VT, sets[i]) for i in range(4)], [4, 1, 1, 1, 1])
                interleave([pre_head(4 + i, T, nt, VT, sets[i]) for i in range(4)])
            else:
                interleave([head_proj(0, T, state_only)])
                interleave([v_phase(T, nt, VT), head_chain(0, T, 0), head_chain(0, T, 1), head_proj(1, T, state_only)],
                           [3, 1, 1, 1])
            for h in (range(NH) if not state_only else ()):
                gens, wts = [head_rec(h, T, nt, VT, nsamp)], [REC_W]
                if h + 1 < NH:
                    gens += [head_chain(h + 1, T, 0), head_chain(h + 1, T, 1)]
                    wts += [2, 2]
                if h + 2 < NH:
                    gens.append(head_proj(h + 2, T, state_only))
                    wts.append(1)
                interleave(gens, wts)
            if state_only:
                if last_pre:
                    for j in range(8):
                        view, wk = load_unit(wc[j], 8, 384)
                        b, q = next_sm()
                        for kb in range(8):
                            mm(ps[:, b, q * 128:q * 128 + 2], R(view[:, kb, 128:256]), R(NT[:, kb, T - 2:T]), kb == 0, kb == 7,
                               [wk] + NTK, [smkey(b, q)])
                        cp("act", SMALL[:, 8:10], ps[:, b, q * 128:q * 128 + 2], [smkey(b, q)], ["SMu"])
                        b2, q2 = next_sm()
                        for kb in range(8):
                            mm(ps[:, b2, q2 * 128:q2 * 128 + 2], R(view[:, kb, 256:384]), R(NT[:, kb, T - 2:T]), kb == 0, kb == 7,
                               [wk] + NTK, [smkey(b2, q2)])
                        tt("dve", UH[:, j, :], SMALL[:, 8:10], ps[:, b2, q2 * 128:q2 * 128 + 2], ALU.mult,
                           ["SMu", smkey(b2, q2)], ["UH%d" % j])
                return
            fence(["VT%d" % t for t in range(6)], ["B2.%d" % b for b in range(8)])
            gated_branch(T, wza, wa, None, True)
            conv_phase(T, Tp, nsamp, last_main)
            MIX = gated_branch(T, wzb, wb, None, False)
            TB = T // 2
            resid_matmul(T, lambda cp_: wo[cp_], 8, lambda kb, tb: R(MIX[:, kb, tb * TB:(tb + 1) * TB]),
                         ["B2.%d" % b for b in range(8)], after_cb=lambda cb: norm_stats_blk(cb, T))
            norm_finish(T, P_GFFN)
            ffn_phase(T)
            norm_finish(T, P_GPLE)
            ple_phase(T, nt, rows)
            if next_tiles is not None:
                nT_, nt0 = len(next_tiles) * 128, next_tiles[0] * 128
                final_out(T, nt, rows, lambda blk: dma(XT[:, blk, 0:nT_], xmain[:, blk, nt0:nt0 + nT_], [], ["XT%d" % blk], "xl%d" % blk))
            else:
                final_out(T, nt, rows)

        for pi, tiles in enumerate(pre_passes):
            run_pass(tiles, False, False, False, pi == len(pre_passes) - 1)
        for pi, tiles in enumerate(main_passes):
            run_pass(tiles, True, pi == 0, pi == len(main_passes) - 1, False, preloaded=(pi > 0),
                     next_tiles=(main_passes[pi + 1] if pi + 1 < len(main_passes) else None))

        dma(sfin[0].rearrange("h k v -> k h v"), SM[:, :, :], ["SM%d" % h for h in range(NH)], ["sfin0"], "o0")
        for k in range(2):
            dma(sfin[1 + k].rearrange("h k v -> k h v"), SS[k][:, :, :], ["SS%d.%d" % (k, h) for h in range(NH)], ["sfin%d" % (1 + k)], "o%d" % (1 + k))
        for blk in range(8):
            b, q = next_sm()
            tr(ps[0:6, b, q * 128:(q + 1) * 128], CBO[:, blk, :, :].rearrange("p s r -> p (s r)"), ident, ["CBO", "CSTp"], [smkey(b, q)])
            cp("dve", COUT[:, blk * 128:(blk + 1) * 128], ps[0:6, b, q * 128:(q + 1) * 128], [smkey(b, q)], ["XIN0"])
        dma(cfin.rearrange("s r d -> (s r) d"), COUT[:, :], ["XIN0"], ["cfin"], "o3")

        cnt = S.resolve()
        sems = {}
        for key in cnt:
            sems[key] = es.enter_context(nc.semaphore("s_%s_%s" % key))
        block = es.enter_context(nc.Block())
        S.emit(nc, block, sems)
    return nc


def _tile_cols(W, col_lists, kb, width):
    out = np.zeros((len(col_lists), 128, kb, width), np.float32)
    Wr = W.reshape(kb, 128, W.shape[1])
    for u, cols in enumerate(col_lists):
        out[u, :, :, :len(cols)] = np.transpose(Wr[:, :, cols], (1, 0, 2))
    return out


def _fm(x):
    t, d = x.shape
    return np.ascontiguousarray(np.transpose(x.reshape(t, d // 128, 128), (2, 1, 0)))


def _pcol(v):
    return np.ascontiguousarray(v.reshape(-1, 128).T)


_NC_CACHE = {}


def _prep_shared(lower_bounds, norm_mix, w_in, conv_w, hg_norm, w_branch_a, w_branch_b, w_out, norm_ffn,
                 w_gate_up, w_down, norm_ple, w_ple, w_ple_gate, norm_final):
    f = lambda a: np.ascontiguousarray(np.asarray(a, dtype=np.float32))
    win = f(w_in)[0]
    ar = np.arange
    HF, HV, CD = 1024, 1024, 1024
    oq, of_, oi, og, oB, oC, oh, oza, ozb = 0, 1024, 2048, 3072, 4096, 5120, 6144, 7168, 8192
    wv_ = _tile_cols(win, [list(oi + ar(0, 384)), list(oi + ar(384, 768)), list(oi + ar(768, 1024))], 8, 384)
    wh_ = _tile_cols(win, [list(oq + h * 128 + ar(128)) + list(of_ + h * 128 + ar(128)) + list(og + h * 128 + ar(128))
                           for h in range(8)], 8, 384)
    wc_ = _tile_cols(win, [list(oB + j * 128 + ar(128)) + list(oC + j * 128 + ar(128)) + list(oh + j * 128 + ar(128))
                           for j in range(8)], 8, 384)
    c256 = [list(c * 256 + ar(256)) for c in range(4)]
    wza_ = _tile_cols(win[:, oza:oza + 1024], c256, 8, 256)
    wzb_ = _tile_cols(win[:, ozb:ozb + 1024], c256, 8, 256)
    wa_ = _tile_cols(f(w_branch_a)[0], c256, 8, 256)
    wb_ = _tile_cols(f(w_branch_b)[0], c256, 8, 256)
    wo_ = _tile_cols(f(w_out)[0], c256, 8, 256)
    wpg_ = _tile_cols(f(w_ple_gate)[0], c256, 8, 256)
    wgu_full = f(w_gate_up)[0]
    wgu_ = _tile_cols(wgu_full, [list(j * 128 + ar(128)) + list(DFF + j * 128 + ar(128)) for j in range(22)], 8, 256)
    wdn = f(w_down)[0]
    wd_ = np.zeros((12, 128, 8, 256), np.float32)
    for g, (j0, nj) in enumerate(FFG):
        sub = wdn[j0 * 128:(j0 + nj) * 128]
        wd_[g * 4:(g + 1) * 4, :, :nj, :] = _tile_cols(sub, c256, nj, 256)
    wpl_ = _tile_cols(f(w_ple)[0], c256, 2, 256)

    prm = np.zeros((128, 96), np.float32)
    lbs = f(lower_bounds)
    prm[:, 0:8] = _pcol(lbs[0])
    prm[:, 8:16] = _pcol(lbs[1])
    prm[:, 16:24] = _pcol(f(norm_mix)[0])
    prm[:, 24:32] = _pcol(f(norm_ffn)[0])
    prm[:, 32:40] = _pcol(f(norm_ple)[0])
    prm[:, 40] = f(hg_norm)[0]
    cw = f(conv_w)[0]
    for tap in range(3):
        prm[:, 41 + tap * 8:41 + tap * 8 + 8] = _pcol(cw[tap])
    cst = np.zeros((128, 4, 128), np.float32)
    cst[:, 0, :] = np.eye(128, dtype=np.float32)
    s_i, t_i = np.meshgrid(ar(128), ar(128), indexing="ij")
    cst[:, 1, :] = ((s_i // 64 == t_i // 64) & (s_i <= t_i)).astype(np.float32)
    cst[:, 2, :] = 1.0 / D
    cst[:, 3, :] = 1.0 / 128
    prm[:, 72:80] = _pcol(f(norm_final))

    shared = dict(prm=prm, cst=cst, wv=wv_, wh=wh_, wza=wza_, wa=wa_, wzb=wzb_, wb=wb_, wo=wo_,
                  wpg=wpg_, wc=wc_, wgu=wgu_, wd=wd_, wpl=wpl_)
    return shared


def kernel(x_prompt, x_sample, p_prompt, p_sample, state_hgrn, state_conv, lower_bounds,
           norm_mix, w_in, conv_w, hg_norm, w_branch_a, w_branch_b, w_out, norm_ffn,
           w_gate_up, w_down, norm_ple, w_ple, w_ple_gate, norm_final):
    f = lambda a: np.ascontiguousarray(np.asarray(a, dtype=np.float32))
    x_prompt, x_sample, p_prompt, p_sample = f(x_prompt), f(x_sample), f(p_prompt), f(p_sample)
    state_hgrn, state_conv = f(state_hgrn), f(state_conv)
    shared = _prep_shared(lower_bounds, norm_mix, w_in, conv_w, hg_norm, w_branch_a, w_branch_b, w_out, norm_ffn,
                          w_gate_up, w_down, norm_ple, w_ple, w_ple_gate, norm_final)
    in_maps = []
    for c in range(NCORES):
        b, half = c // 2, c % 2
        xm = _fm(np.concatenate([x_prompt[b, half * 2048:(half + 1) * 2048], x_sample[2 * c], x_sample[2 * c + 1]], axis=0))
        pm = _fm(np.concatenate([p_prompt[0, b, half * 2048:(half + 1) * 2048], p_sample[0, 2 * c], p_sample[0, 2 * c + 1]], axis=0))
        xp = _fm(x_prompt[b, 0:2048]) if half == 1 else np.zeros((128, 8, TPRE), np.float32)
        s0 = np.zeros((3, NH, 128, 128), np.float32)
        s0[1] = state_hgrn[0, 2 * c]
        s0[2] = state_hgrn[0, 2 * c + 1]
        cb = np.stack([state_conv[0, 2 * c], state_conv[0, 2 * c + 1]], axis=0)
        m = dict(shared)
        m.update(xpre=np.ascontiguousarray(xp), xmain=np.ascontiguousarray(xm), pmain=np.ascontiguousarray(pm),
                 s0=s0, cbuf=np.ascontiguousarray(cb))
        in_maps.append(m)

    if "nc" not in _NC_CACHE:
        _NC_CACHE["nc"] = build_program()
    nc = _NC_CACHE["nc"]
    res = run_bass_kernel_spmd(nc, in_maps, core_ids=list(range(NCORES)))
    R_ = res.results
    y_prompt = np.zeros((4, 4096, D), np.float32)
    y_sample = np.zeros((16, 64, D), np.float32)
    hg_p = np.zeros((1, 4, NH, 128, 128), np.float32)
    cv_p = np.zeros((1, 4, 2, D), np.float32)
    hg_s = np.zeros((1, 16, NH, 128, 128), np.float32)
    cv_s = np.zeros((1, 16, 2, D), np.float32)
    for c in range(NCORES):
        b, half = c // 2, c % 2
        y = np.ascontiguousarray(np.transpose(R_[c]["y"], (2, 1, 0))).reshape(TMAIN, D)
        y_prompt[b, half * 2048:(half + 1) * 2048] = y[0:2048]
        y_sample[2 * c] = y[2048:2112]
        y_sample[2 * c + 1] = y[2112:2176]
        sf, cf = R_[c]["sfin"], R_[c]["cfin"]
        hg_s[0, 2 * c], hg_s[0, 2 * c + 1] = sf[1], sf[2]
        cv_s[0, 2 * c], cv_s[0, 2 * c + 1] = cf[1], cf[2]
        if half == 1:
            hg_p[0, b] = sf[0]
            cv_p[0, b] = cf[0]
    return (y_prompt, y_sample, hg_p, cv_p, hg_s, cv_s)
```
